# Optimizing a Trainium2 kernel written in Bass

```python
import math
import jax
import jax.numpy as jnp
from jax import lax
import numpy as np

D_MODEL = 1024
BATCH = 32
SEQ = 256
DEPTH = 4
DEC_BATCH = 8
DEC_SEQ = 2048
PAST_LEN = 512

GRID_W = 64
N_MIXERS = 3
LAYER_KIND = tuple(l % N_MIXERS for l in range(DEPTH))
LAYER_SLOT = tuple(LAYER_KIND[:l].count(LAYER_KIND[l]) for l in range(DEPTH))
N_DIFF = LAYER_KIND.count(0)
N_RET = LAYER_KIND.count(1)
N_HGRN = LAYER_KIND.count(2)

DA_HEADS = 8
DA_HEAD_DIM = 64
DA_V_DIM = 2 * DA_HEAD_DIM
DA_WIDTH = DA_HEADS * DA_V_DIM
DA_QK = 2 * DA_HEADS * DA_HEAD_DIM
DA_IN = 2 * DA_QK + 2 * DA_WIDTH
ROPE_BASE = 10000.0
Q_BLOCK = 128

RET_HEADS = 8
RET_DK = 128
RET_DV = 256
RET_WIDTH = RET_HEADS * RET_DV
RET_QK = RET_HEADS * RET_DK
RET_IN = 2 * RET_QK + 2 * RET_WIDTH

HG_HEADS = 8
HG_DK = 128
HG_DV = D_MODEL // HG_HEADS
HG_F = HG_HEADS * HG_DK
HG_WIDTH = HG_HEADS * HG_DV
HG_IN = 3 * HG_F + 2 * HG_WIDTH

CHUNK = 64
EPS = 1e-6

kernel_name = 'hybrid_diff_ret_hgrn2_prefix_denoise_step'


def rms_norm(x, gain=None):
    xf = x.astype(jnp.float32)
    y = xf * lax.rsqrt(jnp.mean(xf * xf, axis=-1, keepdims=True) + EPS)
    if gain is not None:
        y = y * gain.astype(jnp.float32)
    return y.astype(x.dtype)


def modulation(cvec, w, b):
    m = jax.nn.silu(cvec) @ w + b
    shift, scale, gate = jnp.split(m, 3, axis=-1)
    return shift[:, None, :], scale[:, None, :], gate[:, None, :]


def axial_rope(x):
    B, T, S, hd = x.shape
    rows = T // GRID_W
    row = jnp.repeat(jnp.arange(rows), GRID_W)
    col = jnp.broadcast_to(jnp.arange(GRID_W), (rows, GRID_W)).reshape(-1)
    n_pair = hd // 4
    inv = ROPE_BASE ** (-jnp.arange(n_pair, dtype=jnp.float32) / n_pair)
    ang = jnp.concatenate([row[:, None] * inv, col[:, None] * inv], axis=-1)
    cos = jnp.cos(ang)[None, :, None, :]
    sin = jnp.sin(ang)[None, :, None, :]
    xf = x.astype(jnp.float32).reshape(B, T, S, hd // 2, 2)
    x1, x2 = xf[..., 0], xf[..., 1]
    out = jnp.stack([x1 * cos - x2 * sin, x1 * sin + x2 * cos], axis=-1)
    return out.reshape(B, T, S, hd).astype(x.dtype)


def to_chunks(x):
    B, T = x.shape[:2]
    return jnp.moveaxis(x.reshape(B, T // CHUNK, CHUNK, *x.shape[2:]), 1, 0)


def from_chunks(x):
    n, B, C = x.shape[:3]
    return jnp.moveaxis(x, 0, 1).reshape(B, n * C, *x.shape[3:])


def diff_softmax_attend(q, k, v, lam):
    B, Tq, S, dh = q.shape
    H = S // 2
    nb = Tq // Q_BLOCK
    qb = jnp.moveaxis(q.reshape(B, nb, Q_BLOCK, S, dh), 1, 0)

    def block(qblk):
        s = jnp.einsum('bqsd,bksd->bsqk', qblk, k).astype(jnp.float32)
        p = jax.nn.softmax(s, axis=-1).reshape(B, H, 2, Q_BLOCK, -1)
        a = p[:, :, 0] - lam * p[:, :, 1]
        return jnp.einsum('bhqk,bkhe->bqhe', a.astype(v.dtype), v)

    o = lax.map(block, qb)
    return jnp.moveaxis(o, 0, 1).reshape(B, Tq, H, v.shape[-1])


def diff_attention(xn, w_in, w_out, lam_vec, sub_g, layer_idx, ctx_k=None, ctx_v=None):
    B, T, _ = xn.shape
    h = xn @ w_in
    q, k, v, g = jnp.split(h, [DA_QK, 2 * DA_QK, 2 * DA_QK + DA_WIDTH], axis=-1)
    q = q.reshape(B, T, 2 * DA_HEADS, DA_HEAD_DIM)
    k = k.reshape(B, T, 2 * DA_HEADS, DA_HEAD_DIM)
    v = v.reshape(B, T, DA_HEADS, DA_V_DIM)
    lam_init = 0.8 - 0.6 * math.exp(-0.3 * layer_idx)
    lf = lam_vec.astype(jnp.float32)
    lam = jnp.exp(jnp.sum(lf[0] * lf[1])) - jnp.exp(jnp.sum(lf[2] * lf[3])) + lam_init
    if ctx_k is None:
        keys, vals = k, v
    else:
        q = axial_rope(q)
        k = axial_rope(k)
        keys = jnp.concatenate([k, ctx_k.astype(k.dtype)], axis=1)
        vals = jnp.concatenate([v, ctx_v.astype(v.dtype)], axis=1)
    o = diff_softmax_attend(q * (DA_HEAD_DIM ** -0.5), keys, vals, lam)
    o = rms_norm(o, sub_g) * (1.0 - lam_init)
    y = (o.reshape(B, T, DA_WIDTH) * jax.nn.silu(g)) @ w_out
    return y, k, v


def retention_scan(q, k, v, log_g, s0):
    idx = jnp.arange(CHUNK, dtype=jnp.float32)
    diff = idx[:, None] - idx[None, :]
    causal = diff >= 0
    decay_mat = jnp.where(causal, jnp.exp(jnp.where(causal, diff, 0.0) * log_g[:, None, None]), 0.0)
    q_dec = jnp.exp((idx + 1.0)[:, None] * log_g[None, :])
    k_dec = jnp.exp((CHUNK - 1.0 - idx)[:, None] * log_g[None, :])
    chunk_dec = jnp.exp(CHUNK * log_g)[None, :, None, None]

    def step(S, inp):
        qc, kc, vc = inp
        sc = jnp.einsum('bthd,bshd->bhts', qc, kc) * decay_mat
        o = (jnp.einsum('bhts,bshe->bthe', sc, vc)
             + jnp.einsum('bthd,bhde->bthe', qc * q_dec[None, :, :, None], S))
        S = chunk_dec * S + jnp.einsum('bshd,bshe->bhde', kc * k_dec[None, :, :, None], vc)
        return S, o

    f32 = jnp.float32
    S, o = lax.scan(step, s0.astype(f32),
                    (to_chunks(q.astype(f32)), to_chunks(k.astype(f32)), to_chunks(v.astype(f32))))
    return from_chunks(o), S


def gla_scan(q, k, v, log_f, s0):
    idx = jnp.arange(CHUNK)
    causal = idx[:, None] >= idx[None, :]
    mid = CHUNK // 2

    def step(S, inp):
        qc, kc, vc, gc = inp
        b = lax.cumsum(gc, axis=1)
        ref = b[:, mid:mid + 1]
        sc = jnp.einsum('bthd,bshd->bhts', qc * jnp.exp(b - ref), kc * jnp.exp(ref - b))
        sc = jnp.where(causal, sc, 0.0)
        o = (jnp.einsum('bhts,bshe->bthe', sc, vc)
             + jnp.einsum('bthd,bhde->bthe', qc * jnp.exp(b), S))
        b_last = b[:, -1:]
        S = (jnp.exp(b_last[:, 0])[..., None] * S
             + jnp.einsum('bshd,bshe->bhde', kc * jnp.exp(b_last - b), vc))
        return S, o

    f32 = jnp.float32
    S, o = lax.scan(step, s0.astype(f32),
                    (to_chunks(q.astype(f32)), to_chunks(k.astype(f32)),
                     to_chunks(v.astype(f32)), to_chunks(log_f.astype(f32))))
    return from_chunks(o), S


def retention(xn, w_in, w_out, decay_param, s0=None):
    B, T, _ = xn.shape
    h = xn @ w_in
    q, k, v, g = jnp.split(h, [RET_QK, 2 * RET_QK, 2 * RET_QK + RET_WIDTH], axis=-1)
    q = q.reshape(B, T, RET_HEADS, RET_DK)
    k = k.reshape(B, T, RET_HEADS, RET_DK) * (RET_DK ** -0.5)
    v = v.reshape(B, T, RET_HEADS, RET_DV)
    log_g = jnp.log1p(-jnp.exp(decay_param.astype(jnp.float32)))
    if s0 is None:
        s0 = jnp.zeros((B, 2, RET_HEADS, RET_DK, RET_DV), jnp.float32)
    of, sf = retention_scan(q, k, v, log_g[0], s0[:, 0])
    ob, sb = retention_scan(q[:, ::-1], k[:, ::-1], v[:, ::-1], log_g[1], s0[:, 1])
    o = rms_norm(of + ob[:, ::-1]).astype(xn.dtype)
    y = (o.reshape(B, T, RET_WIDTH) * jax.nn.silu(g)) @ w_out
    return y, jnp.stack([sf, sb], axis=1)


def hgrn2(xn, w_in, w_out, lb_logits, norm_g, layer_idx, s0=None):
    B, T, _ = xn.shape
    h = xn @ w_in
    q, ff, fb, i, g = jnp.split(h, [HG_F, 2 * HG_F, 3 * HG_F, 3 * HG_F + HG_WIDTH], axis=-1)
    sm = jax.nn.softmax(lb_logits.astype(jnp.float32), axis=1)
    lb = lax.cumsum(sm, axis=1)[:, layer_idx] - sm[:, 0]

    def gates(fl, lbd):
        f = lbd + (1.0 - lbd) * jax.nn.sigmoid(fl.astype(jnp.float32))
        shp = (B, T, HG_HEADS, HG_DK)
        return jnp.log(f).reshape(shp), (1.0 - f).reshape(shp)

    q = jax.nn.silu(q).reshape(B, T, HG_HEADS, HG_DK)
    i = i.reshape(B, T, HG_HEADS, HG_DV)
    gf, kf = gates(ff, lb[0])
    gb, kb = gates(fb, lb[1])
    if s0 is None:
        s0 = jnp.zeros((B, 2, HG_HEADS, HG_DK, HG_DV), jnp.float32)
    of, sf = gla_scan(q, kf, i, gf, s0[:, 0])
    ob, sb = gla_scan(q[:, ::-1], kb[:, ::-1], i[:, ::-1], gb[:, ::-1], s0[:, 1])
    o = rms_norm(of + ob[:, ::-1], norm_g).astype(xn.dtype)
    y = (o.reshape(B, T, HG_WIDTH) * jax.nn.silu(g)) @ w_out
    return y, jnp.stack([sf, sb], axis=1)


def setup_inputs(seed: int = 0) -> dict:
    key = jax.random.key(seed)
    ks = jax.random.split(key, 24)
    f32 = jnp.float32
    D = D_MODEL

    def nrm(k, shape, s):
        return jax.random.normal(k, shape, f32) * s

    ret_base = jnp.asarray(np.log(2.0 ** (-5.0 - np.arange(RET_HEADS))), dtype=f32)
    return {
        'x_prompt': nrm(ks[0], (BATCH, SEQ, D), 1.0),
        'x_sample': nrm(ks[1], (DEC_BATCH, DEC_SEQ, D), 1.0),
        'cache_k': nrm(ks[2], (DEC_BATCH, N_DIFF, PAST_LEN, 2 * DA_HEADS, DA_HEAD_DIM), 1.0),
        'cache_v': nrm(ks[3], (DEC_BATCH, N_DIFF, PAST_LEN, DA_HEADS, DA_V_DIM), 1.0),
        'state_ret': nrm(ks[4], (DEC_BATCH, N_RET, 2, RET_HEADS, RET_DK, RET_DV), 0.5),
        'state_hgrn': nrm(ks[5], (DEC_BATCH, N_HGRN, 2, HG_HEADS, HG_DK, HG_DV), 0.5),
        'c': nrm(ks[6], (DEC_BATCH, D), 1.0),
        'c_ctx': nrm(ks[7], (D,), 1.0),
        'w_mod': nrm(ks[8], (DEPTH, D, 3 * D), 0.5 * D ** -0.5),
        'b_mod': nrm(ks[9], (DEPTH, 3 * D), 0.02),
        'g_pre': 1.0 + nrm(ks[10], (DEPTH, D), 0.05),
        'g_post': 1.0 + nrm(ks[11], (DEPTH, D), 0.05),
        'da_w_in': nrm(ks[12], (N_DIFF, D, DA_IN), D ** -0.5),
        'da_w_out': nrm(ks[13], (N_DIFF, DA_WIDTH, D), DA_WIDTH ** -0.5),
        'da_lambda': nrm(ks[14], (N_DIFF, 4, DA_HEAD_DIM), 0.1),
        'da_subln': 1.0 + nrm(ks[15], (N_DIFF, DA_V_DIM), 0.05),
        'ret_w_in': nrm(ks[16], (N_RET, D, RET_IN), D ** -0.5),
        'ret_w_out': nrm(ks[17], (N_RET, RET_WIDTH, D), RET_WIDTH ** -0.5),
        'ret_decay': ret_base + nrm(ks[18], (N_RET, 2, RET_HEADS), 0.05),
        'hg_w_in': nrm(ks[19], (N_HGRN, D, HG_IN), D ** -0.5),
        'hg_w_out': nrm(ks[20], (N_HGRN, HG_WIDTH, D), HG_WIDTH ** -0.5),
        'hg_lb': nrm(ks[21], (2, DEPTH, HG_F), 0.1),
        'hg_norm': 1.0 + nrm(ks[22], (N_HGRN, HG_DV), 0.05),
    }


def reference(x_prompt, x_sample, cache_k, cache_v, state_ret, state_hgrn, c, c_ctx,
              w_mod, b_mod, g_pre, g_post,
              da_w_in, da_w_out, da_lambda, da_subln,
              ret_w_in, ret_w_out, ret_decay,
              hg_w_in, hg_w_out, hg_lb, hg_norm):
    xp, xs = x_prompt, x_sample
    new_k, new_v, new_r, new_h = [], [], [], []
    for l in range(DEPTH):
        kind = LAYER_KIND[l]
        j = LAYER_SLOT[l]
        sh_p, sc_p, ga_p = modulation(c_ctx[None, :], w_mod[l], b_mod[l])
        sh_s, sc_s, ga_s = modulation(c, w_mod[l], b_mod[l])
        hp = rms_norm(xp, g_pre[l]) * (1.0 + sc_p) + sh_p
        hs = rms_norm(xs, g_pre[l]) * (1.0 + sc_s) + sh_s
        if kind == 0:
            yp, kp, vp = diff_attention(hp, da_w_in[j], da_w_out[j], da_lambda[j], da_subln[j], l)
            ys, _, _ = diff_attention(hs, da_w_in[j], da_w_out[j], da_lambda[j], da_subln[j], l,
                                      cache_k[:, j], cache_v[:, j])
            new_k.append(kp)
            new_v.append(vp)
        elif kind == 1:
            yp, sp = retention(hp, ret_w_in[j], ret_w_out[j], ret_decay[j])
            ys, _ = retention(hs, ret_w_in[j], ret_w_out[j], ret_decay[j], state_ret[:, j])
            new_r.append(sp)
        else:
            yp, sp = hgrn2(hp, hg_w_in[j], hg_w_out[j], hg_lb, hg_norm[j], l)
            ys, _ = hgrn2(hs, hg_w_in[j], hg_w_out[j], hg_lb, hg_norm[j], l, state_hgrn[:, j])
            new_h.append(sp)
        xp = xp + ga_p * rms_norm(yp, g_post[l])
        xs = xs + ga_s * rms_norm(ys, g_post[l])
    new_cache_k = jnp.stack(new_k, axis=1)
    new_cache_v = jnp.stack(new_v, axis=1)
    new_state_ret = jnp.stack(new_r, axis=1)
    new_state_hgrn = jnp.stack(new_h, axis=1)
    return (xp, xs, new_cache_k, new_cache_v, new_state_ret, new_state_hgrn)
```

```python
import math
from contextlib import ExitStack, contextmanager

import numpy as np
import concourse.bass as bass
import concourse.mybir as mybir
from concourse.bass_utils import run_bass_kernel_spmd

F32 = mybir.dt.float32
BF16 = mybir.dt.bfloat16
AF = mybir.ActivationFunctionType
ALU = mybir.AluOpType
AX = mybir.AxisListType


class Buf:
    def __init__(self, name, t):
        self.name = name
        self.t = t
        self.lw = {}
        self.rd = {}
        self.kind = "sb"
        self.dsem = None
        self.dcnt = 0
        self.psum = False

    def __getitem__(self, i):
        return self.t[i]


class Prog:
    ENG = ("pe", "act", "dve", "pool", "sp")

    def __init__(self, nc):
        self.nc = nc
        self.q = {e: [] for e in self.ENG}
        self.esem = {}
        self.cnt = {}
        for e in ("pe", "act", "dve", "pool"):
            self.esem[e] = nc.alloc_semaphore(name="es_" + e)
            self.cnt[e] = 0
        self.seen = {e: {} for e in self.ENG}
        self.freed = {}
        self.dbufs = []
        self.allbufs = []
        self.scopes = []
        self.uid = 0
        self.dsems = {}
        self.free_dsems = {"sw": [], "hw": []}

    @contextmanager
    def scope(self):
        es = ExitStack()
        bufs = []
        self.scopes.append((es, bufs))
        try:
            yield
        finally:
            for b in bufs:
                if b.dsem is not None:
                    for qt, key in b.dsem.items():
                        self.free_dsems[qt].append(key)
                for t in list(b.lw.values()) + list(b.rd.values()):
                    k = t[0]
                    if k not in self.freed or self.freed[k][2] < t[2]:
                        self.freed[k] = t
            es.close()
            self.scopes.pop()

    def _mk(self, name, t):
        b = Buf(name, t)
        b.rd = dict(self.freed)
        self.scopes[-1][1].append(b)
        self.allbufs.append(b)
        return b

    def sb(self, name, shape, dt):
        self.uid += 1
        t = self.scopes[-1][0].enter_context(self.nc.sbuf_tensor("%s_%d" % (name, self.uid), list(shape), dt))
        return self._mk(name, t)

    def ps(self, name, shape, dt):
        self.uid += 1
        t = self.scopes[-1][0].enter_context(self.nc.psum_tensor("%s_%d" % (name, self.uid), list(shape), dt))
        b = self._mk(name, t)
        b.psum = True
        b.kind = "ps"
        return b

    def dram(self, name, shape, dt):
        t = self.nc.dram_tensor(name, list(shape), dt, kind="Internal").ap()
        b = Buf(name, t)
        b.kind = "dram"
        self.allbufs.append(b)
        return b

    def _need(self, eng, waits, tok, raw):
        if tok is None:
            return
        key, semh, val, teng = tok
        if teng == eng and not raw:
            return
        if self.seen[eng].get(key, 0) >= val:
            return
        if key not in waits or waits[key][1] < val:
            waits[key] = (semh, val)

    def _deps(self, eng, r, w, skipkey=None):
        waits = {}
        for b in r:
            for t in b.lw.values():
                self._need(eng, waits, t, True)
            if b.psum:
                for t in b.rd.values():
                    self._need(eng, waits, t, False)
        for b in w:
            for t in b.lw.values():
                if not (skipkey is not None and t[0] == skipkey):
                    self._need(eng, waits, t, False)
            for t in b.rd.values():
                self._need(eng, waits, t, False)
        for k, (s, v) in waits.items():
            self.seen[eng][k] = v
        return list(waits.values())

    def _reg(self, tok, r, w):
        for b in r:
            b.rd[tok[0]] = tok
        for b in w:
            b.lw[tok[0]] = tok
            b.rd = {}

    def op(self, eng, fn, r=(), w=()):
        waits = self._deps(eng, r, w)
        self.cnt[eng] += 1
        tok = (eng, self.esem[eng], self.cnt[eng], eng)
        self._reg(tok, r, w)
        self.q[eng].append((waits, fn, self.esem[eng], 1))

    def dma(self, q, out, in_, r=(), w=(), owner=None, nowaw=False, **kw):
        if owner is None:
            cand = [b for b in list(w) + list(r) if b.kind == "sb"]
            owner = cand[0] if cand else (w[0] if w else r[0])
        extra = None
        qt = "sw" if q == "pool" else "hw"
        if owner.dsem is None:
            owner.dsem = {}
        if qt not in owner.dsem:
            if self.free_dsems[qt]:
                key = self.free_dsems[qt].pop()
                extra = key
            else:
                key = ("d", len(self.dsems))
                self.dsems[key] = [self.nc.alloc_semaphore(name="ds_%d" % len(self.dsems)), 0]
            owner.dsem[qt] = key
        key = owner.dsem[qt]
        semh, tot = self.dsems[key]
        waits = self._deps(q, r, w, skipkey=key if nowaw else None)
        if extra is not None and tot > 0 and self.seen[q].get(key, 0) < tot:
            waits = [x for x in waits if x[0] is not semh] + [(semh, tot)]
            self.seen[q][key] = tot
        tot += 16
        self.dsems[key][1] = tot
        tok = (key, semh, tot, "dma")
        self._reg(tok, r, w)
        self.q[q].append((waits, (lambda e: e.dma_start(out=out, in_=in_, **kw)), semh, 16))

    def finish(self):
        nc = self.nc
        for key, (semh, tot) in self.dsems.items():
            if tot > 0:
                self.q["sp"].append(([(semh, tot)], None, None, 0))

        def run(e, items):
            for waits, fn, sem, inc in items:
                for s, v in waits:
                    e.wait_ge(s, v)
                if fn is not None:
                    fn(e).then_inc(sem, inc)

        with nc.Block() as blk:
            blk.tensor(lambda e: run(e, self.q["pe"]))
            blk.scalar(lambda e: run(e, self.q["act"]))
            blk.vector(lambda e: run(e, self.q["dve"]))
            blk.gpsimd(lambda e: run(e, self.q["pool"]))
            blk.sync(lambda e: run(e, self.q["sp"]))


class RingU:
    def __init__(self, *rings):
        self.b = [b for r in rings for b in r.b]
        self.i = 0

    def next(self):
        b = self.b[self.i % len(self.b)]
        self.i += 1
        return b


class Ring:
    def __init__(self, P, kind, name, n, shape, dt):
        mk = P.sb if kind == "sb" else P.ps
        self.b = [mk("%s%d" % (name, i), shape, dt) for i in range(n)]
        self.i = 0

    def next(self):
        b = self.b[self.i % len(self.b)]
        self.i += 1
        return b


class V:
    def __init__(self, b, ap):
        self.b = b
        self.ap = ap

    def __getitem__(self, i):
        return V(self.b, self.ap[i])

    def bc(self, shape):
        return V(self.b, self.ap.broadcast_to(list(shape)))

    def re(self, pat, **kw):
        return V(self.b, self.ap.rearrange(pat, **kw))

    def un(self, ax):
        return V(self.b, self.ap.unsqueeze(ax))

    def pbc(self, n=128):
        return V(self.b, self.ap.partition_broadcast(n))


def _bgi(self, i):
    return V(self, self.t[i])


Buf.__getitem__ = _bgi


def _u(x, lst):
    if isinstance(x, V):
        lst.append(x.b)
        return x.ap
    return x


def _I(P, eng, meth, out, *args, **kw):
    r, w = [], []
    o = _u(out, w)
    a = [_u(x, r) for x in args]
    k = {}
    for n, x in kw.items():
        k[n] = _u(x, w if n == "accum_out" else r)
    P.op(eng, (lambda e: getattr(e, meth)(o, *a, **k)), r=r, w=w)


def _act(P, out, in_, func, **kw):
    r, w = [], []
    o = _u(out, w)
    i = _u(in_, r)
    k = {}
    for n, x in kw.items():
        k[n] = _u(x, w if n == "accum_out" else r)
    P.op("act", (lambda e: e.activation(out=o, in_=i, func=func, **k)), r=r, w=w)


def _mm(P, out, lhsT, rhs, start=True, stop=True):
    r, w = [], []
    o = _u(out, w)
    a = _u(lhsT, r)
    b = _u(rhs, r)
    P.op("pe", (lambda e: e.matmul(o, a, b, start=start, stop=stop)), r=r, w=w)


def _tr(P, out, in_, ident):
    r, w = [], []
    o = _u(out, w)
    a = _u(in_, r)
    b = _u(ident, r)
    P.op("pe", (lambda e: e.transpose(out=o, in_=a, identity=b)), r=r, w=w)


def _dma(P, q, out, in_, **kw):
    r, w = [], []
    o = _u(out, w)
    i = _u(in_, r)
    P.dma(q, o, i, r=r, w=w, **kw)


Prog.I = _I
Prog.act = _act
Prog.mm = _mm
Prog.tr = _tr
Prog.d = _dma


D = 1024
EPS = 1e-6
BIG = 1.0e6
C_ID, C_SW, C_A, C_B, C_TL1, C_CTL, C_MF, C_MB = 0, 128, 256, 384, 512, 640, 768, 896
C_SDF, C_SDB = 1024, 1025
NCST = 1032


def host_consts():
    c = np.zeros((128, NCST), np.float32)
    i = np.arange(128)
    c[:, C_ID:C_ID + 128] = np.eye(128, dtype=np.float32)
    sw = np.zeros((128, 128), np.float32)
    sw[i, i ^ 1] = 1.0
    c[:, C_SW:C_SW + 128] = sw
    s = i[:, None].astype(np.float64)
    t = i[None, :].astype(np.float64)
    c[:, C_A:C_A + 128] = np.where(t >= s, t - s, BIG)
    c[:, C_B:C_B + 128] = np.where(s >= t, s - t, BIG)
    c[:, C_TL1:C_TL1 + 128] = np.broadcast_to(t + 1.0, (128, 128))
    c[:, C_CTL:C_CTL + 128] = np.broadcast_to(128.0 - t, (128, 128))
    same = (i[:, None] // 64) == (i[None, :] // 64)
    c[:, C_MF:C_MF + 128] = (same & (t >= s)).astype(np.float32)
    c[:, C_MB:C_MB + 128] = (same & (t <= s)).astype(np.float32)
    c[:, C_SDF] = 127.0 - i
    c[:, C_SDB] = i
    T = 2048
    tt = np.arange(T)
    row = (tt // 64).astype(np.float64)
    col = (tt % 64).astype(np.float64)
    inv = 10000.0 ** (-np.arange(16, dtype=np.float32) / 16).astype(np.float64)
    rope = np.zeros((128, 2, T), np.float32)
    for p in range(128):
        d = p % 64
        pr = d // 2
        ang = (row * inv[pr]) if pr < 16 else (col * inv[pr - 16])
        ang = ang.astype(np.float32).astype(np.float64)
        rope[p, 0] = np.cos(ang)
        rope[p, 1] = np.sin(ang) * (-1.0 if d % 2 == 0 else 1.0)
    return c, rope


def build(NL=4):
    nc = bass.Bass("TRN2", target_bir_lowering=False)
    P = Prog(nc)

    def din(n, s):
        return nc.dram_tensor(n, list(s), F32, kind="ExternalInput").ap()

    def dout(n, s):
        return nc.dram_tensor(n, list(s), F32, kind="ExternalOutput").ap()

    xs = din("xs", [2048, D]); xp = din("xp", [1024, D])
    ck = din("ck", [2, 512, 1024]); cv = din("cv", [2, 512, 1024])
    sr = din("sr", [2, 8, 128, 256]); shg = din("shg", [2, 8, 128, 128]); cvec = din("cvec", [2, D])
    w_mod = din("w_mod", [4, D, 3 * D]); b_mod = din("b_mod", [4, 3 * D])
    g_pre = din("g_pre", [4, D]); g_post = din("g_post", [4, D])
    da_w_in = din("da_w_in", [2, D, 4096]); da_w_out = din("da_w_out", [2, 1024, D])
    da_lambda = din("da_lambda", [2, 256]); da_subln = din("da_subln", [2, 128])
    ret_w_in = din("ret_w_in", [1, D, 6144]); ret_w_out = din("ret_w_out", [1, 2048, D]); ret_decay = din("ret_decay", [1, 16])
    hg_w_in = din("hg_w_in", [1, D, 5120]); hg_w_out = din("hg_w_out", [1, 1024, D])
    hg_lb = din("hg_lb", [2, 4, 1024]); hg_norm = din("hg_norm", [1, 128])
    cst = din("cst", [128, NCST]); rope = din("rope", [128, 2, 2048])
    yp = dout("yp", [1024, D]); ys = dout("ys", [2048, D])
    nk = dout("nk", [4, 2, 256, 1024]); nv = dout("nv", [4, 2, 256, 1024])
    nr = dout("nr", [4, 2, 8, 128, 256]); nh = dout("nh", [4, 2, 8, 128, 128])

    msc = P.dram("msc", [4, 2, 3 * D], F32)
    xsc = {"s": [P.dram("xsc_s%d" % i, [2048, D], F32) for i in range(2)],
           "p": [P.dram("xsc_p%d" % i, [1024, D], F32) for i in range(2)]}

    with P.scope():
        cf = P.sb("cf", [128, NCST], F32)
        P.d("sp", cf[:], cst)
        cb = P.sb("cb", [128, 1024], BF16)
        P.I("dve", "tensor_copy", cb[:], cf[:, 0:1024])
        identb = cb[:, C_ID:C_ID + 128]
        swb = cb[:, C_SW:C_SW + 128]

        def rstd_from_ss(ss, n):
            P.act(ss, ss, AF.Ln, scale=1.0 / n, bias=EPS)
            P.act(ss, ss, AF.Exp, scale=-0.5)

        with P.scope():
            cT = P.sb("cT", [128, 2, 8], F32)
            for v in range(2):
                P.d("sp", cT[:, v, :], cvec[v, :].rearrange("(c p) -> p c", p=128), allow_slow_non_contiguous=True, nowaw=True)
            scb = P.sb("scb", [128, 2, 8], BF16)
            P.act(scb[:], cT[:], AF.Silu)
            wr = Ring(P, "sb", "wm", 2, [128, 8, 512], BF16)
            psm = Ring(P, "ps", "psm", 2, [128, 512], F32)
            brow = P.sb("brow", [1, 3 * D], F32)
            mr = Ring(P, "sb", "mrow", 2, [1, 512], F32)
            for l in range(NL):
                P.d("sp", brow[:], b_mod[l:l + 1, :])
                wv = w_mod[l].rearrange("(c p) n -> p c n", p=128)
                for cbk in range(6):
                    w = wr.next()
                    P.d("pool", w[:], wv[:, :, cbk * 512:(cbk + 1) * 512])
                    for v in range(2):
                        ps = psm.next()
                        for kc in range(8):
                            P.mm(ps[0:1, :], scb[:, v, kc:kc + 1], w[:, kc, :], start=(kc == 0), stop=(kc == 7))
                        m = mr.next()
                        P.I("dve", "tensor_tensor", m[:], ps[0:1, :], brow[:, cbk * 512:(cbk + 1) * 512], op=ALU.add)
                        P.d("sp", msc[l, v:v + 1, cbk * 512:(cbk + 1) * 512], m[:], nowaw=True)

        def phaseA(u, l, T, hT, xsrc):
            v = 1 if u == "s" else 0
            with P.scope():
                shT = P.sb("shT", [128, 8], F32); scT = P.sb("scT", [128, 8], F32)
                gpT = P.sb("gpT", [128, 8], F32); AT = P.sb("AT", [128, 8], F32)
                P.d("sp", shT[:], msc[l, v, 0:1024].re("(c p) -> p c", p=128), allow_slow_non_contiguous=True)
                P.d("sp", scT[:], msc[l, v, 1024:2048].re("(c p) -> p c", p=128), allow_slow_non_contiguous=True)
                P.d("sp", gpT[:], g_pre[l, :].rearrange("(c p) -> p c", p=128), allow_slow_non_contiguous=True)
                P.I("dve", "scalar_tensor_tensor", AT[:], scT[:], 1.0, gpT[:], op0=ALU.add, op1=ALU.mult)
                xr = Ring(P, "sb", "xa", 3, [128, D], F32)
                xnr = Ring(P, "sb", "xn", 4, [128, D], BF16)
                jr = Ring(P, "sb", "jk", 2, [128, D], BF16)
                sr_ = Ring(P, "sb", "ssA", 4, [128, 1], F32)
                pst = Ring(P, "ps", "pstA", 2, [128, 1024], BF16)
                def a_stats(tt):
                    x = xr.next()
                    P.d("sp", x[:], xsrc[tt * 128:(tt + 1) * 128, :])
                    junk = jr.next(); ss = sr_.next()
                    P.act(junk[:], x[:], AF.Square, accum_out=ss[:])
                    rstd_from_ss(ss[:], D)
                    xn = xnr.next()
                    P.I("dve", "tensor_scalar", xn[:], x[:], ss[:, 0:1], None, op0=ALU.mult)
                    return xn

                def a_trev(tt, xn):
                    pt = pst.next()
                    for kc in range(8):
                        P.tr(pt[:, kc * 128:(kc + 1) * 128], xn[:, kc * 128:(kc + 1) * 128], identb)
                    for kc in range(8):
                        if kc % 2 == 0:
                            P.I("dve", "tensor_scalar", hT[:, kc, tt * 128:(tt + 1) * 128], pt[:, kc * 128:(kc + 1) * 128],
                                AT[:, kc:kc + 1], shT[:, kc:kc + 1], op0=ALU.mult, op1=ALU.add)
                        else:
                            P.act(hT[:, kc, tt * 128:(tt + 1) * 128], pt[:, kc * 128:(kc + 1) * 128], AF.Identity,
                                  scale=AT[:, kc:kc + 1], bias=shT[:, kc:kc + 1])

                pend_a = []
                for tt in range(T // 128):
                    pend_a.append((tt, a_stats(tt)))
                    if len(pend_a) > 2:
                        a_trev(*pend_a.pop(0))
                for it in pend_a:
                    a_trev(*it)

        def load_wo(KC, wout_ap):
            wo = P.sb("wo", [128, KC, D], BF16)
            wv = wout_ap.rearrange("(c p) n -> p c n", p=128)
            for c4 in range(0, KC, 4):
                P.d("pool", wo[:, c4:c4 + 4, :], wv[:, c4:c4 + 4, :], nowaw=True)
            return wo

        def phaseC(u, l, T, oT, KC, wout_ap, xsrc, xdst, wo=None):
            v = 1 if u == "s" else 0
            with P.scope():
                if wo is None:
                    wo = load_wo(KC, wout_ap)
                G2 = P.sb("G2", [128, D], F32)
                with P.scope():
                    gpo = P.sb("gpo", [128, D], F32)
                    P.d("sp", G2[:], msc[l, v:v + 1, 2048:3072].pbc(128))
                    P.d("sp", gpo[:], g_post[l:l + 1, :].partition_broadcast(128))
                    P.I("dve", "tensor_tensor", G2[:], G2[:], gpo[:], op=ALU.mult)
                psy = Ring(P, "ps", "psY", 2, [128, D], F32)
                xr = Ring(P, "sb", "xc", 2, [128, D], F32)
                tr_ = Ring(P, "sb", "tc", 2, [128, D], F32)
                jr = Ring(P, "sb", "jc", 2, [128, D], BF16)
                sr_ = Ring(P, "sb", "ssC", 4, [128, 1], F32)
                for tt in range(T // 128):
                    y = psy.next()
                    for hb in range(2):
                        for kc in range(KC):
                            P.mm(y[:, hb * 512:(hb + 1) * 512], oT[:, kc, tt * 128:(tt + 1) * 128],
                                 wo[:, kc, hb * 512:(hb + 1) * 512], start=(kc == 0), stop=(kc == KC - 1))
                    junk = jr.next(); ss = sr_.next()
                    P.act(junk[:], y[:], AF.Square, accum_out=ss[:])
                    rstd_from_ss(ss[:], D)
                    x = xr.next()
                    P.d("sp", x[:], xsrc[tt * 128:(tt + 1) * 128, :])
                    t = tr_.next()
                    P.I("dve", "scalar_tensor_tensor", t[:], y[:], ss[:, 0:1], G2[:], op0=ALU.mult, op1=ALU.mult)
                    P.I("pool", "tensor_tensor", x[:], x[:], t[:], op=ALU.add)
                    P.d("sp", xdst[tt * 128:(tt + 1) * 128, :], x[:], nowaw=True)

        def phaseB_da(u, l, j, T, hT, oT, pre):
            lam_init = 0.8 - 0.6 * math.exp(-0.3 * l)
            NT = T // 128
            QB = 256
            if u == "s":
                seqs = [(0, 2048, [(i * 128, i) for i in range(20)])]
                TK = 2560
            else:
                seqs = [(s * 256, 256, [((2 * s + i) * 128, 2 * s + i) for i in range(2)]) for s in range(4)]
                TK = 1024
            NKC = len(seqs[0][2])
            blocks = [(q0_ + qb * QB, kcs) for (q0_, nq, kcs) in seqs for qb in range(nq // QB)]
            win = da_w_in[j].rearrange("(c p) n -> p c n", p=128)
            with P.scope():
                wr = Ring(P, "sb", "wda", 2, [128, 8, 512], BF16)

                def load_panel(h):
                    w = wr.next()
                    for g in range(4):
                        P.d("pool", w[:, :, g * 128:(g + 1) * 128], win[:, :, g * 1024 + h * 128:g * 1024 + (h + 1) * 128], nowaw=True)
                    return w

                wnext = load_panel(0)
                pre()
                lv = P.sb("lv", [128, 256], F32)
                P.d("sp", lv[:], da_lambda[j:j + 1, :].partition_broadcast(128))
                lv4 = lv[:].re("p (a b d) -> p a b d", a=2, b=2)
                pr = P.sb("lpr", [128, 2, 64], F32)
                P.I("dve", "tensor_tensor", pr[:], lv4[:, :, 0, :], lv4[:, :, 1, :], op=ALU.mult)
                s2 = P.sb("ls2", [128, 2], F32)
                P.I("dve", "tensor_reduce", s2[:], pr[:], axis=AX.X, op=ALU.add)
                P.act(s2[:], s2[:], AF.Exp)
                nlam = P.sb("nlam", [128, 1], F32)
                P.I("dve", "tensor_tensor", nlam[:], s2[:, 1:2], s2[:, 0:1], op=ALU.subtract)
                P.I("dve", "tensor_scalar", nlam[:], nlam[:], -lam_init, None, op0=ALU.add)
                sgT = P.sb("sgT", [128, 1], F32)
                P.d("sp", sgT[:], da_subln[j, :].rearrange("(p o) -> p o", o=1), allow_slow_non_contiguous=True)
                P.I("dve", "tensor_scalar", sgT[:], sgT[:], 1.0 - lam_init, None, op0=ALU.mult)
                onesb = P.sb("onesb", [128, 128], BF16)
                P.I("dve", "memset", onesb[:], 1.0)
                if u == "s":
                    ropeb = P.sb("ropeb", [128, 2, 2048], BF16)
                    P.d("pool", ropeb[:], rope)
                qz = [P.sb("qz%d" % i, [128, T], BF16) for i in range(2)]
                P.I("dve", "memset", qz[0][64:128, :], 0.0)
                P.I("dve", "memset", qz[1][0:64, :], 0.0)
                kf = P.sb("kf", [128, TK], BF16)
                gT = P.sb("gT", [128, T], BF16)
                vsb = P.sb("vsb", [128, TK // 128, 128], BF16)
                ptr = Ring(P, "sb", "pT", 2, [128, NKC, 2 * QB], BF16)
                q0r = Ring(P, "sb", "q0", 2, [128, 512], BF16)
                t1r = Ring(P, "sb", "t1", 2, [128, 512], F32)
                t2r = Ring(P, "sb", "t2", 2, [128, 512], F32)
                ckt = P.sb("ckt", [128, 4, 128], BF16)
                kst = Ring(P, "sb", "kst", 2, [128, 256], F32)
                rvr = Ring(P, "sb", "rinv", 2, [128, 2 * QB], F32)
                o1r = Ring(P, "sb", "o1", 2, [128, QB], F32)
                o2r = Ring(P, "sb", "o2", 2, [128, QB], F32)
                sqr = Ring(P, "sb", "sqd", 2, [128, QB], BF16)
                rsr = Ring(P, "sb", "rsd", 2, [128, QB], F32)
                par = Ring(P, "sb", "padd", 4, [128, 2 * QB], BF16)
                psA = Ring(P, "ps", "psA", 2, [128, 512], F32)
                psS = Ring(P, "ps", "psS", 2, [128, 512], F32)
                psO = Ring(P, "ps", "psO", 4, [128, 512], F32)
                psP = RingU(psA, psS)

                def epi1(accO, accS):
                    rinv = rvr.next()
                    if u == "p":
                        P.act(rinv[:], accS[:], AF.Ln)
                        P.act(rinv[:], rinv[:], AF.Exp, scale=-1.0)
                    else:
                        P.I("dve", "reciprocal", rinv[:], accS[:])
                    o1 = o1r.next(); o2 = o2r.next()
                    P.I("dve", "tensor_tensor", o1[:], accO[:, 0:QB], rinv[:, 0:QB], op=ALU.mult)
                    P.I("dve", "scalar_tensor_tensor", o2[:], accO[:, QB:2 * QB], nlam[:, 0:1], rinv[:, QB:2 * QB], op0=ALU.mult, op1=ALU.mult)
                    P.I("pool", "tensor_tensor", o1[:], o1[:], o2[:], op=ALU.add)
                    sq = sqr.next()
                    P.act(sq[:], o1[:], AF.Square)
                    return (o1, o2, sq)

                def epi2(h, qlo, accS, st):
                    o1, o2, sq = st
                    P.mm(accS[:, 0:QB], onesb[:], sq[:])
                    rs = rsr.next()
                    P.act(rs[:], accS[:, 0:QB], AF.Ln, scale=1.0 / 128, bias=EPS)
                    P.act(rs[:], rs[:], AF.Exp, scale=-0.5)
                    P.I("dve", "scalar_tensor_tensor", o2[:], o1[:], sgT[:, 0:1], rs[:], op0=ALU.mult, op1=ALU.mult)
                    P.I("pool", "tensor_tensor", oT[:, h, qlo:qlo + QB], o2[:], gT[:, qlo:qlo + QB], op=ALU.mult)

                for h in range(8):
                    W = wnext
                    if h + 1 < 8:
                        wnext = load_panel(h + 1)
                    for which, dst, c0 in (("q", None, 0), ("k", kf, 128), ("g", gT, 384)):
                        for blk in range(T // 512):
                            ps = psP.next()
                            for kc in range(8):
                                P.mm(ps[:], W[:, kc, c0:c0 + 128], hT[:, kc, blk * 512:(blk + 1) * 512], start=(kc == 0), stop=(kc == 7))
                            bs = slice(blk * 512, (blk + 1) * 512)
                            dsl = dst[:, bs] if dst is not None else None
                            if which == "g":
                                P.act(dsl, ps[:], AF.Silu)
                                continue
                            scl = 0.125 if which == "q" else 1.0
                            if u == "s":
                                q0 = q0r.next()
                                P.I("dve", "tensor_scalar", q0[:], ps[:], scl, None, op0=ALU.mult)
                                sw = psP.next()
                                P.mm(sw[:], swb, q0[:])
                                t1 = t1r.next(); t2 = t2r.next()
                                P.I("pool", "tensor_tensor", t1[:], q0[:], ropeb[:, 0, blk * 512:(blk + 1) * 512], op=ALU.mult)
                                P.I("dve", "tensor_tensor", t2[:], sw[:], ropeb[:, 1, blk * 512:(blk + 1) * 512], op=ALU.mult)
                                if which == "q":
                                    P.I("pool", "tensor_tensor", qz[0][0:64, bs], t1[0:64, :], t2[0:64, :], op=ALU.add)
                                    P.I("pool", "tensor_tensor", qz[1][64:128, bs], t1[64:128, :], t2[64:128, :], op=ALU.add)
                                else:
                                    P.I("pool", "tensor_tensor", dsl, t1[:], t2[:], op=ALU.add)
                            elif which == "q":
                                P.I("dve", "tensor_scalar", qz[0][0:64, bs], ps[0:64, :], scl, None, op0=ALU.mult)
                                P.I("dve", "tensor_scalar", qz[1][64:128, bs], ps[64:128, :], scl, None, op0=ALU.mult)
                            else:
                                P.I("dve", "tensor_scalar", dsl, ps[:], scl, None, op0=ALU.mult)
                    if u == "s":
                        P.d("pool", ckt[:], ck[j].rearrange("(c p) n -> p c n", p=128)[:, :, h * 128:(h + 1) * 128])
                        P.d("pool", vsb[:, 16:20, :], cv[j].rearrange("(c p) n -> p c n", p=128)[:, :, h * 128:(h + 1) * 128])
                        pc = psP.next()
                        for c in range(4):
                            P.mm(pc[:, c * 128:(c + 1) * 128], ckt[:, c, :], identb)
                        P.I("dve", "tensor_copy", kf[:, 2048:2560], pc[:])
                    for tt in range(NT):
                        ps = psP.next()
                        c0 = 128 if u == "p" else 256
                        ncol = 384 - c0
                        for kc in range(8):
                            P.mm(ps[:, 0:ncol], hT[:, kc, tt * 128:(tt + 1) * 128], W[:, kc, c0:384], start=(kc == 0), stop=(kc == 7))
                        P.act(vsb[:, tt, :], ps[:, ncol - 128:ncol], AF.Copy)
                        if u == "p":
                            st = kst.next()
                            P.I("dve", "tensor_copy", st[:], ps[:, 0:256])
                            sq_, tl = tt // 2, tt % 2
                            P.d("sp", nk[sq_, j, tl * 128:(tl + 1) * 128, h * 128:(h + 1) * 128], st[:, 0:128])
                            P.d("sp", nv[sq_, j, tl * 128:(tl + 1) * 128, h * 128:(h + 1) * 128], st[:, 128:256])
                    prev = None
                    pend = None
                    for blk in blocks + [None]:
                        if blk is not None:
                            pT = ptr.next()
                        if prev is not None:
                            accO = psO.next(); accS = psO.next()
                        for i in range(NKC):
                            if blk is not None:
                                qlo, kcs = blk
                                koff, vi = kcs[i]
                                sc = psS.next()
                                for s in range(2):
                                    P.mm(sc[:, s * QB:(s + 1) * QB], kf[:, koff:koff + 128], qz[s][:, qlo:qlo + QB])
                                P.act(pT[:, i, :], sc[:], AF.Exp)
                            if prev is not None:
                                pqlo, pkcs, ppT = prev
                                P.mm(accO[:], vsb[:, pkcs[i][1], :], ppT[:, i, :], start=(i == 0), stop=(i == NKC - 1))
                                P.mm(accS[:], onesb[:], ppT[:, i, :], start=(i == 0), stop=(i == NKC - 1))
                            if pend is not None and i == min(NKC - 1, 9):
                                epi2(*pend)
                                pend = None
                        if pend is not None:
                            epi2(*pend)
                            pend = None
                        if prev is not None:
                            pend = (h, prev[0], accS, epi1(accO, accS))
                        prev = (blk[0], blk[1], pT) if blk is not None else None
                    if pend is not None:
                        epi2(*pend)

        def phaseB_ret(u, l, j, T, hT, oT, pre):
            NT = T // 128
            SC = 128.0 ** -0.5
            if u == "s":
                seqs = [list(range(16))]
            else:
                seqs = [[2 * s, 2 * s + 1] for s in range(4)]
            win = ret_w_in[j].rearrange("(c p) n -> p c n", p=128)
            with P.scope():
                wr = Ring(P, "sb", "wrt", 2, [128, 8, 768], BF16)

                def load_panel(h):
                    w = wr.next()
                    P.d("pool", w[:, :, 0:128], win[:, :, h * 128:(h + 1) * 128], nowaw=True)
                    P.d("pool", w[:, :, 128:256], win[:, :, 1024 + h * 128:1024 + (h + 1) * 128], nowaw=True)
                    P.d("pool", w[:, :, 256:512], win[:, :, 2048 + h * 256:2048 + (h + 1) * 256], nowaw=True)
                    P.d("pool", w[:, :, 512:768], win[:, :, 4096 + h * 256:4096 + (h + 1) * 256], nowaw=True)
                    return w

                wnext = load_panel(0)
                pre()
                lg = P.sb("lg", [128, 16], F32)
                P.d("sp", lg[:], ret_decay[j:j + 1, :].partition_broadcast(128))
                P.act(lg[:], lg[:], AF.Exp)
                P.act(lg[:], lg[:], AF.Ln, scale=-1.0, bias=1.0)
                qT = P.sb("qT", [128, T], BF16); kT = P.sb("kT", [128, T], BF16)
                qfT = P.sb("qfT", [128, T], BF16); qbT = P.sb("qbT", [128, T], BF16)
                kmf = P.sb("kmf", [128, NT, 128], BF16); kmb = P.sb("kmb", [128, NT, 128], BF16)
                vtm = P.sb("vtm", [128, NT, 256], BF16); gs = P.sb("gsr", [128, NT, 256], BF16)
                Sfb = P.sb("Sfb", [128, NT, 256], BF16); Sbb = P.sb("Sbb", [128, NT, 256], BF16)
                S32 = Ring(P, "sb", "S32", 6 if u == "s" else 24, [128, 256], F32)
                m1 = P.sb("m1", [128, 128], F32); m2 = P.sb("m2", [128, 128], F32); Mh = P.sb("Mh", [128, 128], F32)
                qdf = P.sb("qdf", [128, 128], BF16); qdb = P.sb("qdb", [128, 128], BF16)
                kd = P.sb("kd", [128, 4], F32)
                ptr = Ring(P, "sb", "PTs", 2, [128, 128], BF16)
                ogr = Ring(P, "sb", "ogr", 3, [128, 256], BF16)
                jr = Ring(P, "sb", "jrr", 2, [128, 256], BF16)
                ssr = Ring(P, "sb", "ssr", 4, [128, 1], F32)
                psA = Ring(P, "ps", "psA", 2, [128, 512], F32)
                psS = Ring(P, "ps", "psS", 2, [128, 512], F32)
                psO = Ring(P, "ps", "psO", 2, [128, 512], F32)
                psT = Ring(P, "ps", "psT", 2, [128, 1024], BF16)
                psP = RingU(psA, psO)
                for h in range(8):
                    W = wnext
                    if h + 1 < 8:
                        wnext = load_panel(h + 1)
                    lgf = lg[:, h:h + 1]; lgb = lg[:, 8 + h:9 + h]
                    P.act(m1[:], cf[:, C_A:C_A + 128], AF.Exp, scale=lgf)
                    P.act(m2[:], cf[:, C_B:C_B + 128], AF.Exp, scale=lgb)
                    P.I("dve", "tensor_tensor", m1[:], m1[:], m2[:], op=ALU.add)
                    P.I("dve", "tensor_scalar", Mh[:], m1[:], SC, None, op0=ALU.mult)
                    P.act(qdf[:], cf[:, C_TL1:C_TL1 + 128], AF.Exp, scale=lgf)
                    P.act(qdb[:], cf[:, C_CTL:C_CTL + 128], AF.Exp, scale=lgb)
                    P.act(kd[:, 0:1], cf[:, C_SDF:C_SDF + 1], AF.Exp, scale=lgf)
                    P.act(kd[:, 1:2], cf[:, C_SDB:C_SDB + 1], AF.Exp, scale=lgb)
                    P.I("dve", "tensor_scalar", kd[:, 0:2], kd[:, 0:2], SC, None, op0=ALU.mult)
                    P.act(kd[:, 2:3], lgf, AF.Exp, scale=128.0)
                    P.act(kd[:, 3:4], lgb, AF.Exp, scale=128.0)
                    for dst, c0 in ((qT, 0), (kT, 128)):
                        for blk in range(T // 512):
                            ps = psP.next()
                            for kc in range(8):
                                P.mm(ps[:], W[:, kc, c0:c0 + 128], hT[:, kc, blk * 512:(blk + 1) * 512], start=(kc == 0), stop=(kc == 7))
                            P.act(dst[:, blk * 512:(blk + 1) * 512], ps[:], AF.Copy)
                    P.I("dve", "tensor_tensor", qfT[:].re("p (n t) -> p n t", t=128), qT[:].re("p (n t) -> p n t", t=128),
                        qdf[:].un(1).bc([128, NT, 128]), op=ALU.mult)
                    P.I("dve", "tensor_tensor", qbT[:].re("p (n t) -> p n t", t=128), qT[:].re("p (n t) -> p n t", t=128),
                        qdb[:].un(1).bc([128, NT, 128]), op=ALU.mult)
                    for tt in range(NT):
                        ps = psP.next()
                        for kc in range(8):
                            P.mm(ps[:, 0:384], hT[:, kc, tt * 128:(tt + 1) * 128], W[:, kc, 128:512], start=(kc == 0), stop=(kc == 7))
                        P.I("dve", "tensor_scalar", kmf[:, tt, :], ps[:, 0:128], kd[:, 0:1], None, op0=ALU.mult)
                        P.I("dve", "tensor_scalar", kmb[:, tt, :], ps[:, 0:128], kd[:, 1:2], None, op0=ALU.mult)
                        P.act(vtm[:, tt, :], ps[:, 128:384], AF.Copy)
                        ps2 = psP.next()
                        for kc in range(8):
                            P.mm(ps2[:, 0:256], hT[:, kc, tt * 128:(tt + 1) * 128], W[:, kc, 512:768], start=(kc == 0), stop=(kc == 7))
                        P.act(gs[:, tt, :], ps2[:, 0:256], AF.Silu)
                    chains = []
                    for si, tiles in enumerate(seqs):
                        for d, (order, km, Sb_, cdec) in enumerate(((tiles, kmf, Sfb, kd[:, 2:3]), (tiles[::-1], kmb, Sbb, kd[:, 3:4]))):
                            S = S32.next()
                            if u == "s":
                                P.d("sp", S[:], sr[d, h])
                            else:
                                P.I("pool", "memset", S[:], 0.0)
                            chains.append([si, d, order, km, Sb_, cdec, S])
                    for k in range(len(chains[0][2])):
                        for ch in chains:
                            si, d, order, km, Sb_, cdec, S = ch
                            tt = order[k]
                            P.act(Sb_[:, tt, :], S[:], AF.Copy)
                            U = psS.next()
                            P.mm(U[:, 0:256], km[:, tt, :], vtm[:, tt, :])
                            Sn = S32.next()
                            P.I("dve", "scalar_tensor_tensor", Sn[:], S[:], cdec, U[:, 0:256], op0=ALU.mult, op1=ALU.add)
                            ch[6] = Sn
                    if u == "p":
                        for si, d, order, km, Sb_, cdec, S in chains:
                            P.d("sp", nr[si, d, h], S[:])
                    def ret_tr(h_, tt_, og_):
                        pt = psT.next()
                        for e in range(2):
                            P.tr(pt[:, e * 128:(e + 1) * 128], og_[:, e * 128:(e + 1) * 128], identb)
                        P.act(oT[:, 2 * h_:2 * h_ + 2, tt_ * 128:(tt_ + 1) * 128], pt[:, 0:256].re("p (e t) -> p e t", e=2), AF.Copy)

                    pend_tr = None
                    for tt in range(NT):
                        sc = psS.next()
                        P.mm(sc[:, 0:128], kT[:, tt * 128:(tt + 1) * 128], qT[:, tt * 128:(tt + 1) * 128])
                        PT = ptr.next()
                        P.I("dve", "tensor_tensor", PT[:], sc[:, 0:128], Mh[:], op=ALU.mult)
                        o = psO.next()
                        P.mm(o[:, 0:256], PT[:], vtm[:, tt, :], start=True, stop=False)
                        P.mm(o[:, 0:256], qfT[:, tt * 128:(tt + 1) * 128], Sfb[:, tt, :], start=False, stop=False)
                        P.mm(o[:, 0:256], qbT[:, tt * 128:(tt + 1) * 128], Sbb[:, tt, :], start=False, stop=True)
                        junk = jr.next(); ss = ssr.next()
                        P.act(junk[:], o[:, 0:256], AF.Square, accum_out=ss[:])
                        rstd_from_ss(ss[:], 256)
                        og = ogr.next()
                        P.I("dve", "scalar_tensor_tensor", og[:], o[:, 0:256], ss[:, 0:1], gs[:, tt, :], op0=ALU.mult, op1=ALU.mult)
                        if pend_tr is not None:
                            ret_tr(*pend_tr)
                        pend_tr = (h, tt, og)
                    ret_tr(*pend_tr)

        def phaseB_hg(u, l, j, T, hT, oT, pre):
            NT = T // 128
            NCH = T // 64
            GB = 1024
            if u == "s":
                seqs = [list(range(32))]
            else:
                seqs = [list(range(4 * s, 4 * s + 4)) for s in range(4)]
            win = hg_w_in[j].rearrange("(c p) n -> p c n", p=128)
            with P.scope():
                wr = Ring(P, "sb", "whg", 2, [128, 8, 640], BF16)

                def load_panel(h):
                    w = wr.next()
                    for g in range(5):
                        P.d("pool", w[:, :, g * 128:(g + 1) * 128], win[:, :, g * 1024 + h * 128:g * 1024 + (h + 1) * 128], nowaw=True)
                    return w

                wnext = load_panel(0)
                pre()
                L = P.sb("hgL", [128, 2, 4, 8], F32)
                for d in range(2):
                    for e in range(4):
                        P.d("sp", L[:, d, e, :], hg_lb[d, e, :].rearrange("(h p) -> p h", p=128), allow_slow_non_contiguous=True, nowaw=True)
                P.act(L[:], L[:], AF.Exp)
                den = P.sb("hgden", [128, 2, 8], F32); lb = P.sb("hglb", [128, 2, 8], F32); omlb = P.sb("hgomlb", [128, 2, 8], F32)
                P.I("dve", "tensor_tensor", den[:], L[:, :, 0, :], L[:, :, 1, :], op=ALU.add)
                P.I("dve", "tensor_tensor", den[:], den[:], L[:, :, 2, :], op=ALU.add)
                P.I("dve", "tensor_tensor", den[:], den[:], L[:, :, 3, :], op=ALU.add)
                P.I("dve", "reciprocal", den[:], den[:])
                if l == 0:
                    P.I("dve", "memset", lb[:], 0.0)
                else:
                    P.I("dve", "tensor_copy", lb[:], L[:, :, 1, :])
                    for e in range(2, l + 1):
                        P.I("dve", "tensor_tensor", lb[:], lb[:], L[:, :, e, :], op=ALU.add)
                    P.I("dve", "tensor_tensor", lb[:], lb[:], den[:], op=ALU.mult)
                P.I("dve", "tensor_scalar", omlb[:], lb[:], -1.0, 1.0, op0=ALU.mult, op1=ALU.add)
                ngb = P.sb("ngb", [128, 128], F32)
                P.d("sp", ngb[:], hg_norm[j:j + 1, :].partition_broadcast(128))
                ones = P.sb("hgones", [128, GB], F32)
                P.I("dve", "memset", ones[:], 1.0)
                qT = P.sb("hqT", [128, T], BF16)
                outs = [[P.sb("hz%d%d" % (d, k), [128, T], BF16) for k in range(4)] for d in range(2)]
                khtm = [P.sb("khtm%d" % d, [128, NT, 128], BF16) for d in range(2)]
                dec = [P.sb("hdec%d" % d, [128, NCH], F32) for d in range(2)]
                vtm = P.sb("hvtm", [128, NT, 128], BF16); gs = P.sb("hgs", [128, NT, 128], BF16)
                Sb_ = [P.sb("hS%d" % d, [128, NCH, 128], BF16) for d in range(2)]
                S32 = Ring(P, "sb", "hS32", 6 if u == "s" else 24, [128, 128], F32)
                Gr = Ring(P, "sb", "hG", 2, [128, GB], F32); Ir = Ring(P, "sb", "hI", 2, [128, GB], F32); Ar = Ring(P, "sb", "hA", 2, [128, GB], F32)
                kkr = Ring(P, "sb", "hk", 2, [128, GB], BF16); exr = Ring(P, "sb", "hex", 4, [128, GB], BF16)
                ptr = Ring(P, "sb", "hPT", 2, [128, 128], BF16)
                onr = Ring(P, "sb", "hon", 2, [128, 128], F32)
                ogr = Ring(P, "sb", "hog", 3, [128, 128], BF16)
                jr = Ring(P, "sb", "hjr", 2, [128, 128], BF16)
                ssr = Ring(P, "sb", "hss", 4, [128, 1], F32)
                psA = Ring(P, "ps", "psA", 2, [128, 512], F32)
                psS = Ring(P, "ps", "psS", 2, [128, 512], F32)
                psO = Ring(P, "ps", "psO", 2, [128, 512], F32)
                psT = Ring(P, "ps", "psT", 2, [128, 1024], BF16)
                psP = RingU(psA, psO)
                masks = (cb[:, C_MF:C_MF + 128], cb[:, C_MB:C_MB + 128])
                for h in range(8):
                    W = wnext
                    if h + 1 < 8:
                        wnext = load_panel(h + 1)
                    for blk in range(T // 512):
                        ps = psP.next()
                        for kc in range(8):
                            P.mm(ps[:], W[:, kc, 0:128], hT[:, kc, blk * 512:(blk + 1) * 512], start=(kc == 0), stop=(kc == 7))
                        P.act(qT[:, blk * 512:(blk + 1) * 512], ps[:], AF.Silu)
                    nch = GB // 64
                    shp = [128, nch, 64]

                    def g_stage1(d, gb):
                        t0 = gb * GB
                        G = Gr.next(); I_ = Ir.next(); kk = kkr.next()
                        for b2 in range(GB // 512):
                            ps = psP.next()
                            for kc in range(8):
                                P.mm(ps[:], W[:, kc, 128 * (1 + d):128 * (2 + d)], hT[:, kc, t0 + b2 * 512:t0 + (b2 + 1) * 512],
                                     start=(kc == 0), stop=(kc == 7))
                            P.act(G[:, b2 * 512:(b2 + 1) * 512], ps[:], AF.Sigmoid)
                        P.I("dve", "tensor_scalar", G[:], G[:], omlb[:, d, h:h + 1], lb[:, d, h:h + 1], op0=ALU.mult, op1=ALU.add)
                        P.I("dve", "tensor_scalar", kk[:], G[:], -1.0, 1.0, op0=ALU.mult, op1=ALU.add)
                        P.act(G[:], G[:], AF.Ln)
                        P.I("dve", "tensor_tensor_scan", I_[:], ones[:], G[:], 0.0, op0=ALU.mult, op1=ALU.add)
                        P.I("dve", "tensor_tensor", G[:], I_[:], G[:], op=ALU.subtract)
                        return (d, gb, G, I_, kk)

                    def g_stage2(ctx):
                        d, gb, G, I_, kk = ctx
                        t0 = gb * GB
                        I3 = I_[:].re("p (c t) -> p c t", t=64); E3 = G[:].re("p (c t) -> p c t", t=64)
                        Z3 = I3 if d == 0 else E3
                        qs_ = qT[:, t0:t0 + GB]
                        o_qt, o_kt, o_qh, o_kh = [x[:, t0:t0 + GB] for x in outs[d]]
                        sq = 1.0 if d == 0 else -1.0
                        r1 = I3[:, :, 32:33] if d == 0 else E3[:, :, 31:32]
                        r3 = E3[:, :, 0:1] if d == 0 else I3[:, :, 63:64]
                        r4 = I3[:, :, 63:64] if d == 0 else E3[:, :, 0:1]
                        for ref, sgn, src, dst in ((r1, sq, qs_, o_qt), (r1, -sq, kk[:], o_kt), (r3, sq, qs_, o_qh), (r4, -sq, kk[:], o_kh)):
                            if sgn == sq or ref is not r1:
                                A_ = Ar.next(); A3 = A_[:].re("p (c t) -> p c t", t=64)
                                P.I("dve", "tensor_tensor", A3, Z3, ref.bc(shp), op=ALU.subtract)
                            ex = exr.next()
                            P.act(ex[:], A_[:], AF.Exp, scale=sgn)
                            P.I("dve", "tensor_tensor", dst, src, ex[:], op=ALU.mult)
                        c0 = gb * nch
                        P.I("dve", "tensor_tensor", dec[d][:, c0:c0 + nch], I3[:, :, 63], E3[:, :, 0], op=ALU.subtract)
                        P.act(dec[d][:, c0:c0 + nch], dec[d][:, c0:c0 + nch], AF.Exp)

                    pctx = None
                    for d in range(2):
                        for gb in range(T // GB):
                            ctx = g_stage1(d, gb)
                            if pctx is not None:
                                g_stage2(pctx)
                            pctx = ctx
                    g_stage2(pctx)
                    for d in range(2):
                        for t4 in range(0, NT, 4):
                            pt = psT.next()
                            for i in range(4):
                                P.tr(pt[:, i * 128:(i + 1) * 128], outs[d][3][:, (t4 + i) * 128:(t4 + i + 1) * 128], identb)
                            P.act(khtm[d][:, t4:t4 + 4, :], pt[:, 0:512].re("p (a b) -> p a b", a=4), AF.Copy)
                    for tt in range(NT):
                        ps = psP.next()
                        for kc in range(8):
                            P.mm(ps[:, 0:256], hT[:, kc, tt * 128:(tt + 1) * 128], W[:, kc, 384:640], start=(kc == 0), stop=(kc == 7))
                        P.act(vtm[:, tt, :], ps[:, 0:128], AF.Copy)
                        P.act(gs[:, tt, :], ps[:, 128:256], AF.Silu)
                    chains = []
                    for si, chunks in enumerate(seqs):
                        for d in range(2):
                            S = S32.next()
                            if u == "s":
                                P.d("sp", S[:], shg[d, h])
                            else:
                                P.I("pool", "memset", S[:], 0.0)
                            chains.append([si, d, chunks if d == 0 else chunks[::-1], S])
                    for k in range(len(chains[0][2])):
                        for ch in chains:
                            si, d, order, S = ch
                            c = order[k]
                            P.act(Sb_[d][:, c, :], S[:], AF.Copy)
                            tt, hf = c // 2, (c % 2) * 64
                            U = psS.next()
                            P.mm(U[:, 0:128], khtm[d][hf:hf + 64, tt, :], vtm[hf:hf + 64, tt, :])
                            Sn = S32.next()
                            P.I("dve", "scalar_tensor_tensor", Sn[:], S[:], dec[d][:, c:c + 1], U[:, 0:128], op0=ALU.mult, op1=ALU.add)
                            ch[3] = Sn
                    if u == "p":
                        for si, d, order, S in chains:
                            P.d("sp", nh[si, d, h], S[:])
                    def hg_tr(h_, tt_, og_):
                        pt = psT.next()
                        P.tr(pt[:, 0:128], og_[:], identb)
                        P.act(oT[:, h_, tt_ * 128:(tt_ + 1) * 128], pt[:, 0:128], AF.Copy)

                    pend_tr = None
                    for tt in range(NT):
                        PTs = []
                        for d in range(2):
                            sc = psS.next()
                            P.mm(sc[:, 0:128], outs[d][1][:, tt * 128:(tt + 1) * 128], outs[d][0][:, tt * 128:(tt + 1) * 128])
                            PT = ptr.next()
                            P.I("dve", "tensor_tensor", PT[:], sc[:, 0:128], masks[d], op=ALU.mult)
                            PTs.append(PT)
                        o = psO.next()
                        P.mm(o[:, 0:128], PTs[0][:], vtm[:, tt, :], start=True, stop=False)
                        for d in range(2):
                            for hf in range(2):
                                c = 2 * tt + hf
                                P.mm(o[hf * 64:(hf + 1) * 64, 0:128], outs[d][2][:, c * 64:(c + 1) * 64], Sb_[d][:, c, :],
                                     start=False, stop=False)
                        P.mm(o[:, 0:128], PTs[1][:], vtm[:, tt, :], start=False, stop=True)
                        junk = jr.next(); ss = ssr.next()
                        P.act(junk[:], o[:, 0:128], AF.Square, accum_out=ss[:])
                        rstd_from_ss(ss[:], 128)
                        on = onr.next()
                        P.I("dve", "scalar_tensor_tensor", on[:], o[:, 0:128], ss[:, 0:1], ngb[:], op0=ALU.mult, op1=ALU.mult)
                        og = ogr.next()
                        P.I("pool", "tensor_tensor", og[:], on[:], gs[:, tt, :], op=ALU.mult)
                        if pend_tr is not None:
                            hg_tr(*pend_tr)
                        pend_tr = (h, tt, og)
                    hg_tr(*pend_tr)

        for l in range(NL):
            for u in ("s", "p"):
                T = 2048 if u == "s" else 1024
                xin = xs if u == "s" else xp
                xout = ys if u == "s" else yp
                kind = l % 3
                j = l // 3
                KC = 16 if kind == 1 else 8
                xsrc = xin if l == 0 else xsc[u][l % 2]
                xdst = xout if l == NL - 1 else xsc[u][(l + 1) % 2]
                with P.scope():
                    oT = P.sb("oT", [128, KC, T], BF16)
                    wout = (da_w_out, ret_w_out, hg_w_out)[kind][j]
                    early_wo = (kind == 0) or (u == "p")
                    wo_pre = load_wo(KC, wout) if early_wo else None
                    with P.scope():
                        hT = P.sb("hT", [128, 8, T], BF16)

                        def pre(u=u, l=l, T=T, hT=hT, xsrc=xsrc):
                            phaseA(u, l, T, hT, xsrc)

                        if kind == 0:
                            phaseB_da(u, l, j, T, hT, oT, pre)
                        elif kind == 1:
                            phaseB_ret(u, l, j, T, hT, oT, pre)
                        else:
                            phaseB_hg(u, l, j, T, hT, oT, pre)
                    phaseC(u, l, T, oT, KC, wout, xsrc, xdst, wo=wo_pre)
    P.finish()
    return nc


_CACHE = {}


def kernel(x_prompt, x_sample, cache_k, cache_v, state_ret, state_hgrn, c, c_ctx,
           w_mod, b_mod, g_pre, g_post, da_w_in, da_w_out, da_lambda, da_subln,
           ret_w_in, ret_w_out, ret_decay, hg_w_in, hg_w_out, hg_lb, hg_norm, _NL=4):
    f = lambda a: np.ascontiguousarray(np.asarray(a, dtype=np.float32))
    x_prompt, x_sample, cache_k, cache_v, state_ret, state_hgrn, c, c_ctx = map(
        f, (x_prompt, x_sample, cache_k, cache_v, state_ret, state_hgrn, c, c_ctx))
    cstv, ropev = host_consts()
    shared = {
        "w_mod": f(w_mod), "b_mod": f(b_mod), "g_pre": f(g_pre), "g_post": f(g_post),
        "da_w_in": f(da_w_in), "da_w_out": f(da_w_out), "da_lambda": f(da_lambda).reshape(2, 256),
        "da_subln": f(da_subln), "ret_w_in": f(ret_w_in), "ret_w_out": f(ret_w_out),
        "ret_decay": f(ret_decay).reshape(1, 16), "hg_w_in": f(hg_w_in), "hg_w_out": f(hg_w_out),
        "hg_lb": f(hg_lb), "hg_norm": f(hg_norm), "cst": cstv, "rope": ropev,
    }
    in_maps = []
    for i in range(8):
        m = dict(shared)
        m["xs"] = x_sample[i]
        m["xp"] = x_prompt[4 * i:4 * i + 4].reshape(1024, D)
        m["ck"] = cache_k[i].reshape(2, 512, 1024)
        m["cv"] = cache_v[i].reshape(2, 512, 1024)
        m["sr"] = state_ret[i, 0]
        m["shg"] = state_hgrn[i, 0]
        m["cvec"] = np.ascontiguousarray(np.stack([c_ctx, c[i]], 0))
        in_maps.append(m)
    if _NL not in _CACHE:
        _CACHE[_NL] = build(_NL)
    nc = _CACHE[_NL]
    res = run_bass_kernel_spmd(nc, in_maps, core_ids=list(range(8)))
    R = res.results
    y_p = np.concatenate([r["yp"].reshape(4, 256, D) for r in R], 0)
    y_s = np.stack([r["ys"] for r in R], 0)
    n_k = np.concatenate([r["nk"].reshape(4, 2, 256, 16, 64) for r in R], 0)
    n_v = np.concatenate([r["nv"].reshape(4, 2, 256, 8, 128) for r in R], 0)
    n_r = np.concatenate([r["nr"].reshape(4, 1, 2, 8, 128, 256) for r in R], 0)
    n_h = np.concatenate([r["nh"].reshape(4, 1, 2, 8, 128, 128) for r in R], 0)
    return (y_p, y_s, n_k, n_v, n_r, n_h)
```

```python
import math
from contextlib import ExitStack, contextmanager

import numpy as np
import concourse.bass as bass
import concourse.mybir as mybir
from concourse.bass_utils import run_bass_kernel_spmd

F32 = mybir.dt.float32
BF16 = mybir.dt.bfloat16
AF = mybir.ActivationFunctionType
ALU = mybir.AluOpType
AX = mybir.AxisListType


class Buf:
    def __init__(self, name, t):
        self.name = name
        self.t = t
        self.lw = {}
        self.rd = {}
        self.kind = "sb"
        self.dsem = None
        self.dcnt = 0
        self.psum = False

    def __getitem__(self, i):
        return self.t[i]


class Prog:
    ENG = ("pe", "act", "dve", "pool", "sp")

    def __init__(self, nc):
        self.nc = nc
        self.q = {e: [] for e in self.ENG}
        self.esem = {}
        self.cnt = {}
        for e in ("pe", "act", "dve", "pool"):
            self.esem[e] = nc.alloc_semaphore(name="es_" + e)
            self.cnt[e] = 0
        self.seen = {e: {} for e in self.ENG}
        self.freed = {}
        self.dbufs = []
        self.allbufs = []
        self.scopes = []
        self.uid = 0
        self.dsems = {}
        self.free_dsems = {"sw": [], "hw": []}

    @contextmanager
    def scope(self):
        es = ExitStack()
        bufs = []
        self.scopes.append((es, bufs))
        try:
            yield
        finally:
            for b in bufs:
                if b.dsem is not None:
                    for qt, key in b.dsem.items():
                        self.free_dsems[qt].append(key)
                for t in list(b.lw.values()) + list(b.rd.values()):
                    k = t[0]
                    if k not in self.freed or self.freed[k][2] < t[2]:
                        self.freed[k] = t
            es.close()
            self.scopes.pop()

    def _mk(self, name, t):
        b = Buf(name, t)
        b.rd = dict(self.freed)
        self.scopes[-1][1].append(b)
        self.allbufs.append(b)
        return b

    def sb(self, name, shape, dt):
        self.uid += 1
        t = self.scopes[-1][0].enter_context(self.nc.sbuf_tensor("%s_%d" % (name, self.uid), list(shape), dt))
        return self._mk(name, t)

    def ps(self, name, shape, dt):
        self.uid += 1
        t = self.scopes[-1][0].enter_context(self.nc.psum_tensor("%s_%d" % (name, self.uid), list(shape), dt))
        b = self._mk(name, t)
        b.psum = True
        b.kind = "ps"
        return b

    def dram(self, name, shape, dt):
        t = self.nc.dram_tensor(name, list(shape), dt, kind="Internal").ap()
        b = Buf(name, t)
        b.kind = "dram"
        self.allbufs.append(b)
        return b

    def _need(self, eng, waits, tok, raw):
        if tok is None:
            return
        key, semh, val, teng = tok
        if teng == eng and not raw:
            return
        if self.seen[eng].get(key, 0) >= val:
            return
        if key not in waits or waits[key][1] < val:
            waits[key] = (semh, val)

    def _deps(self, eng, r, w, skipkey=None):
        waits = {}
        for b in r:
            for t in b.lw.values():
                self._need(eng, waits, t, True)
            if b.psum:
                for t in b.rd.values():
                    self._need(eng, waits, t, False)
        for b in w:
            for t in b.lw.values():
                if not (skipkey is not None and t[0] == skipkey):
                    self._need(eng, waits, t, False)
            for t in b.rd.values():
                self._need(eng, waits, t, False)
        for k, (s, v) in waits.items():
            self.seen[eng][k] = v
        return list(waits.values())

    def _reg(self, tok, r, w):
        for b in r:
            b.rd[tok[0]] = tok
        for b in w:
            b.lw[tok[0]] = tok
            b.rd = {}

    def op(self, eng, fn, r=(), w=()):
        waits = self._deps(eng, r, w)
        self.cnt[eng] += 1
        tok = (eng, self.esem[eng], self.cnt[eng], eng)
        self._reg(tok, r, w)
        self.q[eng].append((waits, fn, self.esem[eng], 1))

    def dma(self, q, out, in_, r=(), w=(), owner=None, nowaw=False, **kw):
        if owner is None:
            cand = [b for b in list(w) + list(r) if b.kind == "sb"]
            owner = cand[0] if cand else (w[0] if w else r[0])
        extra = None
        qt = "sw" if q == "pool" else "hw"
        if owner.dsem is None:
            owner.dsem = {}
        if qt not in owner.dsem:
            if self.free_dsems[qt]:
                key = self.free_dsems[qt].pop()
                extra = key
            else:
                key = ("d", len(self.dsems))
                self.dsems[key] = [self.nc.alloc_semaphore(name="ds_%d" % len(self.dsems)), 0]
            owner.dsem[qt] = key
        key = owner.dsem[qt]
        semh, tot = self.dsems[key]
        waits = self._deps(q, r, w, skipkey=key if nowaw else None)
        if extra is not None and tot > 0 and self.seen[q].get(key, 0) < tot:
            waits = [x for x in waits if x[0] is not semh] + [(semh, tot)]
            self.seen[q][key] = tot
        tot += 16
        self.dsems[key][1] = tot
        tok = (key, semh, tot, "dma")
        self._reg(tok, r, w)
        self.q[q].append((waits, (lambda e: e.dma_start(out=out, in_=in_, **kw)), semh, 16))

    def finish(self):
        nc = self.nc
        for key, (semh, tot) in self.dsems.items():
            if tot > 0:
                self.q["sp"].append(([(semh, tot)], None, None, 0))

        def run(e, items):
            for waits, fn, sem, inc in items:
                for s, v in waits:
                    e.wait_ge(s, v)
                if fn is not None:
                    fn(e).then_inc(sem, inc)

        with nc.Block() as blk:
            blk.tensor(lambda e: run(e, self.q["pe"]))
            blk.scalar(lambda e: run(e, self.q["act"]))
            blk.vector(lambda e: run(e, self.q["dve"]))
            blk.gpsimd(lambda e: run(e, self.q["pool"]))
            blk.sync(lambda e: run(e, self.q["sp"]))


class RingU:
    def __init__(self, *rings):
        self.b = [b for r in rings for b in r.b]
        self.i = 0

    def next(self):
        b = self.b[self.i % len(self.b)]
        self.i += 1
        return b


class Ring:
    def __init__(self, P, kind, name, n, shape, dt):
        mk = P.sb if kind == "sb" else P.ps
        self.b = [mk("%s%d" % (name, i), shape, dt) for i in range(n)]
        self.i = 0

    def next(self):
        b = self.b[self.i % len(self.b)]
        self.i += 1
        return b


class V:
    def __init__(self, b, ap):
        self.b = b
        self.ap = ap

    def __getitem__(self, i):
        return V(self.b, self.ap[i])

    def bc(self, shape):
        return V(self.b, self.ap.broadcast_to(list(shape)))

    def re(self, pat, **kw):
        return V(self.b, self.ap.rearrange(pat, **kw))

    def un(self, ax):
        return V(self.b, self.ap.unsqueeze(ax))

    def pbc(self, n=128):
        return V(self.b, self.ap.partition_broadcast(n))


def _bgi(self, i):
    return V(self, self.t[i])


Buf.__getitem__ = _bgi


def _u(x, lst):
    if isinstance(x, V):
        lst.append(x.b)
        return x.ap
    return x


def _I(P, eng, meth, out, *args, **kw):
    r, w = [], []
    o = _u(out, w)
    a = [_u(x, r) for x in args]
    k = {}
    for n, x in kw.items():
        k[n] = _u(x, w if n == "accum_out" else r)
    P.op(eng, (lambda e: getattr(e, meth)(o, *a, **k)), r=r, w=w)


def _act(P, out, in_, func, **kw):
    r, w = [], []
    o = _u(out, w)
    i = _u(in_, r)
    k = {}
    for n, x in kw.items():
        k[n] = _u(x, w if n == "accum_out" else r)
    P.op("act", (lambda e: e.activation(out=o, in_=i, func=func, **k)), r=r, w=w)


def _mm(P, out, lhsT, rhs, start=True, stop=True):
    r, w = [], []
    o = _u(out, w)
    a = _u(lhsT, r)
    b = _u(rhs, r)
    P.op("pe", (lambda e: e.matmul(o, a, b, start=start, stop=stop)), r=r, w=w)


def _tr(P, out, in_, ident):
    r, w = [], []
    o = _u(out, w)
    a = _u(in_, r)
    b = _u(ident, r)
    P.op("pe", (lambda e: e.transpose(out=o, in_=a, identity=b)), r=r, w=w)


def _dma(P, q, out, in_, **kw):
    r, w = [], []
    o = _u(out, w)
    i = _u(in_, r)
    P.dma(q, o, i, r=r, w=w, **kw)


Prog.I = _I
Prog.act = _act
Prog.mm = _mm
Prog.tr = _tr
Prog.d = _dma


D = 1024
EPS = 1e-6
BIG = 1.0e6
C_ID, C_SW, C_A, C_B, C_TL1, C_CTL, C_MF, C_MB = 0, 128, 256, 384, 512, 640, 768, 896
C_SDF, C_SDB = 1024, 1025
NCST = 1032


def host_consts():
    c = np.zeros((128, NCST), np.float32)
    i = np.arange(128)
    c[:, C_ID:C_ID + 128] = np.eye(128, dtype=np.float32)
    sw = np.zeros((128, 128), np.float32)
    sw[i, i ^ 1] = 1.0
    c[:, C_SW:C_SW + 128] = sw
    s = i[:, None].astype(np.float64)
    t = i[None, :].astype(np.float64)
    c[:, C_A:C_A + 128] = np.where(t >= s, t - s, BIG)
    c[:, C_B:C_B + 128] = np.where(s >= t, s - t, BIG)
    c[:, C_TL1:C_TL1 + 128] = np.broadcast_to(t + 1.0, (128, 128))
    c[:, C_CTL:C_CTL + 128] = np.broadcast_to(128.0 - t, (128, 128))
    same = (i[:, None] // 64) == (i[None, :] // 64)
    c[:, C_MF:C_MF + 128] = (same & (t >= s)).astype(np.float32)
    c[:, C_MB:C_MB + 128] = (same & (t <= s)).astype(np.float32)
    c[:, C_SDF] = 127.0 - i
    c[:, C_SDB] = i
    T = 2048
    tt = np.arange(T)
    row = (tt // 64).astype(np.float64)
    col = (tt % 64).astype(np.float64)
    inv = 10000.0 ** (-np.arange(16, dtype=np.float32) / 16).astype(np.float64)
    rope = np.zeros((128, 2, T), np.float32)
    for p in range(128):
        d = p % 64
        pr = d // 2
        ang = (row * inv[pr]) if pr < 16 else (col * inv[pr - 16])
        ang = ang.astype(np.float32).astype(np.float64)
        rope[p, 0] = np.cos(ang)
        rope[p, 1] = np.sin(ang) * (-1.0 if d % 2 == 0 else 1.0)
    return c, rope


def build(NL=4):
    nc = bass.Bass("TRN2", target_bir_lowering=False)
    P = Prog(nc)

    def din(n, s):
        return nc.dram_tensor(n, list(s), F32, kind="ExternalInput").ap()

    def dout(n, s):
        return nc.dram_tensor(n, list(s), F32, kind="ExternalOutput").ap()

    xs = din("xs", [2048, D]); xp = din("xp", [1024, D])
    ck = din("ck", [2, 512, 1024]); cv = din("cv", [2, 512, 1024])
    sr = din("sr", [2, 8, 128, 256]); shg = din("shg", [2, 8, 128, 128]); cvec = din("cvec", [2, D])
    w_mod = din("w_mod", [4, D, 3 * D]); b_mod = din("b_mod", [4, 3 * D])
    g_pre = din("g_pre", [4, D]); g_post = din("g_post", [4, D])
    da_w_in = din("da_w_in", [2, D, 4096]); da_w_out = din("da_w_out", [2, 1024, D])
    da_lambda = din("da_lambda", [2, 256]); da_subln = din("da_subln", [2, 128])
    ret_w_in = din("ret_w_in", [1, D, 6144]); ret_w_out = din("ret_w_out", [1, 2048, D]); ret_decay = din("ret_decay", [1, 16])
    hg_w_in = din("hg_w_in", [1, D, 5120]); hg_w_out = din("hg_w_out", [1, 1024, D])
    hg_lb = din("hg_lb", [2, 4, 1024]); hg_norm = din("hg_norm", [1, 128])
    cst = din("cst", [128, NCST]); rope = din("rope", [128, 2, 2048])
    yp = dout("yp", [1024, D]); ys = dout("ys", [2048, D])
    nk = dout("nk", [4, 2, 256, 1024]); nv = dout("nv", [4, 2, 256, 1024])
    nr = dout("nr", [4, 2, 8, 128, 256]); nh = dout("nh", [4, 2, 8, 128, 128])

    msc = P.dram("msc", [4, 2, 3 * D], F32)
    xsc = {"s": [P.dram("xsc_s%d" % i, [2048, D], F32) for i in range(2)],
           "p": [P.dram("xsc_p%d" % i, [1024, D], F32) for i in range(2)]}

    with P.scope():
        cf = P.sb("cf", [128, NCST], F32)
        P.d("sp", cf[:], cst)
        cb = P.sb("cb", [128, 1024], BF16)
        P.I("dve", "tensor_copy", cb[:], cf[:, 0:1024])
        identb = cb[:, C_ID:C_ID + 128]
        swb = cb[:, C_SW:C_SW + 128]

        def rstd_from_ss(ss, n):
            P.act(ss, ss, AF.Ln, scale=1.0 / n, bias=EPS)
            P.act(ss, ss, AF.Exp, scale=-0.5)

        with P.scope():
            cT = P.sb("cT", [128, 2, 8], F32)
            for v in range(2):
                P.d("sp", cT[:, v, :], cvec[v, :].rearrange("(c p) -> p c", p=128), allow_slow_non_contiguous=True, nowaw=True)
            scb = P.sb("scb", [128, 2, 8], BF16)
            P.act(scb[:], cT[:], AF.Silu)
            wr = Ring(P, "sb", "wm", 2, [128, 8, 512], BF16)
            psm = Ring(P, "ps", "psm", 2, [128, 512], F32)
            brow = P.sb("brow", [1, 3 * D], F32)
            mr = Ring(P, "sb", "mrow", 2, [1, 512], F32)
            for l in range(NL):
                P.d("sp", brow[:], b_mod[l:l + 1, :])
                wv = w_mod[l].rearrange("(c p) n -> p c n", p=128)
                for cbk in range(6):
                    w = wr.next()
                    P.d("pool", w[:], wv[:, :, cbk * 512:(cbk + 1) * 512])
                    for v in range(2):
                        ps = psm.next()
                        for kc in range(8):
                            P.mm(ps[0:1, :], scb[:, v, kc:kc + 1], w[:, kc, :], start=(kc == 0), stop=(kc == 7))
                        m = mr.next()
                        P.I("dve", "tensor_tensor", m[:], ps[0:1, :], brow[:, cbk * 512:(cbk + 1) * 512], op=ALU.add)
                        P.d("sp", msc[l, v:v + 1, cbk * 512:(cbk + 1) * 512], m[:], nowaw=True)

        def phaseA(u, l, T, hT, xsrc):
            v = 1 if u == "s" else 0
            with P.scope():
                shT = P.sb("shT", [128, 8], F32); scT = P.sb("scT", [128, 8], F32)
                gpT = P.sb("gpT", [128, 8], F32); AT = P.sb("AT", [128, 8], F32)
                P.d("sp", shT[:], msc[l, v, 0:1024].re("(c p) -> p c", p=128), allow_slow_non_contiguous=True)
                P.d("sp", scT[:], msc[l, v, 1024:2048].re("(c p) -> p c", p=128), allow_slow_non_contiguous=True)
                P.d("sp", gpT[:], g_pre[l, :].rearrange("(c p) -> p c", p=128), allow_slow_non_contiguous=True)
                P.I("dve", "scalar_tensor_tensor", AT[:], scT[:], 1.0, gpT[:], op0=ALU.add, op1=ALU.mult)
                xr = Ring(P, "sb", "xa", 3, [128, D], F32)
                xnr = Ring(P, "sb", "xn", 4, [128, D], BF16)
                jr = Ring(P, "sb", "jk", 2, [128, D], BF16)
                sr_ = Ring(P, "sb", "ssA", 4, [128, 1], F32)
                pst = Ring(P, "ps", "pstA", 2, [128, 1024], BF16)
                def a_stats(tt):
                    x = xr.next()
                    P.d("sp", x[:], xsrc[tt * 128:(tt + 1) * 128, :])
                    junk = jr.next(); ss = sr_.next()
                    P.act(junk[:], x[:], AF.Square, accum_out=ss[:])
                    rstd_from_ss(ss[:], D)
                    xn = xnr.next()
                    P.I("dve", "tensor_scalar", xn[:], x[:], ss[:, 0:1], None, op0=ALU.mult)
                    return xn

                def a_trev(tt, xn):
                    pt = pst.next()
                    for kc in range(8):
                        P.tr(pt[:, kc * 128:(kc + 1) * 128], xn[:, kc * 128:(kc + 1) * 128], identb)
                    for kc in range(8):
                        if kc % 2 == 0:
                            P.I("dve", "tensor_scalar", hT[:, kc, tt * 128:(tt + 1) * 128], pt[:, kc * 128:(kc + 1) * 128],
                                AT[:, kc:kc + 1], shT[:, kc:kc + 1], op0=ALU.mult, op1=ALU.add)
                        else:
                            P.act(hT[:, kc, tt * 128:(tt + 1) * 128], pt[:, kc * 128:(kc + 1) * 128], AF.Identity,
                                  scale=AT[:, kc:kc + 1], bias=shT[:, kc:kc + 1])

                pend_a = []
                for tt in range(T // 128):
                    pend_a.append((tt, a_stats(tt)))
                    if len(pend_a) > 2:
                        a_trev(*pend_a.pop(0))
                for it in pend_a:
                    a_trev(*it)

        def load_wo(KC, wout_ap):
            wo = P.sb("wo", [128, KC, D], BF16)
            wv = wout_ap.rearrange("(c p) n -> p c n", p=128)
            for c4 in range(0, KC, 4):
                P.d("pool", wo[:, c4:c4 + 4, :], wv[:, c4:c4 + 4, :], nowaw=True)
            return wo

        def phaseC(u, l, T, oT, KC, wout_ap, xsrc, xdst, wo=None):
            v = 1 if u == "s" else 0
            with P.scope():
                if wo is None:
                    wo = load_wo(KC, wout_ap)
                G2 = P.sb("G2", [128, D], F32)
                with P.scope():
                    gpo = P.sb("gpo", [128, D], F32)
                    P.d("sp", G2[:], msc[l, v:v + 1, 2048:3072].pbc(128))
                    P.d("sp", gpo[:], g_post[l:l + 1, :].partition_broadcast(128))
                    P.I("dve", "tensor_tensor", G2[:], G2[:], gpo[:], op=ALU.mult)
                psy = Ring(P, "ps", "psY", 2, [128, D], F32)
                xr = Ring(P, "sb", "xc", 2, [128, D], F32)
                tr_ = Ring(P, "sb", "tc", 2, [128, D], F32)
                jr = Ring(P, "sb", "jc", 2, [128, D], BF16)
                sr_ = Ring(P, "sb", "ssC", 4, [128, 1], F32)
                for tt in range(T // 128):
                    y = psy.next()
                    for hb in range(2):
                        for kc in range(KC):
                            P.mm(y[:, hb * 512:(hb + 1) * 512], oT[:, kc, tt * 128:(tt + 1) * 128],
                                 wo[:, kc, hb * 512:(hb + 1) * 512], start=(kc == 0), stop=(kc == KC - 1))
                    junk = jr.next(); ss = sr_.next()
                    P.act(junk[:], y[:], AF.Square, accum_out=ss[:])
                    rstd_from_ss(ss[:], D)
                    x = xr.next()
                    P.d("sp", x[:], xsrc[tt * 128:(tt + 1) * 128, :])
                    t = tr_.next()
                    P.I("dve", "scalar_tensor_tensor", t[:], y[:], ss[:, 0:1], G2[:], op0=ALU.mult, op1=ALU.mult)
                    P.I("pool", "tensor_tensor", x[:], x[:], t[:], op=ALU.add)
                    P.d("sp", xdst[tt * 128:(tt + 1) * 128, :], x[:], nowaw=True)

        def phaseB_da(u, l, j, T, hT, oT, pre):
            lam_init = 0.8 - 0.6 * math.exp(-0.3 * l)
            NT = T // 128
            QB = 256
            if u == "s":
                seqs = [(0, 2048, [(i * 128, i) for i in range(20)])]
                TK = 2560
            else:
                seqs = [(s * 256, 256, [((2 * s + i) * 128, 2 * s + i) for i in range(2)]) for s in range(4)]
                TK = 1024
            NKC = len(seqs[0][2])
            blocks = [(q0_ + qb * QB, kcs) for (q0_, nq, kcs) in seqs for qb in range(nq // QB)]
            win = da_w_in[j].rearrange("(c p) n -> p c n", p=128)
            with P.scope():
                wr = Ring(P, "sb", "wda", 2, [128, 8, 512], BF16)

                def load_panel(h):
                    w = wr.next()
                    for g in range(4):
                        P.d("pool", w[:, :, g * 128:(g + 1) * 128], win[:, :, g * 1024 + h * 128:g * 1024 + (h + 1) * 128], nowaw=True)
                    return w

                wnext = load_panel(0)
                pre()
                lv = P.sb("lv", [128, 256], F32)
                P.d("sp", lv[:], da_lambda[j:j + 1, :].partition_broadcast(128))
                lv4 = lv[:].re("p (a b d) -> p a b d", a=2, b=2)
                pr = P.sb("lpr", [128, 2, 64], F32)
                P.I("dve", "tensor_tensor", pr[:], lv4[:, :, 0, :], lv4[:, :, 1, :], op=ALU.mult)
                s2 = P.sb("ls2", [128, 2], F32)
                P.I("dve", "tensor_reduce", s2[:], pr[:], axis=AX.X, op=ALU.add)
                P.act(s2[:], s2[:], AF.Exp)
                nlam = P.sb("nlam", [128, 1], F32)
                P.I("dve", "tensor_tensor", nlam[:], s2[:, 1:2], s2[:, 0:1], op=ALU.subtract)
                P.I("dve", "tensor_scalar", nlam[:], nlam[:], -lam_init, None, op0=ALU.add)
                sgT = P.sb("sgT", [128, 1], F32)
                P.d("sp", sgT[:], da_subln[j, :].rearrange("(p o) -> p o", o=1), allow_slow_non_contiguous=True)
                P.I("dve", "tensor_scalar", sgT[:], sgT[:], 1.0 - lam_init, None, op0=ALU.mult)
                onesb = P.sb("onesb", [128, 128], BF16)
                P.I("dve", "memset", onesb[:], 1.0)
                if u == "s":
                    ropeb = P.sb("ropeb", [128, 2, 2048], BF16)
                    P.d("pool", ropeb[:], rope)
                qz = [P.sb("qz%d" % i, [128, T], BF16) for i in range(2)]
                P.I("dve", "memset", qz[0][64:128, :], 0.0)
                P.I("dve", "memset", qz[1][0:64, :], 0.0)
                kf = P.sb("kf", [128, TK], BF16)
                gT = P.sb("gT", [128, T], BF16)
                vsb = P.sb("vsb", [128, TK // 128, 128], BF16)
                ptr = Ring(P, "sb", "pT", 2, [128, NKC, 2 * QB], BF16)
                q0r = Ring(P, "sb", "q0", 2, [128, 512], BF16)
                t1r = Ring(P, "sb", "t1", 2, [128, 512], F32)
                t2r = Ring(P, "sb", "t2", 2, [128, 512], F32)
                ckt = P.sb("ckt", [128, 4, 128], BF16)
                kst = Ring(P, "sb", "kst", 2, [128, 256], F32)
                rvr = Ring(P, "sb", "rinv", 2, [128, 2 * QB], F32)
                o1r = Ring(P, "sb", "o1", 2, [128, QB], F32)
                o2r = Ring(P, "sb", "o2", 2, [128, QB], F32)
                sqr = Ring(P, "sb", "sqd", 2, [128, QB], BF16)
                rsr = Ring(P, "sb", "rsd", 2, [128, QB], F32)
                par = Ring(P, "sb", "padd", 4, [128, 2 * QB], BF16)
                psA = Ring(P, "ps", "psA", 2, [128, 512], F32)
                psS = Ring(P, "ps", "psS", 2, [128, 512], F32)
                psO = Ring(P, "ps", "psO", 4, [128, 512], F32)
                psP = RingU(psA, psS)

                def epi1(accO, accS):
                    rinv = rvr.next()
                    if u == "p":
                        P.act(rinv[:], accS[:], AF.Ln)
                        P.act(rinv[:], rinv[:], AF.Exp, scale=-1.0)
                    else:
                        P.I("dve", "reciprocal", rinv[:], accS[:])
                    o1 = o1r.next(); o2 = o2r.next()
                    P.I("dve", "tensor_tensor", o1[:], accO[:, 0:QB], rinv[:, 0:QB], op=ALU.mult)
                    P.I("dve", "scalar_tensor_tensor", o2[:], accO[:, QB:2 * QB], nlam[:, 0:1], rinv[:, QB:2 * QB], op0=ALU.mult, op1=ALU.mult)
                    P.I("pool", "tensor_tensor", o1[:], o1[:], o2[:], op=ALU.add)
                    return [o1, o2, None]

                def epi1b(st):
                    sq = sqr.next()
                    P.act(sq[:], st[0][:], AF.Square)
                    st[2] = sq

                def epi2(h, qlo, accS, st):
                    if st[2] is None:
                        epi1b(st)
                    o1, o2, sq = st
                    P.mm(accS[:, 0:QB], onesb[:], sq[:])
                    rs = rsr.next()
                    P.act(rs[:], accS[:, 0:QB], AF.Ln, scale=1.0 / 128, bias=EPS)
                    P.act(rs[:], rs[:], AF.Exp, scale=-0.5)
                    P.I("dve", "scalar_tensor_tensor", o2[:], o1[:], sgT[:, 0:1], rs[:], op0=ALU.mult, op1=ALU.mult)
                    P.I("pool", "tensor_tensor", oT[:, h, qlo:qlo + QB], o2[:], gT[:, qlo:qlo + QB], op=ALU.mult)

                for h in range(8):
                    W = wnext
                    if h + 1 < 8:
                        wnext = load_panel(h + 1)
                    for which, dst, c0 in (("q", None, 0), ("k", kf, 128), ("g", gT, 384)):
                        for blk in range(T // 512):
                            ps = psP.next()
                            for kc in range(8):
                                P.mm(ps[:], W[:, kc, c0:c0 + 128], hT[:, kc, blk * 512:(blk + 1) * 512], start=(kc == 0), stop=(kc == 7))
                            bs = slice(blk * 512, (blk + 1) * 512)
                            dsl = dst[:, bs] if dst is not None else None
                            if which == "g":
                                P.act(dsl, ps[:], AF.Silu)
                                continue
                            scl = 0.125 if which == "q" else 1.0
                            if u == "s":
                                q0 = q0r.next()
                                P.I("dve", "tensor_scalar", q0[:], ps[:], scl, None, op0=ALU.mult)
                                sw = psP.next()
                                P.mm(sw[:], swb, q0[:])
                                t1 = t1r.next(); t2 = t2r.next()
                                P.I("pool", "tensor_tensor", t1[:], q0[:], ropeb[:, 0, blk * 512:(blk + 1) * 512], op=ALU.mult)
                                P.I("dve", "tensor_tensor", t2[:], sw[:], ropeb[:, 1, blk * 512:(blk + 1) * 512], op=ALU.mult)
                                if which == "q":
                                    P.I("pool", "tensor_tensor", qz[0][0:64, bs], t1[0:64, :], t2[0:64, :], op=ALU.add)
                                    P.I("pool", "tensor_tensor", qz[1][64:128, bs], t1[64:128, :], t2[64:128, :], op=ALU.add)
                                else:
                                    P.I("pool", "tensor_tensor", dsl, t1[:], t2[:], op=ALU.add)
                            elif which == "q":
                                P.I("dve", "tensor_scalar", qz[0][0:64, bs], ps[0:64, :], scl, None, op0=ALU.mult)
                                P.I("dve", "tensor_scalar", qz[1][64:128, bs], ps[64:128, :], scl, None, op0=ALU.mult)
                            else:
                                P.I("dve", "tensor_scalar", dsl, ps[:], scl, None, op0=ALU.mult)
                    if u == "s":
                        P.d("pool", ckt[:], ck[j].rearrange("(c p) n -> p c n", p=128)[:, :, h * 128:(h + 1) * 128])
                        P.d("pool", vsb[:, 16:20, :], cv[j].rearrange("(c p) n -> p c n", p=128)[:, :, h * 128:(h + 1) * 128])
                        pc = psP.next()
                        for c in range(4):
                            P.mm(pc[:, c * 128:(c + 1) * 128], ckt[:, c, :], identb)
                        P.I("dve", "tensor_copy", kf[:, 2048:2560], pc[:])
                    for tt in range(NT):
                        ps = psP.next()
                        c0 = 128 if u == "p" else 256
                        ncol = 384 - c0
                        for kc in range(8):
                            P.mm(ps[:, 0:ncol], hT[:, kc, tt * 128:(tt + 1) * 128], W[:, kc, c0:384], start=(kc == 0), stop=(kc == 7))
                        P.act(vsb[:, tt, :], ps[:, ncol - 128:ncol], AF.Copy)
                        if u == "p":
                            st = kst.next()
                            P.I("dve", "tensor_copy", st[:], ps[:, 0:256])
                            sq_, tl = tt // 2, tt % 2
                            P.d("sp", nk[sq_, j, tl * 128:(tl + 1) * 128, h * 128:(h + 1) * 128], st[:, 0:128])
                            P.d("sp", nv[sq_, j, tl * 128:(tl + 1) * 128, h * 128:(h + 1) * 128], st[:, 128:256])
                    prev = None
                    pend = None
                    for blk in blocks + [None]:
                        if blk is not None:
                            pT = ptr.next()
                        if prev is not None:
                            accO = psO.next(); accS = psO.next()
                        for i in range(NKC):
                            if blk is not None:
                                qlo, kcs = blk
                                koff, vi = kcs[i]
                                sc = psS.next()
                                for s in range(2):
                                    P.mm(sc[:, s * QB:(s + 1) * QB], kf[:, koff:koff + 128], qz[s][:, qlo:qlo + QB])
                                P.act(pT[:, i, :], sc[:], AF.Exp)
                            if prev is not None:
                                pqlo, pkcs, ppT = prev
                                P.mm(accO[:], vsb[:, pkcs[i][1], :], ppT[:, i, :], start=(i == 0), stop=(i == NKC - 1))
                                P.mm(accS[:], onesb[:], ppT[:, i, :], start=(i == 0), stop=(i == NKC - 1))
                            if pend is not None and i == min(NKC - 1, 6) and pend[3][2] is None:
                                epi1b(pend[3])
                            if pend is not None and i == min(NKC - 1, 12):
                                epi2(*pend)
                                pend = None
                        if pend is not None:
                            epi2(*pend)
                            pend = None
                        if prev is not None:
                            pend = (h, prev[0], accS, epi1(accO, accS))
                        prev = (blk[0], blk[1], pT) if blk is not None else None
                    if pend is not None:
                        epi2(*pend)

        def phaseB_ret(u, l, j, T, hT, oT, pre):
            NT = T // 128
            SC = 128.0 ** -0.5
            if u == "s":
                seqs = [list(range(16))]
            else:
                seqs = [[2 * s, 2 * s + 1] for s in range(4)]
            win = ret_w_in[j].rearrange("(c p) n -> p c n", p=128)
            with P.scope():
                wr = Ring(P, "sb", "wrt", 2, [128, 8, 768], BF16)

                def load_panel(h):
                    w = wr.next()
                    P.d("pool", w[:, :, 0:128], win[:, :, h * 128:(h + 1) * 128], nowaw=True)
                    P.d("pool", w[:, :, 128:256], win[:, :, 1024 + h * 128:1024 + (h + 1) * 128], nowaw=True)
                    P.d("pool", w[:, :, 256:512], win[:, :, 2048 + h * 256:2048 + (h + 1) * 256], nowaw=True)
                    P.d("pool", w[:, :, 512:768], win[:, :, 4096 + h * 256:4096 + (h + 1) * 256], nowaw=True)
                    return w

                wnext = load_panel(0)
                pre()
                lg = P.sb("lg", [128, 16], F32)
                P.d("sp", lg[:], ret_decay[j:j + 1, :].partition_broadcast(128))
                P.act(lg[:], lg[:], AF.Exp)
                P.act(lg[:], lg[:], AF.Ln, scale=-1.0, bias=1.0)
                qT = P.sb("qT", [128, T], BF16); kT = P.sb("kT", [128, T], BF16)
                qfT = P.sb("qfT", [128, T], BF16); qbT = P.sb("qbT", [128, T], BF16)
                kmf = P.sb("kmf", [128, NT, 128], BF16); kmb = P.sb("kmb", [128, NT, 128], BF16)
                vtm = P.sb("vtm", [128, NT, 256], BF16); gs = P.sb("gsr", [128, NT, 256], BF16)
                Sfb = P.sb("Sfb", [128, NT, 256], BF16); Sbb = P.sb("Sbb", [128, NT, 256], BF16)
                S32 = Ring(P, "sb", "S32", 6 if u == "s" else 24, [128, 256], F32)
                m1 = P.sb("m1", [128, 128], F32); m2 = P.sb("m2", [128, 128], F32); Mh = P.sb("Mh", [128, 128], F32)
                qdf = P.sb("qdf", [128, 128], BF16); qdb = P.sb("qdb", [128, 128], BF16)
                kd = P.sb("kd", [128, 4], F32)
                ptr = Ring(P, "sb", "PTs", 2, [128, 128], BF16)
                ogr = Ring(P, "sb", "ogr", 3, [128, 256], BF16)
                jr = Ring(P, "sb", "jrr", 2, [128, 256], BF16)
                ssr = Ring(P, "sb", "ssr", 4, [128, 1], F32)
                psA = Ring(P, "ps", "psA", 2, [128, 512], F32)
                psS = Ring(P, "ps", "psS", 2, [128, 512], F32)
                psO = Ring(P, "ps", "psO", 2, [128, 512], F32)
                psT = Ring(P, "ps", "psT", 2, [128, 1024], BF16)
                psP = RingU(psA, psO)
                for h in range(8):
                    W = wnext
                    if h + 1 < 8:
                        wnext = load_panel(h + 1)
                    lgf = lg[:, h:h + 1]; lgb = lg[:, 8 + h:9 + h]
                    P.act(m1[:], cf[:, C_A:C_A + 128], AF.Exp, scale=lgf)
                    P.act(m2[:], cf[:, C_B:C_B + 128], AF.Exp, scale=lgb)
                    P.I("dve", "tensor_tensor", m1[:], m1[:], m2[:], op=ALU.add)
                    P.I("dve", "tensor_scalar", Mh[:], m1[:], SC, None, op0=ALU.mult)
                    P.act(qdf[:], cf[:, C_TL1:C_TL1 + 128], AF.Exp, scale=lgf)
                    P.act(qdb[:], cf[:, C_CTL:C_CTL + 128], AF.Exp, scale=lgb)
                    P.act(kd[:, 0:1], cf[:, C_SDF:C_SDF + 1], AF.Exp, scale=lgf)
                    P.act(kd[:, 1:2], cf[:, C_SDB:C_SDB + 1], AF.Exp, scale=lgb)
                    P.I("dve", "tensor_scalar", kd[:, 0:2], kd[:, 0:2], SC, None, op0=ALU.mult)
                    P.act(kd[:, 2:3], lgf, AF.Exp, scale=128.0)
                    P.act(kd[:, 3:4], lgb, AF.Exp, scale=128.0)
                    for dst, c0 in ((qT, 0), (kT, 128)):
                        for blk in range(T // 512):
                            ps = psP.next()
                            for kc in range(8):
                                P.mm(ps[:], W[:, kc, c0:c0 + 128], hT[:, kc, blk * 512:(blk + 1) * 512], start=(kc == 0), stop=(kc == 7))
                            P.act(dst[:, blk * 512:(blk + 1) * 512], ps[:], AF.Copy)
                    P.I("dve", "tensor_tensor", qfT[:].re("p (n t) -> p n t", t=128), qT[:].re("p (n t) -> p n t", t=128),
                        qdf[:].un(1).bc([128, NT, 128]), op=ALU.mult)
                    P.I("dve", "tensor_tensor", qbT[:].re("p (n t) -> p n t", t=128), qT[:].re("p (n t) -> p n t", t=128),
                        qdb[:].un(1).bc([128, NT, 128]), op=ALU.mult)
                    for tt in range(NT):
                        ps = psP.next()
                        for kc in range(8):
                            P.mm(ps[:, 0:384], hT[:, kc, tt * 128:(tt + 1) * 128], W[:, kc, 128:512], start=(kc == 0), stop=(kc == 7))
                        P.I("dve", "tensor_scalar", kmf[:, tt, :], ps[:, 0:128], kd[:, 0:1], None, op0=ALU.mult)
                        P.I("dve", "tensor_scalar", kmb[:, tt, :], ps[:, 0:128], kd[:, 1:2], None, op0=ALU.mult)
                        P.act(vtm[:, tt, :], ps[:, 128:384], AF.Copy)
                        ps2 = psP.next()
                        for kc in range(8):
                            P.mm(ps2[:, 0:256], hT[:, kc, tt * 128:(tt + 1) * 128], W[:, kc, 512:768], start=(kc == 0), stop=(kc == 7))
                        P.act(gs[:, tt, :], ps2[:, 0:256], AF.Silu)
                    chains = []
                    for si, tiles in enumerate(seqs):
                        for d, (order, km, Sb_, cdec) in enumerate(((tiles, kmf, Sfb, kd[:, 2:3]), (tiles[::-1], kmb, Sbb, kd[:, 3:4]))):
                            S = S32.next()
                            if u == "s":
                                P.d("sp", S[:], sr[d, h])
                            else:
                                P.I("pool", "memset", S[:], 0.0)
                            chains.append([si, d, order, km, Sb_, cdec, S])
                    for k in range(len(chains[0][2])):
                        for ch in chains:
                            si, d, order, km, Sb_, cdec, S = ch
                            tt = order[k]
                            P.act(Sb_[:, tt, :], S[:], AF.Copy)
                            U = psS.next()
                            P.mm(U[:, 0:256], km[:, tt, :], vtm[:, tt, :])
                            Sn = S32.next()
                            P.I("dve", "scalar_tensor_tensor", Sn[:], S[:], cdec, U[:, 0:256], op0=ALU.mult, op1=ALU.add)
                            ch[6] = Sn
                    if u == "p":
                        for si, d, order, km, Sb_, cdec, S in chains:
                            P.d("sp", nr[si, d, h], S[:])
                    def ret_tr(h_, tt_, og_):
                        pt = psT.next()
                        for e in range(2):
                            P.tr(pt[:, e * 128:(e + 1) * 128], og_[:, e * 128:(e + 1) * 128], identb)
                        P.act(oT[:, 2 * h_:2 * h_ + 2, tt_ * 128:(tt_ + 1) * 128], pt[:, 0:256].re("p (e t) -> p e t", e=2), AF.Copy)

                    pend_tr = None
                    for tt in range(NT):
                        sc = psS.next()
                        P.mm(sc[:, 0:128], kT[:, tt * 128:(tt + 1) * 128], qT[:, tt * 128:(tt + 1) * 128])
                        PT = ptr.next()
                        P.I("dve", "tensor_tensor", PT[:], sc[:, 0:128], Mh[:], op=ALU.mult)
                        o = psO.next()
                        P.mm(o[:, 0:256], PT[:], vtm[:, tt, :], start=True, stop=False)
                        P.mm(o[:, 0:256], qfT[:, tt * 128:(tt + 1) * 128], Sfb[:, tt, :], start=False, stop=False)
                        P.mm(o[:, 0:256], qbT[:, tt * 128:(tt + 1) * 128], Sbb[:, tt, :], start=False, stop=True)
                        junk = jr.next(); ss = ssr.next()
                        P.act(junk[:], o[:, 0:256], AF.Square, accum_out=ss[:])
                        rstd_from_ss(ss[:], 256)
                        og = ogr.next()
                        P.I("dve", "scalar_tensor_tensor", og[:], o[:, 0:256], ss[:, 0:1], gs[:, tt, :], op0=ALU.mult, op1=ALU.mult)
                        if pend_tr is not None:
                            ret_tr(*pend_tr)
                        pend_tr = (h, tt, og)
                    ret_tr(*pend_tr)

        def phaseB_hg(u, l, j, T, hT, oT, pre):
            NT = T // 128
            NCH = T // 64
            GB = 1024
            if u == "s":
                seqs = [list(range(32))]
            else:
                seqs = [list(range(4 * s, 4 * s + 4)) for s in range(4)]
            win = hg_w_in[j].rearrange("(c p) n -> p c n", p=128)
            with P.scope():
                wr = Ring(P, "sb", "whg", 2, [128, 8, 640], BF16)

                def load_panel(h):
                    w = wr.next()
                    for g in range(5):
                        P.d("pool", w[:, :, g * 128:(g + 1) * 128], win[:, :, g * 1024 + h * 128:g * 1024 + (h + 1) * 128], nowaw=True)
                    return w

                wnext = load_panel(0)
                pre()
                L = P.sb("hgL", [128, 2, 4, 8], F32)
                for d in range(2):
                    for e in range(4):
                        P.d("sp", L[:, d, e, :], hg_lb[d, e, :].rearrange("(h p) -> p h", p=128), allow_slow_non_contiguous=True, nowaw=True)
                P.act(L[:], L[:], AF.Exp)
                den = P.sb("hgden", [128, 2, 8], F32); lb = P.sb("hglb", [128, 2, 8], F32); omlb = P.sb("hgomlb", [128, 2, 8], F32)
                P.I("dve", "tensor_tensor", den[:], L[:, :, 0, :], L[:, :, 1, :], op=ALU.add)
                P.I("dve", "tensor_tensor", den[:], den[:], L[:, :, 2, :], op=ALU.add)
                P.I("dve", "tensor_tensor", den[:], den[:], L[:, :, 3, :], op=ALU.add)
                P.I("dve", "reciprocal", den[:], den[:])
                if l == 0:
                    P.I("dve", "memset", lb[:], 0.0)
                else:
                    P.I("dve", "tensor_copy", lb[:], L[:, :, 1, :])
                    for e in range(2, l + 1):
                        P.I("dve", "tensor_tensor", lb[:], lb[:], L[:, :, e, :], op=ALU.add)
                    P.I("dve", "tensor_tensor", lb[:], lb[:], den[:], op=ALU.mult)
                P.I("dve", "tensor_scalar", omlb[:], lb[:], -1.0, 1.0, op0=ALU.mult, op1=ALU.add)
                ngb = P.sb("ngb", [128, 128], F32)
                P.d("sp", ngb[:], hg_norm[j:j + 1, :].partition_broadcast(128))
                ones = P.sb("hgones", [128, GB], F32)
                P.I("dve", "memset", ones[:], 1.0)
                qT = P.sb("hqT", [128, T], BF16)
                outs = [[P.sb("hz%d%d" % (d, k), [128, T], BF16) for k in range(4)] for d in range(2)]
                khtm = [P.sb("khtm%d" % d, [128, NT, 128], BF16) for d in range(2)]
                dec = [P.sb("hdec%d" % d, [128, NCH], F32) for d in range(2)]
                vtm = P.sb("hvtm", [128, NT, 128], BF16); gs = P.sb("hgs", [128, NT, 128], BF16)
                Sb_ = [P.sb("hS%d" % d, [128, NCH, 128], BF16) for d in range(2)]
                S32 = Ring(P, "sb", "hS32", 6 if u == "s" else 24, [128, 128], F32)
                Gr = Ring(P, "sb", "hG", 2, [128, GB], F32); Ir = Ring(P, "sb", "hI", 2, [128, GB], F32); Ar = Ring(P, "sb", "hA", 2, [128, GB], F32)
                kkr = Ring(P, "sb", "hk", 2, [128, GB], BF16); exr = Ring(P, "sb", "hex", 4, [128, GB], BF16)
                ptr = Ring(P, "sb", "hPT", 2, [128, 128], BF16)
                onr = Ring(P, "sb", "hon", 2, [128, 128], F32)
                ogr = Ring(P, "sb", "hog", 3, [128, 128], BF16)
                jr = Ring(P, "sb", "hjr", 2, [128, 128], BF16)
                ssr = Ring(P, "sb", "hss", 4, [128, 1], F32)
                psA = Ring(P, "ps", "psA", 2, [128, 512], F32)
                psS = Ring(P, "ps", "psS", 2, [128, 512], F32)
                psO = Ring(P, "ps", "psO", 2, [128, 512], F32)
                psT = Ring(P, "ps", "psT", 2, [128, 1024], BF16)
                psP = RingU(psA, psO)
                masks = (cb[:, C_MF:C_MF + 128], cb[:, C_MB:C_MB + 128])
                for h in range(8):
                    W = wnext
                    if h + 1 < 8:
                        wnext = load_panel(h + 1)
                    for blk in range(T // 512):
                        ps = psP.next()
                        for kc in range(8):
                            P.mm(ps[:], W[:, kc, 0:128], hT[:, kc, blk * 512:(blk + 1) * 512], start=(kc == 0), stop=(kc == 7))
                        P.act(qT[:, blk * 512:(blk + 1) * 512], ps[:], AF.Silu)
                    nch = GB // 64
                    shp = [128, nch, 64]

                    def g_stage1(d, gb):
                        t0 = gb * GB
                        G = Gr.next(); I_ = Ir.next(); kk = kkr.next()
                        for b2 in range(GB // 512):
                            ps = psP.next()
                            for kc in range(8):
                                P.mm(ps[:], W[:, kc, 128 * (1 + d):128 * (2 + d)], hT[:, kc, t0 + b2 * 512:t0 + (b2 + 1) * 512],
                                     start=(kc == 0), stop=(kc == 7))
                            P.act(G[:, b2 * 512:(b2 + 1) * 512], ps[:], AF.Sigmoid)
                        P.I("dve", "tensor_scalar", G[:], G[:], omlb[:, d, h:h + 1], lb[:, d, h:h + 1], op0=ALU.mult, op1=ALU.add)
                        P.I("dve", "tensor_scalar", kk[:], G[:], -1.0, 1.0, op0=ALU.mult, op1=ALU.add)
                        P.act(G[:], G[:], AF.Ln)
                        P.I("dve", "tensor_tensor_scan", I_[:], ones[:], G[:], 0.0, op0=ALU.mult, op1=ALU.add)
                        P.I("dve", "tensor_tensor", G[:], I_[:], G[:], op=ALU.subtract)
                        return (d, gb, G, I_, kk)

                    def g_stage2(ctx):
                        d, gb, G, I_, kk = ctx
                        t0 = gb * GB
                        I3 = I_[:].re("p (c t) -> p c t", t=64); E3 = G[:].re("p (c t) -> p c t", t=64)
                        Z3 = I3 if d == 0 else E3
                        qs_ = qT[:, t0:t0 + GB]
                        o_qt, o_kt, o_qh, o_kh = [x[:, t0:t0 + GB] for x in outs[d]]
                        sq = 1.0 if d == 0 else -1.0
                        r1 = I3[:, :, 32:33] if d == 0 else E3[:, :, 31:32]
                        r3 = E3[:, :, 0:1] if d == 0 else I3[:, :, 63:64]
                        r4 = I3[:, :, 63:64] if d == 0 else E3[:, :, 0:1]
                        for ref, sgn, src, dst in ((r1, sq, qs_, o_qt), (r1, -sq, kk[:], o_kt), (r3, sq, qs_, o_qh), (r4, -sq, kk[:], o_kh)):
                            if sgn == sq or ref is not r1:
                                A_ = Ar.next(); A3 = A_[:].re("p (c t) -> p c t", t=64)
                                P.I("dve", "tensor_tensor", A3, Z3, ref.bc(shp), op=ALU.subtract)
                            ex = exr.next()
                            P.act(ex[:], A_[:], AF.Exp, scale=sgn)
                            P.I("dve", "tensor_tensor", dst, src, ex[:], op=ALU.mult)
                        c0 = gb * nch
                        P.I("dve", "tensor_tensor", dec[d][:, c0:c0 + nch], I3[:, :, 63], E3[:, :, 0], op=ALU.subtract)
                        P.act(dec[d][:, c0:c0 + nch], dec[d][:, c0:c0 + nch], AF.Exp)

                    pctx = None
                    for d in range(2):
                        for gb in range(T // GB):
                            ctx = g_stage1(d, gb)
                            if pctx is not None:
                                g_stage2(pctx)
                            pctx = ctx
                    g_stage2(pctx)
                    for d in range(2):
                        for t4 in range(0, NT, 4):
                            pt = psT.next()
                            for i in range(4):
                                P.tr(pt[:, i * 128:(i + 1) * 128], outs[d][3][:, (t4 + i) * 128:(t4 + i + 1) * 128], identb)
                            P.act(khtm[d][:, t4:t4 + 4, :], pt[:, 0:512].re("p (a b) -> p a b", a=4), AF.Copy)
                    for tt in range(NT):
                        ps = psP.next()
                        for kc in range(8):
                            P.mm(ps[:, 0:256], hT[:, kc, tt * 128:(tt + 1) * 128], W[:, kc, 384:640], start=(kc == 0), stop=(kc == 7))
                        P.act(vtm[:, tt, :], ps[:, 0:128], AF.Copy)
                        P.act(gs[:, tt, :], ps[:, 128:256], AF.Silu)
                    chains = []
                    for si, chunks in enumerate(seqs):
                        for d in range(2):
                            S = S32.next()
                            if u == "s":
                                P.d("sp", S[:], shg[d, h])
                            else:
                                P.I("pool", "memset", S[:], 0.0)
                            chains.append([si, d, chunks if d == 0 else chunks[::-1], S])
                    for k in range(len(chains[0][2])):
                        for ch in chains:
                            si, d, order, S = ch
                            c = order[k]
                            P.act(Sb_[d][:, c, :], S[:], AF.Copy)
                            tt, hf = c // 2, (c % 2) * 64
                            U = psS.next()
                            P.mm(U[:, 0:128], khtm[d][hf:hf + 64, tt, :], vtm[hf:hf + 64, tt, :])
                            Sn = S32.next()
                            P.I("dve", "scalar_tensor_tensor", Sn[:], S[:], dec[d][:, c:c + 1], U[:, 0:128], op0=ALU.mult, op1=ALU.add)
                            ch[3] = Sn
                    if u == "p":
                        for si, d, order, S in chains:
                            P.d("sp", nh[si, d, h], S[:])
                    def hg_tr(h_, tt_, og_):
                        pt = psT.next()
                        P.tr(pt[:, 0:128], og_[:], identb)
                        P.act(oT[:, h_, tt_ * 128:(tt_ + 1) * 128], pt[:, 0:128], AF.Copy)

                    pend_tr = None
                    for tt in range(NT):
                        PTs = []
                        for d in range(2):
                            sc = psS.next()
                            P.mm(sc[:, 0:128], outs[d][1][:, tt * 128:(tt + 1) * 128], outs[d][0][:, tt * 128:(tt + 1) * 128])
                            PT = ptr.next()
                            P.I("dve", "tensor_tensor", PT[:], sc[:, 0:128], masks[d], op=ALU.mult)
                            PTs.append(PT)
                        o = psO.next()
                        P.mm(o[:, 0:128], PTs[0][:], vtm[:, tt, :], start=True, stop=False)
                        for d in range(2):
                            for hf in range(2):
                                c = 2 * tt + hf
                                P.mm(o[hf * 64:(hf + 1) * 64, 0:128], outs[d][2][:, c * 64:(c + 1) * 64], Sb_[d][:, c, :],
                                     start=False, stop=False)
                        P.mm(o[:, 0:128], PTs[1][:], vtm[:, tt, :], start=False, stop=True)
                        junk = jr.next(); ss = ssr.next()
                        P.act(junk[:], o[:, 0:128], AF.Square, accum_out=ss[:])
                        rstd_from_ss(ss[:], 128)
                        on = onr.next()
                        P.I("dve", "scalar_tensor_tensor", on[:], o[:, 0:128], ss[:, 0:1], ngb[:], op0=ALU.mult, op1=ALU.mult)
                        og = ogr.next()
                        P.I("pool", "tensor_tensor", og[:], on[:], gs[:, tt, :], op=ALU.mult)
                        if pend_tr is not None:
                            hg_tr(*pend_tr)
                        pend_tr = (h, tt, og)
                    hg_tr(*pend_tr)

        for l in range(NL):
            for u in ("s", "p"):
                T = 2048 if u == "s" else 1024
                xin = xs if u == "s" else xp
                xout = ys if u == "s" else yp
                kind = l % 3
                j = l // 3
                KC = 16 if kind == 1 else 8
                xsrc = xin if l == 0 else xsc[u][l % 2]
                xdst = xout if l == NL - 1 else xsc[u][(l + 1) % 2]
                with P.scope():
                    oT = P.sb("oT", [128, KC, T], BF16)
                    wout = (da_w_out, ret_w_out, hg_w_out)[kind][j]
                    early_wo = (kind == 0) or (u == "p")
                    wo_pre = load_wo(KC, wout) if early_wo else None
                    with P.scope():
                        hT = P.sb("hT", [128, 8, T], BF16)

                        def pre(u=u, l=l, T=T, hT=hT, xsrc=xsrc):
                            phaseA(u, l, T, hT, xsrc)

                        if kind == 0:
                            phaseB_da(u, l, j, T, hT, oT, pre)
                        elif kind == 1:
                            phaseB_ret(u, l, j, T, hT, oT, pre)
                        else:
                            phaseB_hg(u, l, j, T, hT, oT, pre)
                    phaseC(u, l, T, oT, KC, wout, xsrc, xdst, wo=wo_pre)
    P.finish()
    return nc


_CACHE = {}


def kernel(x_prompt, x_sample, cache_k, cache_v, state_ret, state_hgrn, c, c_ctx,
           w_mod, b_mod, g_pre, g_post, da_w_in, da_w_out, da_lambda, da_subln,
           ret_w_in, ret_w_out, ret_decay, hg_w_in, hg_w_out, hg_lb, hg_norm, _NL=4):
    f = lambda a: np.ascontiguousarray(np.asarray(a, dtype=np.float32))
    x_prompt, x_sample, cache_k, cache_v, state_ret, state_hgrn, c, c_ctx = map(
        f, (x_prompt, x_sample, cache_k, cache_v, state_ret, state_hgrn, c, c_ctx))
    cstv, ropev = host_consts()
    shared = {
        "w_mod": f(w_mod), "b_mod": f(b_mod), "g_pre": f(g_pre), "g_post": f(g_post),
        "da_w_in": f(da_w_in), "da_w_out": f(da_w_out), "da_lambda": f(da_lambda).reshape(2, 256),
        "da_subln": f(da_subln), "ret_w_in": f(ret_w_in), "ret_w_out": f(ret_w_out),
        "ret_decay": f(ret_decay).reshape(1, 16), "hg_w_in": f(hg_w_in), "hg_w_out": f(hg_w_out),
        "hg_lb": f(hg_lb), "hg_norm": f(hg_norm), "cst": cstv, "rope": ropev,
    }
    in_maps = []
    for i in range(8):
        m = dict(shared)
        m["xs"] = x_sample[i]
        m["xp"] = x_prompt[4 * i:4 * i + 4].reshape(1024, D)
        m["ck"] = cache_k[i].reshape(2, 512, 1024)
        m["cv"] = cache_v[i].reshape(2, 512, 1024)
        m["sr"] = state_ret[i, 0]
        m["shg"] = state_hgrn[i, 0]
        m["cvec"] = np.ascontiguousarray(np.stack([c_ctx, c[i]], 0))
        in_maps.append(m)
    if _NL not in _CACHE:
        _CACHE[_NL] = build(_NL)
    nc = _CACHE[_NL]
    res = run_bass_kernel_spmd(nc, in_maps, core_ids=list(range(8)))
    R = res.results
    y_p = np.concatenate([r["yp"].reshape(4, 256, D) for r in R], 0)
    y_s = np.stack([r["ys"] for r in R], 0)
    n_k = np.concatenate([r["nk"].reshape(4, 2, 256, 16, 64) for r in R], 0)
    n_v = np.concatenate([r["nv"].reshape(4, 2, 256, 8, 128) for r in R], 0)
    n_r = np.concatenate([r["nr"].reshape(4, 1, 2, 8, 128, 256) for r in R], 0)
    n_h = np.concatenate([r["nh"].reshape(4, 1, 2, 8, 128, 128) for r in R], 0)
    return (y_p, y_s, n_k, n_v, n_r, n_h)
```

```python
import math
from contextlib import ExitStack, contextmanager

import numpy as np
import concourse.bass as bass
import concourse.mybir as mybir
from concourse.bass_utils import run_bass_kernel_spmd

F32 = mybir.dt.float32
BF16 = mybir.dt.bfloat16
AF = mybir.ActivationFunctionType
ALU = mybir.AluOpType
AX = mybir.AxisListType


class Buf:
    def __init__(self, name, t):
        self.name = name
        self.t = t
        self.lw = {}
        self.rd = {}
        self.kind = "sb"
        self.dsem = None
        self.dcnt = 0
        self.psum = False

    def __getitem__(self, i):
        return self.t[i]


class Prog:
    ENG = ("pe", "act", "dve", "pool", "sp")

    def __init__(self, nc):
        self.nc = nc
        self.q = {e: [] for e in self.ENG}
        self.esem = {}
        self.cnt = {}
        for e in ("pe", "act", "dve", "pool"):
            self.esem[e] = nc.alloc_semaphore(name="es_" + e)
            self.cnt[e] = 0
        self.seen = {e: {} for e in self.ENG}
        self.freed = {}
        self.dbufs = []
        self.allbufs = []
        self.scopes = []
        self.uid = 0
        self.dsems = {}
        self.free_dsems = {"sw": [], "hw": []}

    @contextmanager
    def scope(self):
        es = ExitStack()
        bufs = []
        self.scopes.append((es, bufs))
        try:
            yield
        finally:
            for b in bufs:
                if b.dsem is not None:
                    for qt, key in b.dsem.items():
                        self.free_dsems[qt].append(key)
                for t in list(b.lw.values()) + list(b.rd.values()):
                    k = t[0]
                    if k not in self.freed or self.freed[k][2] < t[2]:
                        self.freed[k] = t
            es.close()
            self.scopes.pop()

    def _mk(self, name, t):
        b = Buf(name, t)
        b.rd = dict(self.freed)
        self.scopes[-1][1].append(b)
        self.allbufs.append(b)
        return b

    def sb(self, name, shape, dt):
        self.uid += 1
        t = self.scopes[-1][0].enter_context(self.nc.sbuf_tensor("%s_%d" % (name, self.uid), list(shape), dt))
        return self._mk(name, t)

    def ps(self, name, shape, dt):
        self.uid += 1
        t = self.scopes[-1][0].enter_context(self.nc.psum_tensor("%s_%d" % (name, self.uid), list(shape), dt))
        b = self._mk(name, t)
        b.psum = True
        b.kind = "ps"
        return b

    def dram(self, name, shape, dt):
        t = self.nc.dram_tensor(name, list(shape), dt, kind="Internal").ap()
        b = Buf(name, t)
        b.kind = "dram"
        self.allbufs.append(b)
        return b

    def _need(self, eng, waits, tok, raw):
        if tok is None:
            return
        key, semh, val, teng = tok
        if teng == eng and not raw:
            return
        if self.seen[eng].get(key, 0) >= val:
            return
        if key not in waits or waits[key][1] < val:
            waits[key] = (semh, val)

    def _deps(self, eng, r, w, skipkey=None):
        waits = {}
        for b in r:
            for t in b.lw.values():
                self._need(eng, waits, t, True)
            if b.psum:
                for t in b.rd.values():
                    self._need(eng, waits, t, False)
        for b in w:
            for t in b.lw.values():
                if not (skipkey is not None and t[0] == skipkey):
                    self._need(eng, waits, t, False)
            for t in b.rd.values():
                self._need(eng, waits, t, False)
        for k, (s, v) in waits.items():
            self.seen[eng][k] = v
        return list(waits.values())

    def _reg(self, tok, r, w):
        for b in r:
            b.rd[tok[0]] = tok
        for b in w:
            b.lw[tok[0]] = tok
            b.rd = {}

    def op(self, eng, fn, r=(), w=()):
        waits = self._deps(eng, r, w)
        self.cnt[eng] += 1
        tok = (eng, self.esem[eng], self.cnt[eng], eng)
        self._reg(tok, r, w)
        self.q[eng].append((waits, fn, self.esem[eng], 1))

    def dma(self, q, out, in_, r=(), w=(), owner=None, nowaw=False, **kw):
        if owner is None:
            cand = [b for b in list(w) + list(r) if b.kind == "sb"]
            owner = cand[0] if cand else (w[0] if w else r[0])
        extra = None
        qt = "sw" if q == "pool" else "hw"
        if owner.dsem is None:
            owner.dsem = {}
        if qt not in owner.dsem:
            if self.free_dsems[qt]:
                key = self.free_dsems[qt].pop()
                extra = key
            else:
                key = ("d", len(self.dsems))
                self.dsems[key] = [self.nc.alloc_semaphore(name="ds_%d" % len(self.dsems)), 0]
            owner.dsem[qt] = key
        key = owner.dsem[qt]
        semh, tot = self.dsems[key]
        waits = self._deps(q, r, w, skipkey=key if nowaw else None)
        if extra is not None and tot > 0 and self.seen[q].get(key, 0) < tot:
            waits = [x for x in waits if x[0] is not semh] + [(semh, tot)]
            self.seen[q][key] = tot
        tot += 16
        self.dsems[key][1] = tot
        tok = (key, semh, tot, "dma")
        self._reg(tok, r, w)
        self.q[q].append((waits, (lambda e: e.dma_start(out=out, in_=in_, **kw)), semh, 16))

    def finish(self):
        nc = self.nc
        for key, (semh, tot) in self.dsems.items():
            if tot > 0:
                self.q["sp"].append(([(semh, tot)], None, None, 0))

        def run(e, items):
            for waits, fn, sem, inc in items:
                for s, v in waits:
                    e.wait_ge(s, v)
                if fn is not None:
                    fn(e).then_inc(sem, inc)

        with nc.Block() as blk:
            blk.tensor(lambda e: run(e, self.q["pe"]))
            blk.scalar(lambda e: run(e, self.q["act"]))
            blk.vector(lambda e: run(e, self.q["dve"]))
            blk.gpsimd(lambda e: run(e, self.q["pool"]))
            blk.sync(lambda e: run(e, self.q["sp"]))


class RingU:
    def __init__(self, *rings):
        self.b = [b for r in rings for b in r.b]
        self.i = 0

    def next(self):
        b = self.b[self.i % len(self.b)]
        self.i += 1
        return b


class Ring:
    def __init__(self, P, kind, name, n, shape, dt):
        mk = P.sb if kind == "sb" else P.ps
        self.b = [mk("%s%d" % (name, i), shape, dt) for i in range(n)]
        self.i = 0

    def next(self):
        b = self.b[self.i % len(self.b)]
        self.i += 1
        return b


class V:
    def __init__(self, b, ap):
        self.b = b
        self.ap = ap

    def __getitem__(self, i):
        return V(self.b, self.ap[i])

    def bc(self, shape):
        return V(self.b, self.ap.broadcast_to(list(shape)))

    def re(self, pat, **kw):
        return V(self.b, self.ap.rearrange(pat, **kw))

    def un(self, ax):
        return V(self.b, self.ap.unsqueeze(ax))

    def pbc(self, n=128):
        return V(self.b, self.ap.partition_broadcast(n))


def _bgi(self, i):
    return V(self, self.t[i])


Buf.__getitem__ = _bgi


def _u(x, lst):
    if isinstance(x, V):
        lst.append(x.b)
        return x.ap
    return x


def _I(P, eng, meth, out, *args, **kw):
    r, w = [], []
    o = _u(out, w)
    a = [_u(x, r) for x in args]
    k = {}
    for n, x in kw.items():
        k[n] = _u(x, w if n == "accum_out" else r)
    P.op(eng, (lambda e: getattr(e, meth)(o, *a, **k)), r=r, w=w)


def _act(P, out, in_, func, **kw):
    r, w = [], []
    o = _u(out, w)
    i = _u(in_, r)
    k = {}
    for n, x in kw.items():
        k[n] = _u(x, w if n == "accum_out" else r)
    P.op("act", (lambda e: e.activation(out=o, in_=i, func=func, **k)), r=r, w=w)


def _mm(P, out, lhsT, rhs, start=True, stop=True):
    r, w = [], []
    o = _u(out, w)
    a = _u(lhsT, r)
    b = _u(rhs, r)
    P.op("pe", (lambda e: e.matmul(o, a, b, start=start, stop=stop)), r=r, w=w)


def _tr(P, out, in_, ident):
    r, w = [], []
    o = _u(out, w)
    a = _u(in_, r)
    b = _u(ident, r)
    P.op("pe", (lambda e: e.transpose(out=o, in_=a, identity=b)), r=r, w=w)


def _dma(P, q, out, in_, **kw):
    r, w = [], []
    o = _u(out, w)
    i = _u(in_, r)
    P.dma(q, o, i, r=r, w=w, **kw)


Prog.I = _I
Prog.act = _act
Prog.mm = _mm
Prog.tr = _tr
Prog.d = _dma


D = 1024
EPS = 1e-6
BIG = 1.0e6
C_ID, C_SW, C_A, C_B, C_TL1, C_CTL, C_MF, C_MB = 0, 128, 256, 384, 512, 640, 768, 896
C_SDF, C_SDB = 1024, 1025
NCST = 1032


def host_consts():
    c = np.zeros((128, NCST), np.float32)
    i = np.arange(128)
    c[:, C_ID:C_ID + 128] = np.eye(128, dtype=np.float32)
    sw = np.zeros((128, 128), np.float32)
    sw[i, i ^ 1] = 1.0
    c[:, C_SW:C_SW + 128] = sw
    s = i[:, None].astype(np.float64)
    t = i[None, :].astype(np.float64)
    c[:, C_A:C_A + 128] = np.where(t >= s, t - s, BIG)
    c[:, C_B:C_B + 128] = np.where(s >= t, s - t, BIG)
    c[:, C_TL1:C_TL1 + 128] = np.broadcast_to(t + 1.0, (128, 128))
    c[:, C_CTL:C_CTL + 128] = np.broadcast_to(128.0 - t, (128, 128))
    same = (i[:, None] // 64) == (i[None, :] // 64)
    c[:, C_MF:C_MF + 128] = (same & (t >= s)).astype(np.float32)
    c[:, C_MB:C_MB + 128] = (same & (t <= s)).astype(np.float32)
    c[:, C_SDF] = 127.0 - i
    c[:, C_SDB] = i
    T = 2048
    tt = np.arange(T)
    row = (tt // 64).astype(np.float64)
    col = (tt % 64).astype(np.float64)
    inv = 10000.0 ** (-np.arange(16, dtype=np.float32) / 16).astype(np.float64)
    rope = np.zeros((128, 2, T), np.float32)
    for p in range(128):
        d = p % 64
        pr = d // 2
        ang = (row * inv[pr]) if pr < 16 else (col * inv[pr - 16])
        ang = ang.astype(np.float32).astype(np.float64)
        rope[p, 0] = np.cos(ang)
        rope[p, 1] = np.sin(ang) * (-1.0 if d % 2 == 0 else 1.0)
    return c, rope


def build(NL=4):
    nc = bass.Bass("TRN2", target_bir_lowering=False)
    P = Prog(nc)

    def din(n, s):
        return nc.dram_tensor(n, list(s), F32, kind="ExternalInput").ap()

    def dout(n, s):
        return nc.dram_tensor(n, list(s), F32, kind="ExternalOutput").ap()

    xs = din("xs", [2048, D]); xp = din("xp", [1024, D])
    ck = din("ck", [2, 512, 1024]); cv = din("cv", [2, 512, 1024])
    sr = din("sr", [2, 8, 128, 256]); shg = din("shg", [2, 8, 128, 128]); cvec = din("cvec", [2, D])
    w_mod = din("w_mod", [4, D, 3 * D]); b_mod = din("b_mod", [4, 3 * D])
    g_pre = din("g_pre", [4, D]); g_post = din("g_post", [4, D])
    da_w_in = din("da_w_in", [2, D, 4096]); da_w_out = din("da_w_out", [2, 1024, D])
    da_lambda = din("da_lambda", [2, 256]); da_subln = din("da_subln", [2, 128])
    ret_w_in = din("ret_w_in", [1, D, 6144]); ret_w_out = din("ret_w_out", [1, 2048, D]); ret_decay = din("ret_decay", [1, 16])
    hg_w_in = din("hg_w_in", [1, D, 5120]); hg_w_out = din("hg_w_out", [1, 1024, D])
    hg_lb = din("hg_lb", [2, 4, 1024]); hg_norm = din("hg_norm", [1, 128])
    cst = din("cst", [128, NCST]); rope = din("rope", [128, 2, 2048])
    yp = dout("yp", [1024, D]); ys = dout("ys", [2048, D])
    nk = dout("nk", [4, 2, 256, 1024]); nv = dout("nv", [4, 2, 256, 1024])
    nr = dout("nr", [4, 2, 8, 128, 256]); nh = dout("nh", [4, 2, 8, 128, 128])

    msc = P.dram("msc", [4, 2, 3 * D], F32)
    xsc = {"s": [P.dram("xsc_s%d" % i, [2048, D], F32) for i in range(2)],
           "p": [P.dram("xsc_p%d" % i, [1024, D], F32) for i in range(2)]}

    with P.scope():
        cf = P.sb("cf", [128, NCST], F32)
        P.d("sp", cf[:], cst)
        cb = P.sb("cb", [128, 1024], BF16)
        P.I("dve", "tensor_copy", cb[:], cf[:, 0:1024])
        identb = cb[:, C_ID:C_ID + 128]
        swb = cb[:, C_SW:C_SW + 128]

        def rstd_from_ss(ss, n):
            P.act(ss, ss, AF.Ln, scale=1.0 / n, bias=EPS)
            P.act(ss, ss, AF.Exp, scale=-0.5)

        with P.scope():
            cT = P.sb("cT", [128, 2, 8], F32)
            for v in range(2):
                P.d("sp", cT[:, v, :], cvec[v, :].rearrange("(c p) -> p c", p=128), allow_slow_non_contiguous=True, nowaw=True)
            scb = P.sb("scb", [128, 2, 8], BF16)
            P.act(scb[:], cT[:], AF.Silu)
            wr = Ring(P, "sb", "wm", 2, [128, 8, 512], BF16)
            psm = Ring(P, "ps", "psm", 2, [128, 512], F32)
            brow = P.sb("brow", [1, 3 * D], F32)
            mr = Ring(P, "sb", "mrow", 2, [1, 512], F32)
            for l in range(NL):
                P.d("sp", brow[:], b_mod[l:l + 1, :])
                wv = w_mod[l].rearrange("(c p) n -> p c n", p=128)
                for cbk in range(6):
                    w = wr.next()
                    P.d("pool", w[:], wv[:, :, cbk * 512:(cbk + 1) * 512])
                    for v in range(2):
                        ps = psm.next()
                        for kc in range(8):
                            P.mm(ps[0:1, :], scb[:, v, kc:kc + 1], w[:, kc, :], start=(kc == 0), stop=(kc == 7))
                        m = mr.next()
                        P.I("dve", "tensor_tensor", m[:], ps[0:1, :], brow[:, cbk * 512:(cbk + 1) * 512], op=ALU.add)
                        P.d("sp", msc[l, v:v + 1, cbk * 512:(cbk + 1) * 512], m[:], nowaw=True)

        def phaseA(u, l, T, hT, xsrc):
            v = 1 if u == "s" else 0
            with P.scope():
                shT = P.sb("shT", [128, 8], F32); scT = P.sb("scT", [128, 8], F32)
                gpT = P.sb("gpT", [128, 8], F32); AT = P.sb("AT", [128, 8], F32)
                P.d("sp", shT[:], msc[l, v, 0:1024].re("(c p) -> p c", p=128), allow_slow_non_contiguous=True)
                P.d("sp", scT[:], msc[l, v, 1024:2048].re("(c p) -> p c", p=128), allow_slow_non_contiguous=True)
                P.d("sp", gpT[:], g_pre[l, :].rearrange("(c p) -> p c", p=128), allow_slow_non_contiguous=True)
                P.I("dve", "scalar_tensor_tensor", AT[:], scT[:], 1.0, gpT[:], op0=ALU.add, op1=ALU.mult)
                xr = Ring(P, "sb", "xa", 3, [128, D], F32)
                xnr = Ring(P, "sb", "xn", 4, [128, D], BF16)
                jr = Ring(P, "sb", "jk", 2, [128, D], BF16)
                sr_ = Ring(P, "sb", "ssA", 4, [128, 1], F32)
                pst = Ring(P, "ps", "pstA", 2, [128, 1024], BF16)
                def a_stats(tt):
                    x = xr.next()
                    P.d("sp", x[:], xsrc[tt * 128:(tt + 1) * 128, :])
                    junk = jr.next(); ss = sr_.next()
                    P.act(junk[:], x[:], AF.Square, accum_out=ss[:])
                    rstd_from_ss(ss[:], D)
                    xn = xnr.next()
                    P.I("dve", "tensor_scalar", xn[:], x[:], ss[:, 0:1], None, op0=ALU.mult)
                    return xn

                def a_trev(tt, xn):
                    pt = pst.next()
                    for kc in range(8):
                        P.tr(pt[:, kc * 128:(kc + 1) * 128], xn[:, kc * 128:(kc + 1) * 128], identb)
                    for kc in range(8):
                        if kc % 2 == 0:
                            P.I("dve", "tensor_scalar", hT[:, kc, tt * 128:(tt + 1) * 128], pt[:, kc * 128:(kc + 1) * 128],
                                AT[:, kc:kc + 1], shT[:, kc:kc + 1], op0=ALU.mult, op1=ALU.add)
                        else:
                            P.act(hT[:, kc, tt * 128:(tt + 1) * 128], pt[:, kc * 128:(kc + 1) * 128], AF.Identity,
                                  scale=AT[:, kc:kc + 1], bias=shT[:, kc:kc + 1])

                pend_a = []
                for tt in range(T // 128):
                    pend_a.append((tt, a_stats(tt)))
                    if len(pend_a) > 2:
                        a_trev(*pend_a.pop(0))
                for it in pend_a:
                    a_trev(*it)

        def load_wo(KC, wout_ap):
            wo = P.sb("wo", [128, KC, D], BF16)
            wv = wout_ap.rearrange("(c p) n -> p c n", p=128)
            for c4 in range(0, KC, 4):
                P.d("pool", wo[:, c4:c4 + 4, :], wv[:, c4:c4 + 4, :], nowaw=True)
            return wo

        def phaseC(u, l, T, oT, KC, wout_ap, xsrc, xdst, wo=None):
            v = 1 if u == "s" else 0
            with P.scope():
                if wo is None:
                    wo = load_wo(KC, wout_ap)
                G2 = P.sb("G2", [128, D], F32)
                with P.scope():
                    gpo = P.sb("gpo", [128, D], F32)
                    P.d("sp", G2[:], msc[l, v:v + 1, 2048:3072].pbc(128))
                    P.d("sp", gpo[:], g_post[l:l + 1, :].partition_broadcast(128))
                    P.I("dve", "tensor_tensor", G2[:], G2[:], gpo[:], op=ALU.mult)
                psy = Ring(P, "ps", "psY", 2, [128, D], F32)
                xr = Ring(P, "sb", "xc", 2, [128, D], F32)
                tr_ = Ring(P, "sb", "tc", 2, [128, D], F32)
                jr = Ring(P, "sb", "jc", 2, [128, D], BF16)
                sr_ = Ring(P, "sb", "ssC", 4, [128, 1], F32)
                for tt in range(T // 128):
                    y = psy.next()
                    for hb in range(2):
                        for kc in range(KC):
                            P.mm(y[:, hb * 512:(hb + 1) * 512], oT[:, kc, tt * 128:(tt + 1) * 128],
                                 wo[:, kc, hb * 512:(hb + 1) * 512], start=(kc == 0), stop=(kc == KC - 1))
                    junk = jr.next(); ss = sr_.next()
                    P.act(junk[:], y[:], AF.Square, accum_out=ss[:])
                    rstd_from_ss(ss[:], D)
                    x = xr.next()
                    P.d("sp", x[:], xsrc[tt * 128:(tt + 1) * 128, :])
                    t = tr_.next()
                    P.I("dve", "scalar_tensor_tensor", t[:], y[:], ss[:, 0:1], G2[:], op0=ALU.mult, op1=ALU.mult)
                    P.I("pool", "tensor_tensor", x[:], x[:], t[:], op=ALU.add)
                    P.d("sp", xdst[tt * 128:(tt + 1) * 128, :], x[:], nowaw=True)

        def phaseB_da(u, l, j, T, hT, oT, pre):
            lam_init = 0.8 - 0.6 * math.exp(-0.3 * l)
            NT = T // 128
            QB = 256
            if u == "s":
                seqs = [(0, 2048, [(i * 128, i) for i in range(20)])]
                TK = 2560
            else:
                seqs = [(s * 256, 256, [((2 * s + i) * 128, 2 * s + i) for i in range(2)]) for s in range(4)]
                TK = 1024
            NKC = len(seqs[0][2])
            blocks = [(q0_ + qb * QB, kcs) for (q0_, nq, kcs) in seqs for qb in range(nq // QB)]
            win = da_w_in[j].rearrange("(c p) n -> p c n", p=128)
            with P.scope():
                wr = Ring(P, "sb", "wda", 2, [128, 8, 512], BF16)

                def load_panel(h):
                    w = wr.next()
                    for g in range(4):
                        P.d("pool", w[:, :, g * 128:(g + 1) * 128], win[:, :, g * 1024 + h * 128:g * 1024 + (h + 1) * 128], nowaw=True)
                    return w

                wnext = load_panel(0)
                pre()
                lv = P.sb("lv", [128, 256], F32)
                P.d("sp", lv[:], da_lambda[j:j + 1, :].partition_broadcast(128))
                lv4 = lv[:].re("p (a b d) -> p a b d", a=2, b=2)
                pr = P.sb("lpr", [128, 2, 64], F32)
                P.I("dve", "tensor_tensor", pr[:], lv4[:, :, 0, :], lv4[:, :, 1, :], op=ALU.mult)
                s2 = P.sb("ls2", [128, 2], F32)
                P.I("dve", "tensor_reduce", s2[:], pr[:], axis=AX.X, op=ALU.add)
                P.act(s2[:], s2[:], AF.Exp)
                nlam = P.sb("nlam", [128, 1], F32)
                P.I("dve", "tensor_tensor", nlam[:], s2[:, 1:2], s2[:, 0:1], op=ALU.subtract)
                P.I("dve", "tensor_scalar", nlam[:], nlam[:], -lam_init, None, op0=ALU.add)
                sgT = P.sb("sgT", [128, 1], F32)
                P.d("sp", sgT[:], da_subln[j, :].rearrange("(p o) -> p o", o=1), allow_slow_non_contiguous=True)
                P.I("dve", "tensor_scalar", sgT[:], sgT[:], 1.0 - lam_init, None, op0=ALU.mult)
                onesb = P.sb("onesb", [128, 128], BF16)
                P.I("dve", "memset", onesb[:], 1.0)
                if u == "s":
                    ropeb = P.sb("ropeb", [128, 2, 2048], BF16)
                    P.d("pool", ropeb[:], rope)
                qz = [P.sb("qz%d" % i, [128, T], BF16) for i in range(2)]
                P.I("dve", "memset", qz[0][64:128, :], 0.0)
                P.I("dve", "memset", qz[1][0:64, :], 0.0)
                kf = P.sb("kf", [128, TK], BF16)
                gT = P.sb("gT", [128, T], BF16)
                vsb = P.sb("vsb", [128, TK // 128, 128], BF16)
                ptr = Ring(P, "sb", "pT", 2, [128, NKC, 2 * QB], BF16)
                q0r = Ring(P, "sb", "q0", 2, [128, 512], BF16)
                t1r = Ring(P, "sb", "t1", 2, [128, 512], F32)
                t2r = Ring(P, "sb", "t2", 2, [128, 512], F32)
                ckt = P.sb("ckt", [128, 4, 128], BF16)
                kst = Ring(P, "sb", "kst", 2, [128, 256], F32)
                rvr = Ring(P, "sb", "rinv", 2, [128, 2 * QB], F32)
                o1r = Ring(P, "sb", "o1", 2, [128, QB], F32)
                o2r = Ring(P, "sb", "o2", 2, [128, QB], F32)
                sqr = Ring(P, "sb", "sqd", 2, [128, QB], BF16)
                rsr = Ring(P, "sb", "rsd", 2, [128, QB], F32)
                psA = Ring(P, "ps", "psA", 2, [128, 512], F32)
                psS = Ring(P, "ps", "psS", 2, [128, 512], F32)
                psO = Ring(P, "ps", "psO", 4, [128, 512], F32)
                psP = RingU(psA, psS)

                def epi1(accO, accS):
                    rinv = rvr.next()
                    if u == "p":
                        P.act(rinv[:], accS[:], AF.Ln)
                        P.act(rinv[:], rinv[:], AF.Exp, scale=-1.0)
                    else:
                        P.I("dve", "reciprocal", rinv[:], accS[:])
                    o1 = o1r.next(); o2 = o2r.next()
                    P.I("dve", "tensor_tensor", o1[:], accO[:, 0:QB], rinv[:, 0:QB], op=ALU.mult)
                    P.I("dve", "scalar_tensor_tensor", o2[:], accO[:, QB:2 * QB], nlam[:, 0:1], rinv[:, QB:2 * QB], op0=ALU.mult, op1=ALU.mult)
                    P.I("pool", "tensor_tensor", o1[:], o1[:], o2[:], op=ALU.add)
                    return [o1, o2, None]

                def epi1b(st):
                    sq = sqr.next()
                    P.act(sq[:], st[0][:], AF.Square)
                    st[2] = sq

                def epi2(h, qlo, accS, st):
                    if st[2] is None:
                        epi1b(st)
                    o1, o2, sq = st
                    P.mm(accS[:, 0:QB], onesb[:], sq[:])
                    rs = rsr.next()
                    P.act(rs[:], accS[:, 0:QB], AF.Ln, scale=1.0 / 128, bias=EPS)
                    P.act(rs[:], rs[:], AF.Exp, scale=-0.5)
                    P.I("dve", "scalar_tensor_tensor", o2[:], o1[:], sgT[:, 0:1], rs[:], op0=ALU.mult, op1=ALU.mult)
                    P.I("pool", "tensor_tensor", oT[:, h, qlo:qlo + QB], o2[:], gT[:, qlo:qlo + QB], op=ALU.mult)

                for h in range(8):
                    W = wnext
                    if h + 1 < 8:
                        wnext = load_panel(h + 1)
                    for which, dst, c0 in (("q", None, 0), ("k", kf, 128), ("g", gT, 384)):
                        for blk in range(T // 512):
                            ps = psP.next()
                            for kc in range(8):
                                P.mm(ps[:], W[:, kc, c0:c0 + 128], hT[:, kc, blk * 512:(blk + 1) * 512], start=(kc == 0), stop=(kc == 7))
                            bs = slice(blk * 512, (blk + 1) * 512)
                            dsl = dst[:, bs] if dst is not None else None
                            if which == "g":
                                P.act(dsl, ps[:], AF.Silu)
                                continue
                            scl = 0.125 if which == "q" else 1.0
                            if u == "s":
                                q0 = q0r.next()
                                P.I("dve", "tensor_scalar", q0[:], ps[:], scl, None, op0=ALU.mult)
                                sw = psP.next()
                                P.mm(sw[:], swb, q0[:])
                                t1 = t1r.next(); t2 = t2r.next()
                                P.I("pool", "tensor_tensor", t1[:], q0[:], ropeb[:, 0, blk * 512:(blk + 1) * 512], op=ALU.mult)
                                P.I("dve", "tensor_tensor", t2[:], sw[:], ropeb[:, 1, blk * 512:(blk + 1) * 512], op=ALU.mult)
                                if which == "q":
                                    P.I("pool", "tensor_tensor", qz[0][0:64, bs], t1[0:64, :], t2[0:64, :], op=ALU.add)
                                    P.I("pool", "tensor_tensor", qz[1][64:128, bs], t1[64:128, :], t2[64:128, :], op=ALU.add)
                                else:
                                    P.I("pool", "tensor_tensor", dsl, t1[:], t2[:], op=ALU.add)
                            elif which == "q":
                                P.I("dve", "tensor_scalar", qz[0][0:64, bs], ps[0:64, :], scl, None, op0=ALU.mult)
                                P.I("dve", "tensor_scalar", qz[1][64:128, bs], ps[64:128, :], scl, None, op0=ALU.mult)
                            else:
                                P.I("dve", "tensor_scalar", dsl, ps[:], scl, None, op0=ALU.mult)
                    if u == "s":
                        P.d("pool", ckt[:], ck[j].rearrange("(c p) n -> p c n", p=128)[:, :, h * 128:(h + 1) * 128])
                        P.d("pool", vsb[:, 16:20, :], cv[j].rearrange("(c p) n -> p c n", p=128)[:, :, h * 128:(h + 1) * 128])
                        pc = psP.next()
                        for c in range(4):
                            P.mm(pc[:, c * 128:(c + 1) * 128], ckt[:, c, :], identb)
                        P.I("dve", "tensor_copy", kf[:, 2048:2560], pc[:])
                    for tt in range(NT):
                        ps = psP.next()
                        c0 = 128 if u == "p" else 256
                        ncol = 384 - c0
                        for kc in range(8):
                            P.mm(ps[:, 0:ncol], hT[:, kc, tt * 128:(tt + 1) * 128], W[:, kc, c0:384], start=(kc == 0), stop=(kc == 7))
                        P.act(vsb[:, tt, :], ps[:, ncol - 128:ncol], AF.Copy)
                        if u == "p":
                            st = kst.next()
                            P.I("dve", "tensor_copy", st[:], ps[:, 0:256])
                            sq_, tl = tt // 2, tt % 2
                            P.d("sp", nk[sq_, j, tl * 128:(tl + 1) * 128, h * 128:(h + 1) * 128], st[:, 0:128])
                            P.d("sp", nv[sq_, j, tl * 128:(tl + 1) * 128, h * 128:(h + 1) * 128], st[:, 128:256])
                    prev = None
                    pend = None
                    for blk in blocks + [None]:
                        if blk is not None:
                            pT = ptr.next()
                        if prev is not None:
                            accO = psO.next(); accS = psO.next()
                        for i in range(NKC):
                            if blk is not None:
                                qlo, kcs = blk
                                koff, vi = kcs[i]
                                sc = psS.next()
                                for s in range(2):
                                    P.mm(sc[:, s * QB:(s + 1) * QB], kf[:, koff:koff + 128], qz[s][:, qlo:qlo + QB])
                                P.act(pT[:, i, :], sc[:], AF.Exp)
                            if prev is not None:
                                pqlo, pkcs, ppT = prev
                                P.mm(accO[:], vsb[:, pkcs[i][1], :], ppT[:, i, :], start=(i == 0), stop=(i == NKC - 1))
                                P.mm(accS[:], onesb[:], ppT[:, i, :], start=(i == 0), stop=(i == NKC - 1))
                            if pend is not None and i == min(NKC - 1, 6) and pend[3][2] is None:
                                epi1b(pend[3])
                            if pend is not None and i == min(NKC - 1, 12):
                                epi2(*pend)
                                pend = None
                        if pend is not None:
                            epi2(*pend)
                            pend = None
                        if prev is not None:
                            pend = (h, prev[0], accS, epi1(accO, accS))
                        prev = (blk[0], blk[1], pT) if blk is not None else None
                    if pend is not None:
                        epi2(*pend)

        def phaseB_ret(u, l, j, T, hT, oT, pre):
            NT = T // 128
            SC = 128.0 ** -0.5
            if u == "s":
                seqs = [list(range(16))]
            else:
                seqs = [[2 * s, 2 * s + 1] for s in range(4)]
            win = ret_w_in[j].rearrange("(c p) n -> p c n", p=128)
            with P.scope():
                wr = Ring(P, "sb", "wrt", 2, [128, 8, 768], BF16)

                def load_panel(h):
                    w = wr.next()
                    P.d("pool", w[:, :, 0:128], win[:, :, h * 128:(h + 1) * 128], nowaw=True)
                    P.d("pool", w[:, :, 128:256], win[:, :, 1024 + h * 128:1024 + (h + 1) * 128], nowaw=True)
                    P.d("pool", w[:, :, 256:512], win[:, :, 2048 + h * 256:2048 + (h + 1) * 256], nowaw=True)
                    P.d("pool", w[:, :, 512:768], win[:, :, 4096 + h * 256:4096 + (h + 1) * 256], nowaw=True)
                    return w

                wnext = load_panel(0)
                pre()
                lg = P.sb("lg", [128, 16], F32)
                P.d("sp", lg[:], ret_decay[j:j + 1, :].partition_broadcast(128))
                P.act(lg[:], lg[:], AF.Exp)
                P.act(lg[:], lg[:], AF.Ln, scale=-1.0, bias=1.0)
                qT = P.sb("qT", [128, T], BF16); kT = P.sb("kT", [128, T], BF16)
                qfT = P.sb("qfT", [128, T], BF16); qbT = P.sb("qbT", [128, T], BF16)
                kmf = P.sb("kmf", [128, NT, 128], BF16); kmb = P.sb("kmb", [128, NT, 128], BF16)
                vtm = P.sb("vtm", [128, NT, 256], BF16); gs = P.sb("gsr", [128, NT, 256], BF16)
                Sfb = P.sb("Sfb", [128, NT, 256], BF16); Sbb = P.sb("Sbb", [128, NT, 256], BF16)
                S32 = Ring(P, "sb", "S32", 6 if u == "s" else 24, [128, 256], F32)
                m1 = P.sb("m1", [128, 128], F32); m2 = P.sb("m2", [128, 128], F32); Mh = P.sb("Mh", [128, 128], F32)
                qdf = P.sb("qdf", [128, 128], BF16); qdb = P.sb("qdb", [128, 128], BF16)
                kd = P.sb("kd", [128, 4], F32)
                ptr = Ring(P, "sb", "PTs", 3, [128, 128], BF16)
                ogr = Ring(P, "sb", "ogr", 4, [128, 256], BF16)
                jr = Ring(P, "sb", "jrr", 3, [128, 256], BF16)
                ssr = Ring(P, "sb", "ssr", 6, [128, 1], F32)
                psA = Ring(P, "ps", "psA", 1, [128, 512], F32)
                psS = Ring(P, "ps", "psS", 2, [128, 512], F32)
                psO = Ring(P, "ps", "psO", 3, [128, 512], F32)
                psT = Ring(P, "ps", "psT", 2, [128, 1024], BF16)
                psP = RingU(psA, psO)
                for h in range(8):
                    W = wnext
                    if h + 1 < 8:
                        wnext = load_panel(h + 1)
                    lgf = lg[:, h:h + 1]; lgb = lg[:, 8 + h:9 + h]
                    P.act(m1[:], cf[:, C_A:C_A + 128], AF.Exp, scale=lgf)
                    P.act(m2[:], cf[:, C_B:C_B + 128], AF.Exp, scale=lgb)
                    P.I("dve", "tensor_tensor", m1[:], m1[:], m2[:], op=ALU.add)
                    P.I("dve", "tensor_scalar", Mh[:], m1[:], SC, None, op0=ALU.mult)
                    P.act(qdf[:], cf[:, C_TL1:C_TL1 + 128], AF.Exp, scale=lgf)
                    P.act(qdb[:], cf[:, C_CTL:C_CTL + 128], AF.Exp, scale=lgb)
                    P.act(kd[:, 0:1], cf[:, C_SDF:C_SDF + 1], AF.Exp, scale=lgf)
                    P.act(kd[:, 1:2], cf[:, C_SDB:C_SDB + 1], AF.Exp, scale=lgb)
                    P.I("dve", "tensor_scalar", kd[:, 0:2], kd[:, 0:2], SC, None, op0=ALU.mult)
                    P.act(kd[:, 2:3], lgf, AF.Exp, scale=128.0)
                    P.act(kd[:, 3:4], lgb, AF.Exp, scale=128.0)
                    for dst, c0 in ((qT, 0), (kT, 128)):
                        for blk in range(T // 512):
                            ps = psP.next()
                            for kc in range(8):
                                P.mm(ps[:], W[:, kc, c0:c0 + 128], hT[:, kc, blk * 512:(blk + 1) * 512], start=(kc == 0), stop=(kc == 7))
                            P.act(dst[:, blk * 512:(blk + 1) * 512], ps[:], AF.Copy)
                    P.I("dve", "tensor_tensor", qfT[:].re("p (n t) -> p n t", t=128), qT[:].re("p (n t) -> p n t", t=128),
                        qdf[:].un(1).bc([128, NT, 128]), op=ALU.mult)
                    P.I("dve", "tensor_tensor", qbT[:].re("p (n t) -> p n t", t=128), qT[:].re("p (n t) -> p n t", t=128),
                        qdb[:].un(1).bc([128, NT, 128]), op=ALU.mult)
                    for tt in range(NT):
                        ps = psP.next()
                        for kc in range(8):
                            P.mm(ps[:, 0:384], hT[:, kc, tt * 128:(tt + 1) * 128], W[:, kc, 128:512], start=(kc == 0), stop=(kc == 7))
                        P.I("dve", "tensor_scalar", kmf[:, tt, :], ps[:, 0:128], kd[:, 0:1], None, op0=ALU.mult)
                        P.I("dve", "tensor_scalar", kmb[:, tt, :], ps[:, 0:128], kd[:, 1:2], None, op0=ALU.mult)
                        P.act(vtm[:, tt, :], ps[:, 128:384], AF.Copy)
                        ps2 = psP.next()
                        for kc in range(8):
                            P.mm(ps2[:, 0:256], hT[:, kc, tt * 128:(tt + 1) * 128], W[:, kc, 512:768], start=(kc == 0), stop=(kc == 7))
                        P.act(gs[:, tt, :], ps2[:, 0:256], AF.Silu)
                    chains = []
                    for si, tiles in enumerate(seqs):
                        for d, (order, km, Sb_, cdec) in enumerate(((tiles, kmf, Sfb, kd[:, 2:3]), (tiles[::-1], kmb, Sbb, kd[:, 3:4]))):
                            S = S32.next()
                            if u == "s":
                                P.d("sp", S[:], sr[d, h])
                            else:
                                P.I("pool", "memset", S[:], 0.0)
                            chains.append([si, d, order, km, Sb_, cdec, S])
                    for k in range(len(chains[0][2])):
                        for ch in chains:
                            si, d, order, km, Sb_, cdec, S = ch
                            tt = order[k]
                            P.act(Sb_[:, tt, :], S[:], AF.Copy)
                            U = psS.next()
                            P.mm(U[:, 0:256], km[:, tt, :], vtm[:, tt, :])
                            Sn = S32.next()
                            P.I("dve", "scalar_tensor_tensor", Sn[:], S[:], cdec, U[:, 0:256], op0=ALU.mult, op1=ALU.add)
                            ch[6] = Sn
                    if u == "p":
                        for si, d, order, km, Sb_, cdec, S in chains:
                            P.d("sp", nr[si, d, h], S[:])
                    def stA(tt):
                        sc = psS.next()
                        P.mm(sc[:, 0:128], kT[:, tt * 128:(tt + 1) * 128], qT[:, tt * 128:(tt + 1) * 128])
                        PT = ptr.next()
                        P.I("dve", "tensor_tensor", PT[:], sc[:, 0:128], Mh[:], op=ALU.mult)
                        return PT

                    def stB(tt, PT):
                        o = psO.next()
                        P.mm(o[:, 0:256], PT[:], vtm[:, tt, :], start=True, stop=False)
                        P.mm(o[:, 0:256], qfT[:, tt * 128:(tt + 1) * 128], Sfb[:, tt, :], start=False, stop=False)
                        P.mm(o[:, 0:256], qbT[:, tt * 128:(tt + 1) * 128], Sbb[:, tt, :], start=False, stop=True)
                        junk = jr.next(); ss = ssr.next()
                        P.act(junk[:], o[:, 0:256], AF.Square, accum_out=ss[:])
                        rstd_from_ss(ss[:], 256)
                        return (o, ss)

                    def stC(tt, o, ss):
                        og = ogr.next()
                        P.I("dve", "scalar_tensor_tensor", og[:], o[:, 0:256], ss[:, 0:1], gs[:, tt, :], op0=ALU.mult, op1=ALU.mult)
                        return og

                    def stD(tt, og):
                        pt = psT.next()
                        for e in range(2):
                            P.tr(pt[:, e * 128:(e + 1) * 128], og[:, e * 128:(e + 1) * 128], identb)
                        P.act(oT[:, 2 * h:2 * h + 2, tt * 128:(tt + 1) * 128], pt[:, 0:256].re("p (e t) -> p e t", e=2), AF.Copy)

                    qa, qb_, qc = {}, {}, {}
                    for i in range(NT + 3):
                        if i < NT:
                            qa[i] = stA(i)
                        if 0 <= i - 1 < NT:
                            qb_[i - 1] = stB(i - 1, qa.pop(i - 1))
                        if 0 <= i - 2 < NT:
                            qc[i - 2] = stC(i - 2, *qb_.pop(i - 2))
                        if 0 <= i - 3 < NT:
                            stD(i - 3, qc.pop(i - 3))

        def phaseB_hg(u, l, j, T, hT, oT, pre):
            NT = T // 128
            NCH = T // 64
            GB = 1024
            if u == "s":
                seqs = [list(range(32))]
            else:
                seqs = [list(range(4 * s, 4 * s + 4)) for s in range(4)]
            win = hg_w_in[j].rearrange("(c p) n -> p c n", p=128)
            with P.scope():
                wr = Ring(P, "sb", "whg", 2, [128, 8, 640], BF16)

                def load_panel(h):
                    w = wr.next()
                    for g in range(5):
                        P.d("pool", w[:, :, g * 128:(g + 1) * 128], win[:, :, g * 1024 + h * 128:g * 1024 + (h + 1) * 128], nowaw=True)
                    return w

                wnext = load_panel(0)
                pre()
                L = P.sb("hgL", [128, 2, 4, 8], F32)
                for d in range(2):
                    for e in range(4):
                        P.d("sp", L[:, d, e, :], hg_lb[d, e, :].rearrange("(h p) -> p h", p=128), allow_slow_non_contiguous=True, nowaw=True)
                P.act(L[:], L[:], AF.Exp)
                den = P.sb("hgden", [128, 2, 8], F32); lb = P.sb("hglb", [128, 2, 8], F32); omlb = P.sb("hgomlb", [128, 2, 8], F32)
                P.I("dve", "tensor_tensor", den[:], L[:, :, 0, :], L[:, :, 1, :], op=ALU.add)
                P.I("dve", "tensor_tensor", den[:], den[:], L[:, :, 2, :], op=ALU.add)
                P.I("dve", "tensor_tensor", den[:], den[:], L[:, :, 3, :], op=ALU.add)
                P.I("dve", "reciprocal", den[:], den[:])
                if l == 0:
                    P.I("dve", "memset", lb[:], 0.0)
                else:
                    P.I("dve", "tensor_copy", lb[:], L[:, :, 1, :])
                    for e in range(2, l + 1):
                        P.I("dve", "tensor_tensor", lb[:], lb[:], L[:, :, e, :], op=ALU.add)
                    P.I("dve", "tensor_tensor", lb[:], lb[:], den[:], op=ALU.mult)
                P.I("dve", "tensor_scalar", omlb[:], lb[:], -1.0, 1.0, op0=ALU.mult, op1=ALU.add)
                ngb = P.sb("ngb", [128, 128], F32)
                P.d("sp", ngb[:], hg_norm[j:j + 1, :].partition_broadcast(128))
                ones = P.sb("hgones", [128, GB], F32)
                P.I("dve", "memset", ones[:], 1.0)
                qT = P.sb("hqT", [128, T], BF16)
                outs = [[P.sb("hz%d%d" % (d, k), [128, T], BF16) for k in range(4)] for d in range(2)]
                khtm = [P.sb("khtm%d" % d, [128, NT, 128], BF16) for d in range(2)]
                dec = [P.sb("hdec%d" % d, [128, NCH], F32) for d in range(2)]
                vtm = P.sb("hvtm", [128, NT, 128], BF16); gs = P.sb("hgs", [128, NT, 128], BF16)
                Sb_ = [P.sb("hS%d" % d, [128, NCH, 128], BF16) for d in range(2)]
                S32 = Ring(P, "sb", "hS32", 6 if u == "s" else 24, [128, 128], F32)
                Gr = Ring(P, "sb", "hG", 2, [128, GB], F32); Ir = Ring(P, "sb", "hI", 2, [128, GB], F32); Ar = Ring(P, "sb", "hA", 2, [128, GB], F32)
                kkr = Ring(P, "sb", "hk", 2, [128, GB], BF16); exr = Ring(P, "sb", "hex", 4, [128, GB], BF16)
                ptr = Ring(P, "sb", "hPT", 6, [128, 128], BF16)
                onr = Ring(P, "sb", "hon", 3, [128, 128], F32)
                ogr = Ring(P, "sb", "hog", 4, [128, 128], BF16)
                jr = Ring(P, "sb", "hjr", 3, [128, 128], BF16)
                ssr = Ring(P, "sb", "hss", 6, [128, 1], F32)
                psA = Ring(P, "ps", "psA", 1, [128, 512], F32)
                psS = Ring(P, "ps", "psS", 2, [128, 512], F32)
                psO = Ring(P, "ps", "psO", 3, [128, 512], F32)
                psT = Ring(P, "ps", "psT", 2, [128, 1024], BF16)
                psP = RingU(psA, psO)
                masks = (cb[:, C_MF:C_MF + 128], cb[:, C_MB:C_MB + 128])
                for h in range(8):
                    W = wnext
                    if h + 1 < 8:
                        wnext = load_panel(h + 1)
                    for blk in range(T // 512):
                        ps = psP.next()
                        for kc in range(8):
                            P.mm(ps[:], W[:, kc, 0:128], hT[:, kc, blk * 512:(blk + 1) * 512], start=(kc == 0), stop=(kc == 7))
                        P.act(qT[:, blk * 512:(blk + 1) * 512], ps[:], AF.Silu)
                    nch = GB // 64
                    shp = [128, nch, 64]

                    def g_stage1(d, gb):
                        t0 = gb * GB
                        G = Gr.next(); I_ = Ir.next(); kk = kkr.next()
                        for b2 in range(GB // 512):
                            ps = psP.next()
                            for kc in range(8):
                                P.mm(ps[:], W[:, kc, 128 * (1 + d):128 * (2 + d)], hT[:, kc, t0 + b2 * 512:t0 + (b2 + 1) * 512],
                                     start=(kc == 0), stop=(kc == 7))
                            P.act(G[:, b2 * 512:(b2 + 1) * 512], ps[:], AF.Sigmoid)
                        P.I("dve", "tensor_scalar", G[:], G[:], omlb[:, d, h:h + 1], lb[:, d, h:h + 1], op0=ALU.mult, op1=ALU.add)
                        P.I("dve", "tensor_scalar", kk[:], G[:], -1.0, 1.0, op0=ALU.mult, op1=ALU.add)
                        P.act(G[:], G[:], AF.Ln)
                        P.I("dve", "tensor_tensor_scan", I_[:], ones[:], G[:], 0.0, op0=ALU.mult, op1=ALU.add)
                        P.I("dve", "tensor_tensor", G[:], I_[:], G[:], op=ALU.subtract)
                        return (d, gb, G, I_, kk)

                    def g_stage2(ctx):
                        d, gb, G, I_, kk = ctx
                        t0 = gb * GB
                        I3 = I_[:].re("p (c t) -> p c t", t=64); E3 = G[:].re("p (c t) -> p c t", t=64)
                        Z3 = I3 if d == 0 else E3
                        qs_ = qT[:, t0:t0 + GB]
                        o_qt, o_kt, o_qh, o_kh = [x[:, t0:t0 + GB] for x in outs[d]]
                        sq = 1.0 if d == 0 else -1.0
                        r1 = I3[:, :, 32:33] if d == 0 else E3[:, :, 31:32]
                        r3 = E3[:, :, 0:1] if d == 0 else I3[:, :, 63:64]
                        r4 = I3[:, :, 63:64] if d == 0 else E3[:, :, 0:1]
                        for ref, sgn, src, dst in ((r1, sq, qs_, o_qt), (r1, -sq, kk[:], o_kt), (r3, sq, qs_, o_qh), (r4, -sq, kk[:], o_kh)):
                            if sgn == sq or ref is not r1:
                                A_ = Ar.next(); A3 = A_[:].re("p (c t) -> p c t", t=64)
                                P.I("dve", "tensor_tensor", A3, Z3, ref.bc(shp), op=ALU.subtract)
                            ex = exr.next()
                            P.act(ex[:], A_[:], AF.Exp, scale=sgn)
                            P.I("dve", "tensor_tensor", dst, src, ex[:], op=ALU.mult)
                        c0 = gb * nch
                        P.I("dve", "tensor_tensor", dec[d][:, c0:c0 + nch], I3[:, :, 63], E3[:, :, 0], op=ALU.subtract)
                        P.act(dec[d][:, c0:c0 + nch], dec[d][:, c0:c0 + nch], AF.Exp)

                    pctx = None
                    for d in range(2):
                        for gb in range(T // GB):
                            ctx = g_stage1(d, gb)
                            if pctx is not None:
                                g_stage2(pctx)
                            pctx = ctx
                    g_stage2(pctx)
                    for d in range(2):
                        for t4 in range(0, NT, 4):
                            pt = psT.next()
                            for i in range(4):
                                P.tr(pt[:, i * 128:(i + 1) * 128], outs[d][3][:, (t4 + i) * 128:(t4 + i + 1) * 128], identb)
                            P.act(khtm[d][:, t4:t4 + 4, :], pt[:, 0:512].re("p (a b) -> p a b", a=4), AF.Copy)
                    for tt in range(NT):
                        ps = psP.next()
                        for kc in range(8):
                            P.mm(ps[:, 0:256], hT[:, kc, tt * 128:(tt + 1) * 128], W[:, kc, 384:640], start=(kc == 0), stop=(kc == 7))
                        P.act(vtm[:, tt, :], ps[:, 0:128], AF.Copy)
                        P.act(gs[:, tt, :], ps[:, 128:256], AF.Silu)
                    chains = []
                    for si, chunks in enumerate(seqs):
                        for d in range(2):
                            S = S32.next()
                            if u == "s":
                                P.d("sp", S[:], shg[d, h])
                            else:
                                P.I("pool", "memset", S[:], 0.0)
                            chains.append([si, d, chunks if d == 0 else chunks[::-1], S])
                    for k in range(len(chains[0][2])):
                        for ch in chains:
                            si, d, order, S = ch
                            c = order[k]
                            P.act(Sb_[d][:, c, :], S[:], AF.Copy)
                            tt, hf = c // 2, (c % 2) * 64
                            U = psS.next()
                            P.mm(U[:, 0:128], khtm[d][hf:hf + 64, tt, :], vtm[hf:hf + 64, tt, :])
                            Sn = S32.next()
                            P.I("dve", "scalar_tensor_tensor", Sn[:], S[:], dec[d][:, c:c + 1], U[:, 0:128], op0=ALU.mult, op1=ALU.add)
                            ch[3] = Sn
                    if u == "p":
                        for si, d, order, S in chains:
                            P.d("sp", nh[si, d, h], S[:])
                    def stA(tt):
                        PTs = []
                        for d in range(2):
                            sc = psS.next()
                            P.mm(sc[:, 0:128], outs[d][1][:, tt * 128:(tt + 1) * 128], outs[d][0][:, tt * 128:(tt + 1) * 128])
                            PT = ptr.next()
                            P.I("dve", "tensor_tensor", PT[:], sc[:, 0:128], masks[d], op=ALU.mult)
                            PTs.append(PT)
                        return PTs

                    def stB(tt, PTs):
                        o = psO.next()
                        P.mm(o[:, 0:128], PTs[0][:], vtm[:, tt, :], start=True, stop=False)
                        for d in range(2):
                            for hf in range(2):
                                c = 2 * tt + hf
                                P.mm(o[hf * 64:(hf + 1) * 64, 0:128], outs[d][2][:, c * 64:(c + 1) * 64], Sb_[d][:, c, :],
                                     start=False, stop=False)
                        P.mm(o[:, 0:128], PTs[1][:], vtm[:, tt, :], start=False, stop=True)
                        junk = jr.next(); ss = ssr.next()
                        P.act(junk[:], o[:, 0:128], AF.Square, accum_out=ss[:])
                        rstd_from_ss(ss[:], 128)
                        return (o, ss)

                    def stC(tt, o, ss):
                        on = onr.next()
                        P.I("dve", "scalar_tensor_tensor", on[:], o[:, 0:128], ss[:, 0:1], ngb[:], op0=ALU.mult, op1=ALU.mult)
                        og = ogr.next()
                        P.I("pool", "tensor_tensor", og[:], on[:], gs[:, tt, :], op=ALU.mult)
                        return og

                    def stD(tt, og):
                        pt = psT.next()
                        P.tr(pt[:, 0:128], og[:], identb)
                        P.act(oT[:, h, tt * 128:(tt + 1) * 128], pt[:, 0:128], AF.Copy)

                    qa, qb_, qc = {}, {}, {}
                    for i in range(NT + 3):
                        if i < NT:
                            qa[i] = stA(i)
                        if 0 <= i - 1 < NT:
                            qb_[i - 1] = stB(i - 1, qa.pop(i - 1))
                        if 0 <= i - 2 < NT:
                            qc[i - 2] = stC(i - 2, *qb_.pop(i - 2))
                        if 0 <= i - 3 < NT:
                            stD(i - 3, qc.pop(i - 3))

        for l in range(NL):
            for u in ("s", "p"):
                T = 2048 if u == "s" else 1024
                xin = xs if u == "s" else xp
                xout = ys if u == "s" else yp
                kind = l % 3
                j = l // 3
                KC = 16 if kind == 1 else 8
                xsrc = xin if l == 0 else xsc[u][l % 2]
                xdst = xout if l == NL - 1 else xsc[u][(l + 1) % 2]
                with P.scope():
                    oT = P.sb("oT", [128, KC, T], BF16)
                    wout = (da_w_out, ret_w_out, hg_w_out)[kind][j]
                    early_wo = (kind == 0) or (u == "p")
                    wo_pre = load_wo(KC, wout) if early_wo else None
                    with P.scope():
                        hT = P.sb("hT", [128, 8, T], BF16)

                        def pre(u=u, l=l, T=T, hT=hT, xsrc=xsrc):
                            phaseA(u, l, T, hT, xsrc)

                        if kind == 0:
                            phaseB_da(u, l, j, T, hT, oT, pre)
                        elif kind == 1:
                            phaseB_ret(u, l, j, T, hT, oT, pre)
                        else:
                            phaseB_hg(u, l, j, T, hT, oT, pre)
                    phaseC(u, l, T, oT, KC, wout, xsrc, xdst, wo=wo_pre)
    P.finish()
    return nc


_CACHE = {}


def kernel(x_prompt, x_sample, cache_k, cache_v, state_ret, state_hgrn, c, c_ctx,
           w_mod, b_mod, g_pre, g_post, da_w_in, da_w_out, da_lambda, da_subln,
           ret_w_in, ret_w_out, ret_decay, hg_w_in, hg_w_out, hg_lb, hg_norm, _NL=4):
    f = lambda a: np.ascontiguousarray(np.asarray(a, dtype=np.float32))
    x_prompt, x_sample, cache_k, cache_v, state_ret, state_hgrn, c, c_ctx = map(
        f, (x_prompt, x_sample, cache_k, cache_v, state_ret, state_hgrn, c, c_ctx))
    cstv, ropev = host_consts()
    shared = {
        "w_mod": f(w_mod), "b_mod": f(b_mod), "g_pre": f(g_pre), "g_post": f(g_post),
        "da_w_in": f(da_w_in), "da_w_out": f(da_w_out), "da_lambda": f(da_lambda).reshape(2, 256),
        "da_subln": f(da_subln), "ret_w_in": f(ret_w_in), "ret_w_out": f(ret_w_out),
        "ret_decay": f(ret_decay).reshape(1, 16), "hg_w_in": f(hg_w_in), "hg_w_out": f(hg_w_out),
        "hg_lb": f(hg_lb), "hg_norm": f(hg_norm), "cst": cstv, "rope": ropev,
    }
    in_maps = []
    for i in range(8):
        m = dict(shared)
        m["xs"] = x_sample[i]
        m["xp"] = x_prompt[4 * i:4 * i + 4].reshape(1024, D)
        m["ck"] = cache_k[i].reshape(2, 512, 1024)
        m["cv"] = cache_v[i].reshape(2, 512, 1024)
        m["sr"] = state_ret[i, 0]
        m["shg"] = state_hgrn[i, 0]
        m["cvec"] = np.ascontiguousarray(np.stack([c_ctx, c[i]], 0))
        in_maps.append(m)
    if _NL not in _CACHE:
        _CACHE[_NL] = build(_NL)
    nc = _CACHE[_NL]
    res = run_bass_kernel_spmd(nc, in_maps, core_ids=list(range(8)))
    R = res.results
    y_p = np.concatenate([r["yp"].reshape(4, 256, D) for r in R], 0)
    y_s = np.stack([r["ys"] for r in R], 0)
    n_k = np.concatenate([r["nk"].reshape(4, 2, 256, 16, 64) for r in R], 0)
    n_v = np.concatenate([r["nv"].reshape(4, 2, 256, 8, 128) for r in R], 0)
    n_r = np.concatenate([r["nr"].reshape(4, 1, 2, 8, 128, 256) for r in R], 0)
    n_h = np.concatenate([r["nh"].reshape(4, 1, 2, 8, 128, 128) for r in R], 0)
    return (y_p, y_s, n_k, n_v, n_r, n_h)
```

```python
import math
from contextlib import ExitStack, contextmanager

import numpy as np
import concourse.bass as bass
import concourse.mybir as mybir
from concourse.bass_utils import run_bass_kernel_spmd

F32 = mybir.dt.float32
BF16 = mybir.dt.bfloat16
AF = mybir.ActivationFunctionType
ALU = mybir.AluOpType
AX = mybir.AxisListType


class Buf:
    def __init__(self, name, t):
        self.name = name
        self.t = t
        self.lw = {}
        self.rd = {}
        self.kind = "sb"
        self.dsem = None
        self.dcnt = 0
        self.psum = False

    def __getitem__(self, i):
        return self.t[i]


class Prog:
    ENG = ("pe", "act", "dve", "pool", "sp")

    def __init__(self, nc):
        self.nc = nc
        self.q = {e: [] for e in self.ENG}
        self.esem = {}
        self.cnt = {}
        for e in ("pe", "act", "dve", "pool"):
            self.esem[e] = nc.alloc_semaphore(name="es_" + e)
            self.cnt[e] = 0
        self.seen = {e: {} for e in self.ENG}
        self.freed = {}
        self.dbufs = []
        self.allbufs = []
        self.scopes = []
        self.uid = 0
        self.dsems = {}
        self.free_dsems = {"sw": [], "hw": []}

    @contextmanager
    def scope(self):
        es = ExitStack()
        bufs = []
        self.scopes.append((es, bufs))
        try:
            yield
        finally:
            for b in bufs:
                if b.dsem is not None:
                    for qt, key in b.dsem.items():
                        self.free_dsems[qt].append(key)
                for t in list(b.lw.values()) + list(b.rd.values()):
                    k = t[0]
                    if k not in self.freed or self.freed[k][2] < t[2]:
                        self.freed[k] = t
            es.close()
            self.scopes.pop()

    def _mk(self, name, t):
        b = Buf(name, t)
        b.rd = dict(self.freed)
        self.scopes[-1][1].append(b)
        self.allbufs.append(b)
        return b

    def sb(self, name, shape, dt):
        self.uid += 1
        t = self.scopes[-1][0].enter_context(self.nc.sbuf_tensor("%s_%d" % (name, self.uid), list(shape), dt))
        return self._mk(name, t)

    def ps(self, name, shape, dt):
        self.uid += 1
        t = self.scopes[-1][0].enter_context(self.nc.psum_tensor("%s_%d" % (name, self.uid), list(shape), dt))
        b = self._mk(name, t)
        b.psum = True
        b.kind = "ps"
        return b

    def dram(self, name, shape, dt):
        t = self.nc.dram_tensor(name, list(shape), dt, kind="Internal").ap()
        b = Buf(name, t)
        b.kind = "dram"
        self.allbufs.append(b)
        return b

    def _need(self, eng, waits, tok, raw):
        if tok is None:
            return
        key, semh, val, teng = tok
        if teng == eng and not raw:
            return
        if self.seen[eng].get(key, 0) >= val:
            return
        if key not in waits or waits[key][1] < val:
            waits[key] = (semh, val)

    def _deps(self, eng, r, w, skipkey=None):
        waits = {}
        for b in r:
            for t in b.lw.values():
                self._need(eng, waits, t, True)
            if b.psum:
                for t in b.rd.values():
                    self._need(eng, waits, t, False)
        for b in w:
            for t in b.lw.values():
                if not (skipkey is not None and t[0] == skipkey):
                    self._need(eng, waits, t, False)
            for t in b.rd.values():
                self._need(eng, waits, t, False)
        for k, (s, v) in waits.items():
            self.seen[eng][k] = v
        return list(waits.values())

    def _reg(self, tok, r, w):
        for b in r:
            b.rd[tok[0]] = tok
        for b in w:
            b.lw[tok[0]] = tok
            b.rd = {}

    def op(self, eng, fn, r=(), w=()):
        waits = self._deps(eng, r, w)
        self.cnt[eng] += 1
        tok = (eng, self.esem[eng], self.cnt[eng], eng)
        self._reg(tok, r, w)
        self.q[eng].append((waits, fn, self.esem[eng], 1))

    def dma(self, q, out, in_, r=(), w=(), owner=None, nowaw=False, **kw):
        if owner is None:
            cand = [b for b in list(w) + list(r) if b.kind == "sb"]
            owner = cand[0] if cand else (w[0] if w else r[0])
        extra = None
        qt = "sw" if q == "pool" else "hw"
        if owner.dsem is None:
            owner.dsem = {}
        if qt not in owner.dsem:
            if self.free_dsems[qt]:
                key = self.free_dsems[qt].pop()
                extra = key
            else:
                key = ("d", len(self.dsems))
                self.dsems[key] = [self.nc.alloc_semaphore(name="ds_%d" % len(self.dsems)), 0]
            owner.dsem[qt] = key
        key = owner.dsem[qt]
        semh, tot = self.dsems[key]
        waits = self._deps(q, r, w, skipkey=key if nowaw else None)
        if extra is not None and tot > 0 and self.seen[q].get(key, 0) < tot:
            waits = [x for x in waits if x[0] is not semh] + [(semh, tot)]
            self.seen[q][key] = tot
        tot += 16
        self.dsems[key][1] = tot
        tok = (key, semh, tot, "dma")
        self._reg(tok, r, w)
        self.q[q].append((waits, (lambda e: e.dma_start(out=out, in_=in_, **kw)), semh, 16))

    def finish(self):
        nc = self.nc
        for key, (semh, tot) in self.dsems.items():
            if tot > 0:
                self.q["sp"].append(([(semh, tot)], None, None, 0))

        def run(e, items):
            for waits, fn, sem, inc in items:
                for s, v in waits:
                    e.wait_ge(s, v)
                if fn is not None:
                    fn(e).then_inc(sem, inc)

        with nc.Block() as blk:
            blk.tensor(lambda e: run(e, self.q["pe"]))
            blk.scalar(lambda e: run(e, self.q["act"]))
            blk.vector(lambda e: run(e, self.q["dve"]))
            blk.gpsimd(lambda e: run(e, self.q["pool"]))
            blk.sync(lambda e: run(e, self.q["sp"]))


class RingU:
    def __init__(self, *rings):
        self.b = [b for r in rings for b in r.b]
        self.i = 0

    def next(self):
        b = self.b[self.i % len(self.b)]
        self.i += 1
        return b


class Ring:
    def __init__(self, P, kind, name, n, shape, dt):
        mk = P.sb if kind == "sb" else P.ps
        self.b = [mk("%s%d" % (name, i), shape, dt) for i in range(n)]
        self.i = 0

    def next(self):
        b = self.b[self.i % len(self.b)]
        self.i += 1
        return b


class V:
    def __init__(self, b, ap):
        self.b = b
        self.ap = ap

    def __getitem__(self, i):
        return V(self.b, self.ap[i])

    def bc(self, shape):
        return V(self.b, self.ap.broadcast_to(list(shape)))

    def re(self, pat, **kw):
        return V(self.b, self.ap.rearrange(pat, **kw))

    def un(self, ax):
        return V(self.b, self.ap.unsqueeze(ax))

    def pbc(self, n=128):
        return V(self.b, self.ap.partition_broadcast(n))


def _bgi(self, i):
    return V(self, self.t[i])


Buf.__getitem__ = _bgi


def _u(x, lst):
    if isinstance(x, V):
        lst.append(x.b)
        return x.ap
    return x


def _I(P, eng, meth, out, *args, **kw):
    r, w = [], []
    o = _u(out, w)
    a = [_u(x, r) for x in args]
    k = {}
    for n, x in kw.items():
        k[n] = _u(x, w if n == "accum_out" else r)
    P.op(eng, (lambda e: getattr(e, meth)(o, *a, **k)), r=r, w=w)


def _act(P, out, in_, func, **kw):
    r, w = [], []
    o = _u(out, w)
    i = _u(in_, r)
    k = {}
    for n, x in kw.items():
        k[n] = _u(x, w if n == "accum_out" else r)
    P.op("act", (lambda e: e.activation(out=o, in_=i, func=func, **k)), r=r, w=w)


def _mm(P, out, lhsT, rhs, start=True, stop=True):
    r, w = [], []
    o = _u(out, w)
    a = _u(lhsT, r)
    b = _u(rhs, r)
    P.op("pe", (lambda e: e.matmul(o, a, b, start=start, stop=stop)), r=r, w=w)


def _tr(P, out, in_, ident):
    r, w = [], []
    o = _u(out, w)
    a = _u(in_, r)
    b = _u(ident, r)
    P.op("pe", (lambda e: e.transpose(out=o, in_=a, identity=b)), r=r, w=w)


def _dma(P, q, out, in_, **kw):
    r, w = [], []
    o = _u(out, w)
    i = _u(in_, r)
    P.dma(q, o, i, r=r, w=w, **kw)


Prog.I = _I
Prog.act = _act
Prog.mm = _mm
Prog.tr = _tr
Prog.d = _dma


D = 1024
EPS = 1e-6
BIG = 1.0e6
C_ID, C_SW, C_A, C_B, C_TL1, C_CTL, C_MF, C_MB = 0, 128, 256, 384, 512, 640, 768, 896
C_SDF, C_SDB = 1024, 1025
NCST = 1032


def host_consts():
    c = np.zeros((128, NCST), np.float32)
    i = np.arange(128)
    c[:, C_ID:C_ID + 128] = np.eye(128, dtype=np.float32)
    sw = np.zeros((128, 128), np.float32)
    sw[i, i ^ 1] = 1.0
    c[:, C_SW:C_SW + 128] = sw
    s = i[:, None].astype(np.float64)
    t = i[None, :].astype(np.float64)
    c[:, C_A:C_A + 128] = np.where(t >= s, t - s, BIG)
    c[:, C_B:C_B + 128] = np.where(s >= t, s - t, BIG)
    c[:, C_TL1:C_TL1 + 128] = np.broadcast_to(t + 1.0, (128, 128))
    c[:, C_CTL:C_CTL + 128] = np.broadcast_to(128.0 - t, (128, 128))
    same = (i[:, None] // 64) == (i[None, :] // 64)
    c[:, C_MF:C_MF + 128] = (same & (t >= s)).astype(np.float32)
    c[:, C_MB:C_MB + 128] = (same & (t <= s)).astype(np.float32)
    c[:, C_SDF] = 127.0 - i
    c[:, C_SDB] = i
    T = 2048
    tt = np.arange(T)
    row = (tt // 64).astype(np.float64)
    col = (tt % 64).astype(np.float64)
    inv = 10000.0 ** (-np.arange(16, dtype=np.float32) / 16).astype(np.float64)
    rope = np.zeros((128, 2, T), np.float32)
    for p in range(128):
        d = p % 64
        pr = d // 2
        ang = (row * inv[pr]) if pr < 16 else (col * inv[pr - 16])
        ang = ang.astype(np.float32).astype(np.float64)
        rope[p, 0] = np.cos(ang)
        rope[p, 1] = np.sin(ang) * (-1.0 if d % 2 == 0 else 1.0)
    return c, rope


def build(NL=4):
    nc = bass.Bass("TRN2", target_bir_lowering=False)
    P = Prog(nc)

    def din(n, s):
        return nc.dram_tensor(n, list(s), F32, kind="ExternalInput").ap()

    def dout(n, s):
        return nc.dram_tensor(n, list(s), F32, kind="ExternalOutput").ap()

    xs = din("xs", [2048, D]); xp = din("xp", [1024, D])
    ck = din("ck", [2, 512, 1024]); cv = din("cv", [2, 512, 1024])
    sr = din("sr", [2, 8, 128, 256]); shg = din("shg", [2, 8, 128, 128]); cvec = din("cvec", [2, D])
    w_mod = din("w_mod", [4, D, 3 * D]); b_mod = din("b_mod", [4, 3 * D])
    g_pre = din("g_pre", [4, D]); g_post = din("g_post", [4, D])
    da_w_in = din("da_w_in", [2, D, 4096]); da_w_out = din("da_w_out", [2, 1024, D])
    da_lambda = din("da_lambda", [2, 256]); da_subln = din("da_subln", [2, 128])
    ret_w_in = din("ret_w_in", [1, D, 6144]); ret_w_out = din("ret_w_out", [1, 2048, D]); ret_decay = din("ret_decay", [1, 16])
    hg_w_in = din("hg_w_in", [1, D, 5120]); hg_w_out = din("hg_w_out", [1, 1024, D])
    hg_lb = din("hg_lb", [2, 4, 1024]); hg_norm = din("hg_norm", [1, 128])
    cst = din("cst", [128, NCST]); rope = din("rope", [128, 2, 2048])
    yp = dout("yp", [1024, D]); ys = dout("ys", [2048, D])
    nk = dout("nk", [4, 2, 256, 1024]); nv = dout("nv", [4, 2, 256, 1024])
    nr = dout("nr", [4, 2, 8, 128, 256]); nh = dout("nh", [4, 2, 8, 128, 128])

    msc = P.dram("msc", [4, 2, 3 * D], F32)
    xsc = {"s": [P.dram("xsc_s%d" % i, [2048, D], F32) for i in range(2)],
           "p": [P.dram("xsc_p%d" % i, [1024, D], F32) for i in range(2)]}

    with P.scope():
        cf = P.sb("cf", [128, NCST], F32)
        P.d("sp", cf[:], cst)
        cb = P.sb("cb", [128, 1024], BF16)
        P.I("dve", "tensor_copy", cb[:], cf[:, 0:1024])
        identb = cb[:, C_ID:C_ID + 128]
        swb = cb[:, C_SW:C_SW + 128]

        def rstd_from_ss(ss, n):
            P.act(ss, ss, AF.Ln, scale=1.0 / n, bias=EPS)
            P.act(ss, ss, AF.Exp, scale=-0.5)

        with P.scope():
            cT = P.sb("cT", [128, 2, 8], F32)
            for v in range(2):
                P.d("sp", cT[:, v, :], cvec[v, :].rearrange("(c p) -> p c", p=128), allow_slow_non_contiguous=True, nowaw=True)
            scb = P.sb("scb", [128, 2, 8], BF16)
            P.act(scb[:], cT[:], AF.Silu)
            wr = Ring(P, "sb", "wm", 2, [128, 8, 512], BF16)
            psm = Ring(P, "ps", "psm", 2, [128, 512], F32)
            brow = P.sb("brow", [1, 3 * D], F32)
            mr = Ring(P, "sb", "mrow", 2, [1, 512], F32)
            for l in range(NL):
                P.d("sp", brow[:], b_mod[l:l + 1, :])
                wv = w_mod[l].rearrange("(c p) n -> p c n", p=128)
                for cbk in range(6):
                    w = wr.next()
                    P.d("pool", w[:], wv[:, :, cbk * 512:(cbk + 1) * 512])
                    for v in range(2):
                        ps = psm.next()
                        for kc in range(8):
                            P.mm(ps[0:1, :], scb[:, v, kc:kc + 1], w[:, kc, :], start=(kc == 0), stop=(kc == 7))
                        m = mr.next()
                        P.I("dve", "tensor_tensor", m[:], ps[0:1, :], brow[:, cbk * 512:(cbk + 1) * 512], op=ALU.add)
                        P.d("sp", msc[l, v:v + 1, cbk * 512:(cbk + 1) * 512], m[:], nowaw=True)

        def phaseA(u, l, T, hT, xsrc):
            v = 1 if u == "s" else 0
            with P.scope():
                shT = P.sb("shT", [128, 8], F32); scT = P.sb("scT", [128, 8], F32)
                gpT = P.sb("gpT", [128, 8], F32); AT = P.sb("AT", [128, 8], F32)
                P.d("sp", shT[:], msc[l, v, 0:1024].re("(c p) -> p c", p=128), allow_slow_non_contiguous=True)
                P.d("sp", scT[:], msc[l, v, 1024:2048].re("(c p) -> p c", p=128), allow_slow_non_contiguous=True)
                P.d("sp", gpT[:], g_pre[l, :].rearrange("(c p) -> p c", p=128), allow_slow_non_contiguous=True)
                P.I("dve", "scalar_tensor_tensor", AT[:], scT[:], 1.0, gpT[:], op0=ALU.add, op1=ALU.mult)
                xr = Ring(P, "sb", "xa", 3, [128, D], F32)
                xnr = Ring(P, "sb", "xn", 4, [128, D], BF16)
                jr = Ring(P, "sb", "jk", 2, [128, D], BF16)
                sr_ = Ring(P, "sb", "ssA", 4, [128, 1], F32)
                pst = Ring(P, "ps", "pstA", 2, [128, 1024], BF16)
                def a_stats(tt):
                    x = xr.next()
                    P.d("sp", x[:], xsrc[tt * 128:(tt + 1) * 128, :])
                    junk = jr.next(); ss = sr_.next()
                    P.act(junk[:], x[:], AF.Square, accum_out=ss[:])
                    rstd_from_ss(ss[:], D)
                    xn = xnr.next()
                    P.I("dve", "tensor_scalar", xn[:], x[:], ss[:, 0:1], None, op0=ALU.mult)
                    return xn

                def a_trev(tt, xn):
                    pt = pst.next()
                    for kc in range(8):
                        P.tr(pt[:, kc * 128:(kc + 1) * 128], xn[:, kc * 128:(kc + 1) * 128], identb)
                    for kc in range(8):
                        if kc % 2 == 0:
                            P.I("dve", "tensor_scalar", hT[:, kc, tt * 128:(tt + 1) * 128], pt[:, kc * 128:(kc + 1) * 128],
                                AT[:, kc:kc + 1], shT[:, kc:kc + 1], op0=ALU.mult, op1=ALU.add)
                        else:
                            P.act(hT[:, kc, tt * 128:(tt + 1) * 128], pt[:, kc * 128:(kc + 1) * 128], AF.Identity,
                                  scale=AT[:, kc:kc + 1], bias=shT[:, kc:kc + 1])

                pend_a = []
                for tt in range(T // 128):
                    pend_a.append((tt, a_stats(tt)))
                    if len(pend_a) > 2:
                        a_trev(*pend_a.pop(0))
                for it in pend_a:
                    a_trev(*it)

        def load_wo(KC, wout_ap):
            wo = P.sb("wo", [128, KC, D], BF16)
            wv = wout_ap.rearrange("(c p) n -> p c n", p=128)
            for c4 in range(0, KC, 4):
                P.d("pool", wo[:, c4:c4 + 4, :], wv[:, c4:c4 + 4, :], nowaw=True)
            return wo

        def phaseC(u, l, T, oT, KC, wout_ap, xsrc, xdst, wo=None):
            v = 1 if u == "s" else 0
            with P.scope():
                if wo is None:
                    wo = load_wo(KC, wout_ap)
                G2 = P.sb("G2", [128, D], F32)
                with P.scope():
                    gpo = P.sb("gpo", [128, D], F32)
                    P.d("sp", G2[:], msc[l, v:v + 1, 2048:3072].pbc(128))
                    P.d("sp", gpo[:], g_post[l:l + 1, :].partition_broadcast(128))
                    P.I("dve", "tensor_tensor", G2[:], G2[:], gpo[:], op=ALU.mult)
                psy = Ring(P, "ps", "psY", 2, [128, D], F32)
                xr = Ring(P, "sb", "xc", 2, [128, D], F32)
                tr_ = Ring(P, "sb", "tc", 2, [128, D], F32)
                jr = Ring(P, "sb", "jc", 2, [128, D], BF16)
                sr_ = Ring(P, "sb", "ssC", 4, [128, 1], F32)
                for tt in range(T // 128):
                    y = psy.next()
                    for hb in range(2):
                        for kc in range(KC):
                            P.mm(y[:, hb * 512:(hb + 1) * 512], oT[:, kc, tt * 128:(tt + 1) * 128],
                                 wo[:, kc, hb * 512:(hb + 1) * 512], start=(kc == 0), stop=(kc == KC - 1))
                    junk = jr.next(); ss = sr_.next()
                    P.act(junk[:], y[:], AF.Square, accum_out=ss[:])
                    rstd_from_ss(ss[:], D)
                    x = xr.next()
                    P.d("sp", x[:], xsrc[tt * 128:(tt + 1) * 128, :])
                    t = tr_.next()
                    P.I("dve", "scalar_tensor_tensor", t[:], y[:], ss[:, 0:1], G2[:], op0=ALU.mult, op1=ALU.mult)
                    P.I("pool", "tensor_tensor", x[:], x[:], t[:], op=ALU.add)
                    P.d("sp", xdst[tt * 128:(tt + 1) * 128, :], x[:], nowaw=True)

        def phaseB_da(u, l, j, T, hT, oT, pre):
            lam_init = 0.8 - 0.6 * math.exp(-0.3 * l)
            NT = T // 128
            QB = 256
            if u == "s":
                seqs = [(0, 2048, [(i * 128, i) for i in range(20)])]
                TK = 2560
            else:
                seqs = [(s * 256, 256, [((2 * s + i) * 128, 2 * s + i) for i in range(2)]) for s in range(4)]
                TK = 1024
            NKC = len(seqs[0][2])
            blocks = [(q0_ + qb * QB, kcs) for (q0_, nq, kcs) in seqs for qb in range(nq // QB)]
            win = da_w_in[j].rearrange("(c p) n -> p c n", p=128)
            with P.scope():
                wr = Ring(P, "sb", "wda", 2, [128, 8, 512], BF16)

                def load_panel(h):
                    w = wr.next()
                    for g in range(4):
                        P.d("pool", w[:, :, g * 128:(g + 1) * 128], win[:, :, g * 1024 + h * 128:g * 1024 + (h + 1) * 128], nowaw=True)
                    return w

                wnext = load_panel(0)
                pre()
                lv = P.sb("lv", [128, 256], F32)
                P.d("sp", lv[:], da_lambda[j:j + 1, :].partition_broadcast(128))
                lv4 = lv[:].re("p (a b d) -> p a b d", a=2, b=2)
                pr = P.sb("lpr", [128, 2, 64], F32)
                P.I("dve", "tensor_tensor", pr[:], lv4[:, :, 0, :], lv4[:, :, 1, :], op=ALU.mult)
                s2 = P.sb("ls2", [128, 2], F32)
                P.I("dve", "tensor_reduce", s2[:], pr[:], axis=AX.X, op=ALU.add)
                P.act(s2[:], s2[:], AF.Exp)
                nlam = P.sb("nlam", [128, 1], F32)
                P.I("dve", "tensor_tensor", nlam[:], s2[:, 1:2], s2[:, 0:1], op=ALU.subtract)
                P.I("dve", "tensor_scalar", nlam[:], nlam[:], -lam_init, None, op0=ALU.add)
                sgT = P.sb("sgT", [128, 1], F32)
                P.d("sp", sgT[:], da_subln[j, :].rearrange("(p o) -> p o", o=1), allow_slow_non_contiguous=True)
                P.I("dve", "tensor_scalar", sgT[:], sgT[:], 1.0 - lam_init, None, op0=ALU.mult)
                onesb = P.sb("onesb", [128, 128], BF16)
                P.I("dve", "memset", onesb[:], 1.0)
                if u == "s":
                    ropeb = P.sb("ropeb", [128, 2, 2048], BF16)
                    P.d("pool", ropeb[:], rope)
                qz = [P.sb("qz%d" % i, [128, T], BF16) for i in range(2)]
                P.I("dve", "memset", qz[0][64:128, :], 0.0)
                P.I("dve", "memset", qz[1][0:64, :], 0.0)
                kf = P.sb("kf", [128, TK], BF16)
                gT = P.sb("gT", [128, T], BF16)
                vsb = P.sb("vsb", [128, TK // 128, 128], BF16)
                ptr = Ring(P, "sb", "pT", 2, [128, NKC, 2 * QB], BF16)
                q0r = Ring(P, "sb", "q0", 2, [128, 512], BF16)
                t1r = Ring(P, "sb", "t1", 2, [128, 512], F32)
                t2r = Ring(P, "sb", "t2", 2, [128, 512], F32)
                ckt = P.sb("ckt", [128, 4, 128], BF16)
                kst = Ring(P, "sb", "kst", 2, [128, 256], F32)
                rvr = Ring(P, "sb", "rinv", 2, [128, 2 * QB], F32)
                o1r = Ring(P, "sb", "o1", 2, [128, QB], F32)
                o2r = Ring(P, "sb", "o2", 2, [128, QB], F32)
                sqr = Ring(P, "sb", "sqd", 2, [128, QB], BF16)
                rsr = Ring(P, "sb", "rsd", 2, [128, QB], F32)
                psA = Ring(P, "ps", "psA", 2, [128, 512], F32)
                psS = Ring(P, "ps", "psS", 2, [128, 512], F32)
                psO = Ring(P, "ps", "psO", 4, [128, 512], F32)
                psP = RingU(psA, psS)

                def epi1(accO, accS):
                    rinv = rvr.next()
                    if u == "p":
                        P.act(rinv[:], accS[:], AF.Ln)
                        P.act(rinv[:], rinv[:], AF.Exp, scale=-1.0)
                    else:
                        P.I("dve", "reciprocal", rinv[:], accS[:])
                    o1 = o1r.next(); o2 = o2r.next()
                    P.I("dve", "tensor_tensor", o1[:], accO[:, 0:QB], rinv[:, 0:QB], op=ALU.mult)
                    P.I("dve", "scalar_tensor_tensor", o2[:], accO[:, QB:2 * QB], nlam[:, 0:1], rinv[:, QB:2 * QB], op0=ALU.mult, op1=ALU.mult)
                    P.I("pool", "tensor_tensor", o1[:], o1[:], o2[:], op=ALU.add)
                    return [o1, o2, None]

                def epi1b(st):
                    sq = sqr.next()
                    P.act(sq[:], st[0][:], AF.Square)
                    st[2] = sq

                def epi2(h, qlo, accS, st):
                    if st[2] is None:
                        epi1b(st)
                    o1, o2, sq = st
                    P.mm(accS[:, 0:QB], onesb[:], sq[:])
                    rs = rsr.next()
                    P.act(rs[:], accS[:, 0:QB], AF.Ln, scale=1.0 / 128, bias=EPS)
                    P.act(rs[:], rs[:], AF.Exp, scale=-0.5)
                    P.I("dve", "scalar_tensor_tensor", o2[:], o1[:], sgT[:, 0:1], rs[:], op0=ALU.mult, op1=ALU.mult)
                    P.I("pool", "tensor_tensor", oT[:, h, qlo:qlo + QB], o2[:], gT[:, qlo:qlo + QB], op=ALU.mult)

                for h in range(8):
                    W = wnext
                    if h + 1 < 8:
                        wnext = load_panel(h + 1)
                    for which, dst, c0 in (("q", None, 0), ("k", kf, 128), ("g", gT, 384)):
                        for blk in range(T // 512):
                            ps = psP.next()
                            for kc in range(8):
                                P.mm(ps[:], W[:, kc, c0:c0 + 128], hT[:, kc, blk * 512:(blk + 1) * 512], start=(kc == 0), stop=(kc == 7))
                            bs = slice(blk * 512, (blk + 1) * 512)
                            dsl = dst[:, bs] if dst is not None else None
                            if which == "g":
                                P.act(dsl, ps[:], AF.Silu)
                                continue
                            scl = 0.125 if which == "q" else 1.0
                            if u == "s":
                                q0 = q0r.next()
                                P.I("dve", "tensor_scalar", q0[:], ps[:], scl, None, op0=ALU.mult)
                                sw = psP.next()
                                P.mm(sw[:], swb, q0[:])
                                t1 = t1r.next(); t2 = t2r.next()
                                P.I("pool", "tensor_tensor", t1[:], q0[:], ropeb[:, 0, blk * 512:(blk + 1) * 512], op=ALU.mult)
                                P.I("dve", "tensor_tensor", t2[:], sw[:], ropeb[:, 1, blk * 512:(blk + 1) * 512], op=ALU.mult)
                                if which == "q":
                                    P.I("pool", "tensor_tensor", qz[0][0:64, bs], t1[0:64, :], t2[0:64, :], op=ALU.add)
                                    P.I("pool", "tensor_tensor", qz[1][64:128, bs], t1[64:128, :], t2[64:128, :], op=ALU.add)
                                else:
                                    P.I("pool", "tensor_tensor", dsl, t1[:], t2[:], op=ALU.add)
                            elif which == "q":
                                P.I("dve", "tensor_scalar", qz[0][0:64, bs], ps[0:64, :], scl, None, op0=ALU.mult)
                                P.I("dve", "tensor_scalar", qz[1][64:128, bs], ps[64:128, :], scl, None, op0=ALU.mult)
                            else:
                                P.I("dve", "tensor_scalar", dsl, ps[:], scl, None, op0=ALU.mult)
                    if u == "s":
                        P.d("pool", ckt[:], ck[j].rearrange("(c p) n -> p c n", p=128)[:, :, h * 128:(h + 1) * 128])
                        P.d("pool", vsb[:, 16:20, :], cv[j].rearrange("(c p) n -> p c n", p=128)[:, :, h * 128:(h + 1) * 128])
                        pc = psP.next()
                        for c in range(4):
                            P.mm(pc[:, c * 128:(c + 1) * 128], ckt[:, c, :], identb)
                        P.I("dve", "tensor_copy", kf[:, 2048:2560], pc[:])
                    for tt in range(NT):
                        ps = psP.next()
                        c0 = 128 if u == "p" else 256
                        ncol = 384 - c0
                        for kc in range(8):
                            P.mm(ps[:, 0:ncol], hT[:, kc, tt * 128:(tt + 1) * 128], W[:, kc, c0:384], start=(kc == 0), stop=(kc == 7))
                        P.act(vsb[:, tt, :], ps[:, ncol - 128:ncol], AF.Copy)
                        if u == "p":
                            st = kst.next()
                            P.I("dve", "tensor_copy", st[:], ps[:, 0:256])
                            sq_, tl = tt // 2, tt % 2
                            P.d("sp", nk[sq_, j, tl * 128:(tl + 1) * 128, h * 128:(h + 1) * 128], st[:, 0:128])
                            P.d("sp", nv[sq_, j, tl * 128:(tl + 1) * 128, h * 128:(h + 1) * 128], st[:, 128:256])
                    prev = None
                    pend = None
                    for blk in blocks + [None]:
                        if blk is not None:
                            pT = ptr.next()
                        if prev is not None:
                            accO = psO.next(); accS = psO.next()
                        for i in range(NKC):
                            if blk is not None:
                                qlo, kcs = blk
                                koff, vi = kcs[i]
                                sc = psS.next()
                                for s in range(2):
                                    P.mm(sc[:, s * QB:(s + 1) * QB], kf[:, koff:koff + 128], qz[s][:, qlo:qlo + QB])
                                P.act(pT[:, i, :], sc[:], AF.Exp)
                            if prev is not None:
                                pqlo, pkcs, ppT = prev
                                P.mm(accO[:], vsb[:, pkcs[i][1], :], ppT[:, i, :], start=(i == 0), stop=(i == NKC - 1))
                                P.mm(accS[:], onesb[:], ppT[:, i, :], start=(i == 0), stop=(i == NKC - 1))
                            if pend is not None and i == min(NKC - 1, 6) and pend[3][2] is None:
                                epi1b(pend[3])
                            if pend is not None and i == min(NKC - 1, 12):
                                epi2(*pend)
                                pend = None
                        if pend is not None:
                            epi2(*pend)
                            pend = None
                        if prev is not None:
                            pend = (h, prev[0], accS, epi1(accO, accS))
                        prev = (blk[0], blk[1], pT) if blk is not None else None
                    if pend is not None:
                        epi2(*pend)

        def phaseB_ret(u, l, j, T, hT, oT, pre):
            NT = T // 128
            SC = 128.0 ** -0.5
            if u == "s":
                seqs = [list(range(16))]
            else:
                seqs = [[2 * s, 2 * s + 1] for s in range(4)]
            win = ret_w_in[j].rearrange("(c p) n -> p c n", p=128)
            with P.scope():
                wr = Ring(P, "sb", "wrt", 2, [128, 8, 768], BF16)

                def load_panel(h):
                    w = wr.next()
                    P.d("pool", w[:, :, 0:128], win[:, :, h * 128:(h + 1) * 128], nowaw=True)
                    P.d("pool", w[:, :, 128:256], win[:, :, 1024 + h * 128:1024 + (h + 1) * 128], nowaw=True)
                    P.d("pool", w[:, :, 256:512], win[:, :, 2048 + h * 256:2048 + (h + 1) * 256], nowaw=True)
                    P.d("pool", w[:, :, 512:768], win[:, :, 4096 + h * 256:4096 + (h + 1) * 256], nowaw=True)
                    return w

                wnext = load_panel(0)
                pre()
                lg = P.sb("lg", [128, 16], F32)
                P.d("sp", lg[:], ret_decay[j:j + 1, :].partition_broadcast(128))
                P.act(lg[:], lg[:], AF.Exp)
                P.act(lg[:], lg[:], AF.Ln, scale=-1.0, bias=1.0)
                qT = P.sb("qT", [128, T], BF16); kT = P.sb("kT", [128, T], BF16)
                qfT = P.sb("qfT", [128, T], BF16); qbT = P.sb("qbT", [128, T], BF16)
                kmf = P.sb("kmf", [128, NT, 128], BF16); kmb = P.sb("kmb", [128, NT, 128], BF16)
                vtm = P.sb("vtm", [128, NT, 256], BF16); gs = P.sb("gsr", [128, NT, 256], BF16)
                Sfb = P.sb("Sfb", [128, NT, 256], BF16); Sbb = P.sb("Sbb", [128, NT, 256], BF16)
                S32 = Ring(P, "sb", "S32", 6 if u == "s" else 24, [128, 256], F32)
                m1 = P.sb("m1", [128, 128], F32); m2 = P.sb("m2", [128, 128], F32); Mh = P.sb("Mh", [128, 128], F32)
                qdf = P.sb("qdf", [128, 128], BF16); qdb = P.sb("qdb", [128, 128], BF16)
                kd = P.sb("kd", [128, 4], F32)
                ptr = Ring(P, "sb", "PTs", 3, [128, 128], BF16)
                ogr = Ring(P, "sb", "ogr", 4, [128, 256], BF16)
                jr = Ring(P, "sb", "jrr", 3, [128, 256], BF16)
                ssr = Ring(P, "sb", "ssr", 6, [128, 1], F32)
                psA = Ring(P, "ps", "psA", 1, [128, 512], F32)
                psS = Ring(P, "ps", "psS", 2, [128, 512], F32)
                psO = Ring(P, "ps", "psO", 3, [128, 512], F32)
                psT = Ring(P, "ps", "psT", 2, [128, 1024], BF16)
                psP = RingU(psA, psO)
                for h in range(8):
                    W = wnext
                    if h + 1 < 8:
                        wnext = load_panel(h + 1)
                    lgf = lg[:, h:h + 1]; lgb = lg[:, 8 + h:9 + h]
                    P.act(m1[:], cf[:, C_A:C_A + 128], AF.Exp, scale=lgf)
                    P.act(m2[:], cf[:, C_B:C_B + 128], AF.Exp, scale=lgb)
                    P.I("dve", "tensor_tensor", m1[:], m1[:], m2[:], op=ALU.add)
                    P.I("dve", "tensor_scalar", Mh[:], m1[:], SC, None, op0=ALU.mult)
                    P.act(qdf[:], cf[:, C_TL1:C_TL1 + 128], AF.Exp, scale=lgf)
                    P.act(qdb[:], cf[:, C_CTL:C_CTL + 128], AF.Exp, scale=lgb)
                    P.act(kd[:, 0:1], cf[:, C_SDF:C_SDF + 1], AF.Exp, scale=lgf)
                    P.act(kd[:, 1:2], cf[:, C_SDB:C_SDB + 1], AF.Exp, scale=lgb)
                    P.I("dve", "tensor_scalar", kd[:, 0:2], kd[:, 0:2], SC, None, op0=ALU.mult)
                    P.act(kd[:, 2:3], lgf, AF.Exp, scale=128.0)
                    P.act(kd[:, 3:4], lgb, AF.Exp, scale=128.0)
                    for dst, c0 in ((qT, 0), (kT, 128)):
                        for blk in range(T // 512):
                            ps = psP.next()
                            for kc in range(8):
                                P.mm(ps[:], W[:, kc, c0:c0 + 128], hT[:, kc, blk * 512:(blk + 1) * 512], start=(kc == 0), stop=(kc == 7))
                            P.act(dst[:, blk * 512:(blk + 1) * 512], ps[:], AF.Copy)
                    P.I("dve", "tensor_tensor", qfT[:].re("p (n t) -> p n t", t=128), qT[:].re("p (n t) -> p n t", t=128),
                        qdf[:].un(1).bc([128, NT, 128]), op=ALU.mult)
                    P.I("dve", "tensor_tensor", qbT[:].re("p (n t) -> p n t", t=128), qT[:].re("p (n t) -> p n t", t=128),
                        qdb[:].un(1).bc([128, NT, 128]), op=ALU.mult)
                    for tt in range(NT):
                        ps = psP.next()
                        for kc in range(8):
                            P.mm(ps[:, 0:384], hT[:, kc, tt * 128:(tt + 1) * 128], W[:, kc, 128:512], start=(kc == 0), stop=(kc == 7))
                        P.I("dve", "tensor_scalar", kmf[:, tt, :], ps[:, 0:128], kd[:, 0:1], None, op0=ALU.mult)
                        P.I("dve", "tensor_scalar", kmb[:, tt, :], ps[:, 0:128], kd[:, 1:2], None, op0=ALU.mult)
                        P.act(vtm[:, tt, :], ps[:, 128:384], AF.Copy)
                        ps2 = psP.next()
                        for kc in range(8):
                            P.mm(ps2[:, 0:256], hT[:, kc, tt * 128:(tt + 1) * 128], W[:, kc, 512:768], start=(kc == 0), stop=(kc == 7))
                        P.act(gs[:, tt, :], ps2[:, 0:256], AF.Silu)
                    chains = []
                    for si, tiles in enumerate(seqs):
                        for d, (order, km, Sb_, cdec) in enumerate(((tiles, kmf, Sfb, kd[:, 2:3]), (tiles[::-1], kmb, Sbb, kd[:, 3:4]))):
                            S = S32.next()
                            if u == "s":
                                P.d("sp", S[:], sr[d, h])
                            else:
                                P.I("pool", "memset", S[:], 0.0)
                            chains.append([si, d, order, km, Sb_, cdec, S])
                    for k in range(len(chains[0][2])):
                        for ch in chains:
                            si, d, order, km, Sb_, cdec, S = ch
                            tt = order[k]
                            P.act(Sb_[:, tt, :], S[:], AF.Copy)
                            U = psS.next()
                            P.mm(U[:, 0:256], km[:, tt, :], vtm[:, tt, :])
                            Sn = S32.next()
                            P.I("dve", "scalar_tensor_tensor", Sn[:], S[:], cdec, U[:, 0:256], op0=ALU.mult, op1=ALU.add)
                            ch[6] = Sn
                    if u == "p":
                        for si, d, order, km, Sb_, cdec, S in chains:
                            P.d("sp", nr[si, d, h], S[:])
                    def stA(tt):
                        sc = psS.next()
                        P.mm(sc[:, 0:128], kT[:, tt * 128:(tt + 1) * 128], qT[:, tt * 128:(tt + 1) * 128])
                        PT = ptr.next()
                        P.I("dve", "tensor_tensor", PT[:], sc[:, 0:128], Mh[:], op=ALU.mult)
                        return PT

                    def stB(tt, PT):
                        o = psO.next()
                        P.mm(o[:, 0:256], PT[:], vtm[:, tt, :], start=True, stop=False)
                        P.mm(o[:, 0:256], qfT[:, tt * 128:(tt + 1) * 128], Sfb[:, tt, :], start=False, stop=False)
                        P.mm(o[:, 0:256], qbT[:, tt * 128:(tt + 1) * 128], Sbb[:, tt, :], start=False, stop=True)
                        junk = jr.next(); ss = ssr.next()
                        P.act(junk[:], o[:, 0:256], AF.Square, accum_out=ss[:])
                        rstd_from_ss(ss[:], 256)
                        return (o, ss)

                    def stC(tt, o, ss):
                        og = ogr.next()
                        P.I("dve", "scalar_tensor_tensor", og[:], o[:, 0:256], ss[:, 0:1], gs[:, tt, :], op0=ALU.mult, op1=ALU.mult)
                        return og

                    def stD(tt, og):
                        pt = psT.next()
                        for e in range(2):
                            P.tr(pt[:, e * 128:(e + 1) * 128], og[:, e * 128:(e + 1) * 128], identb)
                        P.act(oT[:, 2 * h:2 * h + 2, tt * 128:(tt + 1) * 128], pt[:, 0:256].re("p (e t) -> p e t", e=2), AF.Copy)

                    qa, qb_, qc = {}, {}, {}
                    for i in range(NT + 3):
                        if i < NT:
                            qa[i] = stA(i)
                        if 0 <= i - 1 < NT:
                            qb_[i - 1] = stB(i - 1, qa.pop(i - 1))
                        if 0 <= i - 2 < NT:
                            qc[i - 2] = stC(i - 2, *qb_.pop(i - 2))
                        if 0 <= i - 3 < NT:
                            stD(i - 3, qc.pop(i - 3))

        def phaseB_hg(u, l, j, T, hT, oT, pre):
            NT = T // 128
            NCH = T // 64
            GB = 1024
            if u == "s":
                seqs = [list(range(32))]
            else:
                seqs = [list(range(4 * s, 4 * s + 4)) for s in range(4)]
            win = hg_w_in[j].rearrange("(c p) n -> p c n", p=128)
            with P.scope():
                wr = Ring(P, "sb", "whg", 2, [128, 8, 640], BF16)

                def load_panel(h):
                    w = wr.next()
                    for g in range(5):
                        P.d("pool", w[:, :, g * 128:(g + 1) * 128], win[:, :, g * 1024 + h * 128:g * 1024 + (h + 1) * 128], nowaw=True)
                    return w

                wnext = load_panel(0)
                pre()
                L = P.sb("hgL", [128, 2, 4, 8], F32)
                for d in range(2):
                    for e in range(4):
                        P.d("sp", L[:, d, e, :], hg_lb[d, e, :].rearrange("(h p) -> p h", p=128), allow_slow_non_contiguous=True, nowaw=True)
                P.act(L[:], L[:], AF.Exp)
                den = P.sb("hgden", [128, 2, 8], F32); lb = P.sb("hglb", [128, 2, 8], F32); omlb = P.sb("hgomlb", [128, 2, 8], F32)
                P.I("dve", "tensor_tensor", den[:], L[:, :, 0, :], L[:, :, 1, :], op=ALU.add)
                P.I("dve", "tensor_tensor", den[:], den[:], L[:, :, 2, :], op=ALU.add)
                P.I("dve", "tensor_tensor", den[:], den[:], L[:, :, 3, :], op=ALU.add)
                P.I("dve", "reciprocal", den[:], den[:])
                if l == 0:
                    P.I("dve", "memset", lb[:], 0.0)
                else:
                    P.I("dve", "tensor_copy", lb[:], L[:, :, 1, :])
                    for e in range(2, l + 1):
                        P.I("dve", "tensor_tensor", lb[:], lb[:], L[:, :, e, :], op=ALU.add)
                    P.I("dve", "tensor_tensor", lb[:], lb[:], den[:], op=ALU.mult)
                P.I("dve", "tensor_scalar", omlb[:], lb[:], -1.0, 1.0, op0=ALU.mult, op1=ALU.add)
                ngb = P.sb("ngb", [128, 128], F32)
                P.d("sp", ngb[:], hg_norm[j:j + 1, :].partition_broadcast(128))
                ones = P.sb("hgones", [128, GB], BF16)
                P.I("dve", "memset", ones[:], 1.0)
                qT = P.sb("hqT", [128, T], BF16)
                outs = [[P.sb("hz%d%d" % (d, k), [128, T], BF16) for k in range(4)] for d in range(2)]
                khtm = [P.sb("khtm%d" % d, [128, NT, 128], BF16) for d in range(2)]
                dec = [P.sb("hdec%d" % d, [128, NCH], F32) for d in range(2)]
                vtm = P.sb("hvtm", [128, NT, 128], BF16); gs = P.sb("hgs", [128, NT, 128], BF16)
                Sb_ = [P.sb("hS%d" % d, [128, NCH, 128], BF16) for d in range(2)]
                S32 = Ring(P, "sb", "hS32", 4 if u == "s" else 24, [128, 128], F32)
                Gr = Ring(P, "sb", "hG", 2, [128, GB], F32); Ir = Ring(P, "sb", "hI", 2, [128, GB], F32); Ar = Ring(P, "sb", "hA", 3, [128, GB], F32)
                kkr = Ring(P, "sb", "hk", 2, [128, GB], BF16); exr = Ring(P, "sb", "hex", 4, [128, GB], BF16)
                ptr = Ring(P, "sb", "hPT", 6, [128, 128], BF16)
                onr = Ring(P, "sb", "hon", 2, [128, 128], F32)
                ogr = Ring(P, "sb", "hog", 3, [128, 128], BF16)
                jr = Ring(P, "sb", "hjr", 2, [128, 128], BF16)
                ssr = Ring(P, "sb", "hss", 6, [128, 1], F32)
                psA = Ring(P, "ps", "psA", 1, [128, 512], F32)
                psS = Ring(P, "ps", "psS", 2, [128, 512], F32)
                psO = Ring(P, "ps", "psO", 3, [128, 512], F32)
                psT = Ring(P, "ps", "psT", 2, [128, 1024], BF16)
                psP = RingU(psA, psO)
                masks = (cb[:, C_MF:C_MF + 128], cb[:, C_MB:C_MB + 128])
                for h in range(8):
                    W = wnext
                    if h + 1 < 8:
                        wnext = load_panel(h + 1)
                    for blk in range(T // 512):
                        ps = psP.next()
                        for kc in range(8):
                            P.mm(ps[:], W[:, kc, 0:128], hT[:, kc, blk * 512:(blk + 1) * 512], start=(kc == 0), stop=(kc == 7))
                        P.act(qT[:, blk * 512:(blk + 1) * 512], ps[:], AF.Silu)
                    nch = GB // 64
                    shp = [128, nch, 64]

                    def g_s1a(d, gb):
                        t0 = gb * GB
                        G = Gr.next(); I_ = Ir.next(); kk = kkr.next()
                        for b2 in range(GB // 512):
                            ps = psP.next()
                            for kc in range(8):
                                P.mm(ps[:], W[:, kc, 128 * (1 + d):128 * (2 + d)], hT[:, kc, t0 + b2 * 512:t0 + (b2 + 1) * 512],
                                     start=(kc == 0), stop=(kc == 7))
                            P.act(G[:, b2 * 512:(b2 + 1) * 512], ps[:], AF.Sigmoid)
                        return [d, gb, G, I_, kk]

                    def g_s1b(ctx):
                        d, gb, G, I_, kk = ctx
                        P.I("dve", "tensor_scalar", G[:], G[:], omlb[:, d, h:h + 1], lb[:, d, h:h + 1], op0=ALU.mult, op1=ALU.add)
                        P.I("dve", "tensor_scalar", kk[:], G[:], -1.0, 1.0, op0=ALU.mult, op1=ALU.add)

                    def g_s1c(ctx):
                        d, gb, G, I_, kk = ctx
                        P.act(G[:], G[:], AF.Ln)

                    def g_s1d(ctx):
                        d, gb, G, I_, kk = ctx
                        P.I("dve", "tensor_tensor_scan", I_[:], ones[:], G[:], 0.0, op0=ALU.mult, op1=ALU.add)
                        P.I("dve", "tensor_tensor", G[:], I_[:], G[:], op=ALU.subtract)

                    def g_refs(ctx):
                        d, gb, G, I_, kk = ctx[:5]
                        I3 = I_[:].re("p (c t) -> p c t", t=64); E3 = G[:].re("p (c t) -> p c t", t=64)
                        Z3 = I3 if d == 0 else E3
                        r1 = I3[:, :, 32:33] if d == 0 else E3[:, :, 31:32]
                        r3 = E3[:, :, 0:1] if d == 0 else I3[:, :, 63:64]
                        r4 = I3[:, :, 63:64] if d == 0 else E3[:, :, 0:1]
                        return I3, E3, Z3, (r1, r3, r4)

                    def g_s2a(ctx):
                        I3, E3, Z3, refs = g_refs(ctx)
                        As = []
                        for ref in refs:
                            A_ = Ar.next()
                            P.I("dve", "tensor_tensor", A_[:].re("p (c t) -> p c t", t=64), Z3, ref.bc(shp), op=ALU.subtract)
                            As.append(A_)
                        ctx.append(As)

                    def g_s2b(ctx):
                        d = ctx[0]
                        As = ctx[5]
                        sq = 1.0 if d == 0 else -1.0
                        exs = []
                        for A_, sgn in ((As[0], sq), (As[0], -sq), (As[1], sq), (As[2], -sq)):
                            ex = exr.next()
                            P.act(ex[:], A_[:], AF.Exp, scale=sgn)
                            exs.append(ex)
                        ctx.append(exs)

                    def g_s2c(ctx):
                        d, gb, G, I_, kk = ctx[:5]
                        exs = ctx[6]
                        t0 = gb * GB
                        qs_ = qT[:, t0:t0 + GB]
                        o_qt, o_kt, o_qh, o_kh = [x[:, t0:t0 + GB] for x in outs[d]]
                        for dst, src, ex in ((o_qt, qs_, exs[0]), (o_kt, kk[:], exs[1]), (o_qh, qs_, exs[2]), (o_kh, kk[:], exs[3])):
                            P.I("dve", "tensor_tensor", dst, src, ex[:], op=ALU.mult)
                        I3, E3, Z3, refs = g_refs(ctx)
                        c0 = gb * nch
                        P.I("dve", "tensor_tensor", dec[d][:, c0:c0 + nch], I3[:, :, 63], E3[:, :, 0], op=ALU.subtract)
                        P.act(dec[d][:, c0:c0 + nch], dec[d][:, c0:c0 + nch], AF.Exp)

                    gblocks = [(d, gb) for d in range(2) for gb in range(T // GB)]
                    cur = g_s1a(*gblocks[0]); g_s1b(cur); g_s1c(cur); g_s1d(cur)
                    for bi in range(len(gblocks)):
                        nxt = g_s1a(*gblocks[bi + 1]) if bi + 1 < len(gblocks) else None
                        g_s2a(cur)
                        if nxt is not None:
                            g_s1b(nxt)
                        g_s2b(cur)
                        if nxt is not None:
                            g_s1c(nxt)
                        g_s2c(cur)
                        if nxt is not None:
                            g_s1d(nxt)
                        cur = nxt
                    for d in range(2):
                        for t4 in range(0, NT, 4):
                            pt = psT.next()
                            for i in range(4):
                                P.tr(pt[:, i * 128:(i + 1) * 128], outs[d][3][:, (t4 + i) * 128:(t4 + i + 1) * 128], identb)
                            P.act(khtm[d][:, t4:t4 + 4, :], pt[:, 0:512].re("p (a b) -> p a b", a=4), AF.Copy)
                    for tt in range(NT):
                        ps = psP.next()
                        for kc in range(8):
                            P.mm(ps[:, 0:256], hT[:, kc, tt * 128:(tt + 1) * 128], W[:, kc, 384:640], start=(kc == 0), stop=(kc == 7))
                        P.act(vtm[:, tt, :], ps[:, 0:128], AF.Copy)
                        P.act(gs[:, tt, :], ps[:, 128:256], AF.Silu)
                    chains = []
                    for si, chunks in enumerate(seqs):
                        for d in range(2):
                            S = S32.next()
                            if u == "s":
                                P.d("sp", S[:], shg[d, h])
                            else:
                                P.I("pool", "memset", S[:], 0.0)
                            chains.append([si, d, chunks if d == 0 else chunks[::-1], S])
                    for k in range(len(chains[0][2])):
                        for ch in chains:
                            si, d, order, S = ch
                            c = order[k]
                            P.act(Sb_[d][:, c, :], S[:], AF.Copy)
                            tt, hf = c // 2, (c % 2) * 64
                            U = psS.next()
                            P.mm(U[:, 0:128], khtm[d][hf:hf + 64, tt, :], vtm[hf:hf + 64, tt, :])
                            Sn = S32.next()
                            P.I("dve", "scalar_tensor_tensor", Sn[:], S[:], dec[d][:, c:c + 1], U[:, 0:128], op0=ALU.mult, op1=ALU.add)
                            ch[3] = Sn
                    if u == "p":
                        for si, d, order, S in chains:
                            P.d("sp", nh[si, d, h], S[:])
                    def stA(tt):
                        PTs = []
                        for d in range(2):
                            sc = psS.next()
                            P.mm(sc[:, 0:128], outs[d][1][:, tt * 128:(tt + 1) * 128], outs[d][0][:, tt * 128:(tt + 1) * 128])
                            PT = ptr.next()
                            P.I("dve", "tensor_tensor", PT[:], sc[:, 0:128], masks[d], op=ALU.mult)
                            PTs.append(PT)
                        return PTs

                    def stB(tt, PTs):
                        o = psO.next()
                        P.mm(o[:, 0:128], PTs[0][:], vtm[:, tt, :], start=True, stop=False)
                        for d in range(2):
                            for hf in range(2):
                                c = 2 * tt + hf
                                P.mm(o[hf * 64:(hf + 1) * 64, 0:128], outs[d][2][:, c * 64:(c + 1) * 64], Sb_[d][:, c, :],
                                     start=False, stop=False)
                        P.mm(o[:, 0:128], PTs[1][:], vtm[:, tt, :], start=False, stop=True)
                        junk = jr.next(); ss = ssr.next()
                        P.act(junk[:], o[:, 0:128], AF.Square, accum_out=ss[:])
                        rstd_from_ss(ss[:], 128)
                        return (o, ss)

                    def stC(tt, o, ss):
                        on = onr.next()
                        P.I("dve", "scalar_tensor_tensor", on[:], o[:, 0:128], ss[:, 0:1], ngb[:], op0=ALU.mult, op1=ALU.mult)
                        og = ogr.next()
                        P.I("pool", "tensor_tensor", og[:], on[:], gs[:, tt, :], op=ALU.mult)
                        return og

                    def stD(tt, og):
                        pt = psT.next()
                        P.tr(pt[:, 0:128], og[:], identb)
                        P.act(oT[:, h, tt * 128:(tt + 1) * 128], pt[:, 0:128], AF.Copy)

                    qa, qb_, qc = {}, {}, {}
                    for i in range(NT + 3):
                        if i < NT:
                            qa[i] = stA(i)
                        if 0 <= i - 1 < NT:
                            qb_[i - 1] = stB(i - 1, qa.pop(i - 1))
                        if 0 <= i - 2 < NT:
                            qc[i - 2] = stC(i - 2, *qb_.pop(i - 2))
                        if 0 <= i - 3 < NT:
                            stD(i - 3, qc.pop(i - 3))

        for l in range(NL):
            for u in ("s", "p"):
                T = 2048 if u == "s" else 1024
                xin = xs if u == "s" else xp
                xout = ys if u == "s" else yp
                kind = l % 3
                j = l // 3
                KC = 16 if kind == 1 else 8
                xsrc = xin if l == 0 else xsc[u][l % 2]
                xdst = xout if l == NL - 1 else xsc[u][(l + 1) % 2]
                with P.scope():
                    oT = P.sb("oT", [128, KC, T], BF16)
                    wout = (da_w_out, ret_w_out, hg_w_out)[kind][j]
                    early_wo = (kind == 0) or (u == "p")
                    wo_pre = load_wo(KC, wout) if early_wo else None
                    with P.scope():
                        hT = P.sb("hT", [128, 8, T], BF16)

                        def pre(u=u, l=l, T=T, hT=hT, xsrc=xsrc):
                            phaseA(u, l, T, hT, xsrc)

                        if kind == 0:
                            phaseB_da(u, l, j, T, hT, oT, pre)
                        elif kind == 1:
                            phaseB_ret(u, l, j, T, hT, oT, pre)
                        else:
                            phaseB_hg(u, l, j, T, hT, oT, pre)
                    phaseC(u, l, T, oT, KC, wout, xsrc, xdst, wo=wo_pre)
    P.finish()
    return nc


_CACHE = {}


def kernel(x_prompt, x_sample, cache_k, cache_v, state_ret, state_hgrn, c, c_ctx,
           w_mod, b_mod, g_pre, g_post, da_w_in, da_w_out, da_lambda, da_subln,
           ret_w_in, ret_w_out, ret_decay, hg_w_in, hg_w_out, hg_lb, hg_norm, _NL=4):
    f = lambda a: np.ascontiguousarray(np.asarray(a, dtype=np.float32))
    x_prompt, x_sample, cache_k, cache_v, state_ret, state_hgrn, c, c_ctx = map(
        f, (x_prompt, x_sample, cache_k, cache_v, state_ret, state_hgrn, c, c_ctx))
    cstv, ropev = host_consts()
    shared = {
        "w_mod": f(w_mod), "b_mod": f(b_mod), "g_pre": f(g_pre), "g_post": f(g_post),
        "da_w_in": f(da_w_in), "da_w_out": f(da_w_out), "da_lambda": f(da_lambda).reshape(2, 256),
        "da_subln": f(da_subln), "ret_w_in": f(ret_w_in), "ret_w_out": f(ret_w_out),
        "ret_decay": f(ret_decay).reshape(1, 16), "hg_w_in": f(hg_w_in), "hg_w_out": f(hg_w_out),
        "hg_lb": f(hg_lb), "hg_norm": f(hg_norm), "cst": cstv, "rope": ropev,
    }
    in_maps = []
    for i in range(8):
        m = dict(shared)
        m["xs"] = x_sample[i]
        m["xp"] = x_prompt[4 * i:4 * i + 4].reshape(1024, D)
        m["ck"] = cache_k[i].reshape(2, 512, 1024)
        m["cv"] = cache_v[i].reshape(2, 512, 1024)
        m["sr"] = state_ret[i, 0]
        m["shg"] = state_hgrn[i, 0]
        m["cvec"] = np.ascontiguousarray(np.stack([c_ctx, c[i]], 0))
        in_maps.append(m)
    if _NL not in _CACHE:
        _CACHE[_NL] = build(_NL)
    nc = _CACHE[_NL]
    res = run_bass_kernel_spmd(nc, in_maps, core_ids=list(range(8)))
    R = res.results
    y_p = np.concatenate([r["yp"].reshape(4, 256, D) for r in R], 0)
    y_s = np.stack([r["ys"] for r in R], 0)
    n_k = np.concatenate([r["nk"].reshape(4, 2, 256, 16, 64) for r in R], 0)
    n_v = np.concatenate([r["nv"].reshape(4, 2, 256, 8, 128) for r in R], 0)
    n_r = np.concatenate([r["nr"].reshape(4, 1, 2, 8, 128, 256) for r in R], 0)
    n_h = np.concatenate([r["nh"].reshape(4, 1, 2, 8, 128, 128) for r in R], 0)
    return (y_p, y_s, n_k, n_v, n_r, n_h)
```
